# Optimizing a Trainium2 kernel written in Bass

```python
import math
import jax
import jax.numpy as jnp
from jax import lax
import numpy as np

D_MODEL = 1024
BATCH = 8
SEQ = 2048
DEPTH = 2
DEC_BATCH = 16
DEC_SEQ = 16
PAST_LEN = 1024

CHUNK = 64
EPS = 1e-6
D_FF = 2816
RET_HEADS = 8
RET_DK = 64
RET_DV = 64
MLA_HEADS = 8
MLA_Q_LORA = 384
MLA_KV_LORA = 256
MLA_NOPE = 64
MLA_ROPE = 32
MLA_DQK = MLA_NOPE + MLA_ROPE
MLA_DV = 64
ROPE_BASE = 10000.0
Q_BLOCK = 128
SSM_HEADS = 16
SSM_HEADDIM = 64
SSM_D_INNER = SSM_HEADS * SSM_HEADDIM
SSM_GROUPS = 2
SSM_STATE = 128
CONV_W = 4
CONV_DIM = SSM_D_INNER + 2 * SSM_GROUPS * SSM_STATE
D_MIX = RET_HEADS * RET_DV + MLA_HEADS * MLA_DV + SSM_D_INNER
RET_COLS = 2 * RET_HEADS * RET_DK + 2 * RET_HEADS * RET_DV
MLA_COLS = MLA_Q_LORA + MLA_KV_LORA + MLA_ROPE
SSM_COLS = SSM_D_INNER + CONV_DIM + SSM_HEADS
IN_COLS = RET_COLS + MLA_COLS + SSM_COLS

kernel_name = 'hymba_style_retention_mla_ssd_streaming_step'


def rmsnorm(x, g):
    xf = x.astype(jnp.float32)
    y = xf * lax.rsqrt(jnp.mean(xf * xf, axis=-1, keepdims=True) + EPS)
    return (y * g.astype(jnp.float32)).astype(x.dtype)


def swiglu(h, wgu, wd):
    gate, up = jnp.split(h @ wgu, 2, axis=-1)
    return (jax.nn.silu(gate) * up) @ wd


def rope(x, pos, theta):
    ang = pos.astype(jnp.float32)[:, None] * theta[None, :]
    cos = jnp.cos(ang)[:, None, :]
    sin = jnp.sin(ang)[:, None, :]
    x1, x2 = jnp.split(x.astype(jnp.float32), 2, axis=-1)
    return jnp.concatenate([x1 * cos - x2 * sin, x1 * sin + x2 * cos], axis=-1).astype(x.dtype)


def ret_theta():
    return 1.0 / (10000.0 ** jnp.linspace(0.0, 1.0, RET_DK // 2, dtype=jnp.float32))


def mla_theta():
    return 1.0 / (ROPE_BASE ** (jnp.arange(0, MLA_ROPE, 2, dtype=jnp.float32) / MLA_ROPE))


def retention_scan(q, k, v, s0):
    bsz, t, h, _ = q.shape
    c = min(CHUNK, t)
    n = t // c
    log_g = jnp.log(1.0 - 2.0 ** (-5.0 - jnp.arange(h, dtype=jnp.float32)))
    idx = jnp.arange(c, dtype=jnp.float32)
    diff = idx[:, None] - idx[None, :]
    dmask = jnp.where(diff >= 0, jnp.exp(log_g[:, None, None] * jnp.maximum(diff, 0.0)), 0.0)
    q_dec = jnp.exp(log_g[:, None] * (idx[None, :] + 1.0))
    k_dec = jnp.exp(log_g[:, None] * (c - 1.0 - idx[None, :]))
    c_dec = jnp.exp(log_g * c)

    def to_chunks(a):
        return a.astype(jnp.float32).reshape(bsz, n, c, h, a.shape[-1]).transpose(1, 0, 3, 2, 4)

    qc, kc, vc = to_chunks(q), to_chunks(k) * RET_DK ** -0.5, to_chunks(v)

    def step(s, inp):
        qi, ki, vi = inp
        att = jnp.einsum('bhqd,bhkd->bhqk', qi, ki) * dmask
        o = (jnp.einsum('bhqk,bhkv->bhqv', att, vi)
             + jnp.einsum('bhqd,bhdv->bhqv', qi, s) * q_dec[None, :, :, None])
        s = c_dec[None, :, None, None] * s + jnp.einsum('bhkd,bhkv->bhdv', ki * k_dec[None, :, :, None], vi)
        return s, o

    s_new, o = lax.scan(step, s0.astype(jnp.float32), (qc, kc, vc))
    return o.transpose(1, 0, 3, 2, 4).reshape(bsz, t, h, -1), s_new


def retention_group(cols, pos, s0, g_norm):
    bsz, t, _ = cols.shape
    hk = RET_HEADS * RET_DK
    hv = RET_HEADS * RET_DV
    q, k, v, gate = jnp.split(cols, [hk, 2 * hk, 2 * hk + hv], axis=-1)
    theta = ret_theta()
    q = rope(q.reshape(bsz, t, RET_HEADS, RET_DK), pos, theta)
    k = rope(k.reshape(bsz, t, RET_HEADS, RET_DK), pos, theta)
    v = v.reshape(bsz, t, RET_HEADS, RET_DV)
    o, s_new = retention_scan(q, k, v, s0)
    o = rmsnorm(o.astype(cols.dtype), g_norm).reshape(bsz, t, hv)
    return o * jax.nn.silu(gate), s_new


def rope_tail(x, pos, theta):
    return jnp.concatenate([x[..., :MLA_NOPE], rope(x[..., MLA_NOPE:], pos, theta)], axis=-1)


def block_attend(q, k, v, q_pos, k_pos):
    s = jnp.einsum('bqhd,bkhd->bhqk', q, k).astype(jnp.float32) * MLA_DQK ** -0.5
    visible = (k_pos[None, :] // CHUNK) <= (q_pos[:, None] // CHUNK)
    s = jnp.where(visible[None, None], s, -jnp.inf)
    p = jax.nn.softmax(s, axis=-1).astype(v.dtype)
    return jnp.einsum('bhqk,bkhv->bqhv', p, v)


def mla_attend(q, k, v, q_pos, k_pos):
    bsz, t, h, d = q.shape
    if t > Q_BLOCK and t % Q_BLOCK == 0:
        nb = t // Q_BLOCK
        qb = q.reshape(bsz, nb, Q_BLOCK, h, d).transpose(1, 0, 2, 3, 4)
        pb = q_pos.reshape(nb, Q_BLOCK)
        o = lax.map(lambda a: block_attend(a[0], k, v, a[1], k_pos), (qb, pb))
        return o.transpose(1, 0, 2, 3, 4).reshape(bsz, t, h, -1)
    return block_attend(q, k, v, q_pos, k_pos)


def mla_group(cols, pos, ckv_past, krope_past, past_pos, q_norm, kv_norm, w_uq, w_ukv, q_gain, k_gain):
    bsz, t, _ = cols.shape
    c_q, c_kv, k_rope = jnp.split(cols, [MLA_Q_LORA, MLA_Q_LORA + MLA_KV_LORA], axis=-1)
    c_q = rmsnorm(c_q, q_norm)
    c_kv = rmsnorm(c_kv, kv_norm)
    theta = mla_theta()
    q = (c_q @ w_uq).reshape(bsz, t, MLA_HEADS, MLA_DQK)
    q = rope_tail(rmsnorm(q, q_gain), pos, theta)
    if ckv_past is None:
        ckv_all, krope_all, k_pos = c_kv, k_rope, pos
    else:
        ckv_all = jnp.concatenate([ckv_past.astype(c_kv.dtype), c_kv], axis=1)
        krope_all = jnp.concatenate([krope_past.astype(k_rope.dtype), k_rope], axis=1)
        k_pos = jnp.concatenate([past_pos, pos])
    s = ckv_all.shape[1]
    kv = (ckv_all @ w_ukv).reshape(bsz, s, MLA_HEADS, MLA_NOPE + MLA_DV)
    k_nope, v = jnp.split(kv, [MLA_NOPE], axis=-1)
    k = jnp.concatenate([k_nope, jnp.broadcast_to(krope_all[:, :, None, :], (bsz, s, MLA_HEADS, MLA_ROPE))], axis=-1)
    k = rope_tail(rmsnorm(k, k_gain), k_pos, theta)
    o = mla_attend(q, k, v, pos, k_pos)
    return o.reshape(bsz, t, MLA_HEADS * MLA_DV), c_kv, k_rope


def ssd_scan(x, dt, a, bm, cm, h0):
    bsz, t, h, p = x.shape
    c = min(CHUNK, t)
    n = t // c
    rep = h // SSM_GROUPS
    bh = jnp.repeat(bm, rep, axis=2)
    ch = jnp.repeat(cm, rep, axis=2)

    def to_chunks(u):
        return jnp.swapaxes(u.astype(jnp.float32).reshape((bsz, n, c) + u.shape[2:]), 0, 1)

    causal = jnp.arange(c)[:, None] >= jnp.arange(c)[None, :]

    def step(hs, inp):
        xi, dti, bi, ci = inp
        acum = jnp.cumsum(dti * a, axis=1)
        seg = acum[:, :, None, :] - acum[:, None, :, :]
        decay = jnp.exp(jnp.where(causal[None, :, :, None], seg, -jnp.inf))
        xdt = xi * dti[..., None]
        scores = jnp.einsum('bthn,bshn->btsh', ci, bi) * decay
        y = (jnp.einsum('btsh,bshp->bthp', scores, xdt)
             + jnp.einsum('bthn,bhpn->bthp', ci, hs) * jnp.exp(acum)[..., None])
        alast = acum[:, -1]
        w = jnp.exp(alast[:, None, :] - acum)
        hs = jnp.exp(alast)[:, :, None, None] * hs + jnp.einsum('bshp,bshn->bhpn', xdt * w[..., None], bi)
        return hs, y

    h_new, y = lax.scan(step, h0.astype(jnp.float32), (to_chunks(x), to_chunks(dt), to_chunks(bh), to_chunks(ch)))
    return jnp.swapaxes(y, 0, 1).reshape(bsz, t, h, p), h_new


def ssm_group(cols, conv_buf, h0, conv_w, conv_b, dt_bias, a_log, d_skip, norm_g):
    bsz, t, _ = cols.shape
    z, xbc, dt = jnp.split(cols, [SSM_D_INNER, SSM_D_INNER + CONV_DIM], axis=-1)
    xpad = jnp.concatenate([conv_buf.astype(xbc.dtype), xbc], axis=1)
    new_buf = xpad[:, xpad.shape[1] - (CONV_W - 1):]
    conv = lax.conv_general_dilated(xpad, conv_w[:, None, :], window_strides=(1,), padding='VALID',
                                    dimension_numbers=('NWC', 'WIO', 'NWC'), feature_group_count=CONV_DIM)
    xbc = jax.nn.silu(conv + conv_b)
    xs, bm, cm = jnp.split(xbc, [SSM_D_INNER, SSM_D_INNER + SSM_GROUPS * SSM_STATE], axis=-1)
    dt = jax.nn.softplus(dt.astype(jnp.float32) + dt_bias.astype(jnp.float32))
    a = -jnp.exp(a_log.astype(jnp.float32))
    xs = xs.reshape(bsz, t, SSM_HEADS, SSM_HEADDIM)
    y, h_new = ssd_scan(xs, dt, a, bm.reshape(bsz, t, SSM_GROUPS, SSM_STATE),
                        cm.reshape(bsz, t, SSM_GROUPS, SSM_STATE), h0)
    y = (y + d_skip.astype(jnp.float32)[:, None] * xs.astype(jnp.float32)).astype(cols.dtype)
    gated = (y.reshape(bsz, t, SSM_D_INNER) * jax.nn.silu(z)).reshape(bsz, t, SSM_GROUPS, SSM_D_INNER // SSM_GROUPS)
    out = rmsnorm(gated, norm_g).reshape(bsz, t, SSM_D_INNER)
    return out, h_new, new_buf


def layer(x, pos, ckv_past, krope_past, past_pos, ret_s0, ssm_h0, conv_buf, p):
    x = x + 0.5 * swiglu(rmsnorm(x, p['ffn1_norm']), p['ffn1_wgu'], p['ffn1_wd'])
    cols = rmsnorm(x, p['mix_norm']) @ p['w_in']
    ret_cols, mla_cols, ssm_cols = jnp.split(cols, [RET_COLS, RET_COLS + MLA_COLS], axis=-1)
    o_ret, ret_s = retention_group(ret_cols, pos, ret_s0, p['ret_norm'])
    o_mla, c_kv, k_rope = mla_group(mla_cols, pos, ckv_past, krope_past, past_pos, p['mla_q_norm'],
                                    p['mla_kv_norm'], p['mla_w_uq'], p['mla_w_ukv'], p['mla_q_gain'], p['mla_k_gain'])
    o_ssm, ssm_h, conv_new = ssm_group(ssm_cols, conv_buf, ssm_h0, p['ssm_conv_w'], p['ssm_conv_b'],
                                       p['ssm_dt_bias'], p['ssm_a_log'], p['ssm_d'], p['ssm_norm'])
    x = x + jnp.concatenate([o_ret, o_mla, o_ssm], axis=-1) @ p['w_out']
    x = x + 0.5 * swiglu(rmsnorm(x, p['ffn2_norm']), p['ffn2_wgu'], p['ffn2_wd'])
    return x, (c_kv, k_rope, ret_s.astype(x.dtype), ssm_h.astype(x.dtype), conv_new)


def setup_inputs(seed: int = 0) -> dict:
    key = jax.random.key(seed)
    ks = iter(jax.random.split(key, 48))

    def nrm(shape, scale):
        return scale * jax.random.normal(next(ks), shape, jnp.float32)

    def gain(shape):
        return 1.0 + nrm(shape, 0.02)

    dt0 = jnp.exp(jax.random.uniform(next(ks), (DEPTH, SSM_HEADS), jnp.float32, math.log(1e-3), math.log(1e-1)))
    dt_bias = dt0 + jnp.log(-jnp.expm1(-dt0))
    a_log = jnp.log(jax.random.uniform(next(ks), (DEPTH, SSM_HEADS), jnp.float32, 1.0, 16.0))
    return {
        'x_prompt': nrm((BATCH, SEQ, D_MODEL), 1.0),
        'x_sample': nrm((DEC_BATCH, DEC_SEQ, D_MODEL), 1.0),
        'cache_mla_ckv': nrm((DEPTH, DEC_BATCH, PAST_LEN, MLA_KV_LORA), 1.0),
        'cache_mla_krope': nrm((DEPTH, DEC_BATCH, PAST_LEN, MLA_ROPE), 1.0),
        'state_ret': nrm((DEPTH, DEC_BATCH, RET_HEADS, RET_DK, RET_DV), 0.3),
        'state_ssm': nrm((DEPTH, DEC_BATCH, SSM_HEADS, SSM_HEADDIM, SSM_STATE), 0.3),
        'state_conv': nrm((DEPTH, DEC_BATCH, CONV_W - 1, CONV_DIM), 1.0),
        'ffn1_norm': gain((DEPTH, D_MODEL)),
        'ffn1_wgu': nrm((DEPTH, D_MODEL, 2 * D_FF), D_MODEL ** -0.5),
        'ffn1_wd': nrm((DEPTH, D_FF, D_MODEL), D_FF ** -0.5),
        'mix_norm': gain((DEPTH, D_MODEL)),
        'w_in': nrm((DEPTH, D_MODEL, IN_COLS), D_MODEL ** -0.5),
        'ret_norm': gain((DEPTH, RET_HEADS, RET_DV)),
        'mla_q_norm': gain((DEPTH, MLA_Q_LORA)),
        'mla_kv_norm': gain((DEPTH, MLA_KV_LORA)),
        'mla_w_uq': nrm((DEPTH, MLA_Q_LORA, MLA_HEADS * MLA_DQK), MLA_Q_LORA ** -0.5),
        'mla_w_ukv': nrm((DEPTH, MLA_KV_LORA, MLA_HEADS * (MLA_NOPE + MLA_DV)), MLA_KV_LORA ** -0.5),
        'mla_q_gain': gain((DEPTH, MLA_DQK)),
        'mla_k_gain': gain((DEPTH, MLA_DQK)),
        'ssm_conv_w': nrm((DEPTH, CONV_W, CONV_DIM), CONV_W ** -0.5),
        'ssm_conv_b': nrm((DEPTH, CONV_DIM), 0.02),
        'ssm_dt_bias': dt_bias,
        'ssm_a_log': a_log,
        'ssm_d': gain((DEPTH, SSM_HEADS)),
        'ssm_norm': gain((DEPTH, SSM_GROUPS, SSM_D_INNER // SSM_GROUPS)),
        'w_out': nrm((DEPTH, D_MIX, D_MODEL), D_MIX ** -0.5),
        'ffn2_norm': gain((DEPTH, D_MODEL)),
        'ffn2_wgu': nrm((DEPTH, D_MODEL, 2 * D_FF), D_MODEL ** -0.5),
        'ffn2_wd': nrm((DEPTH, D_FF, D_MODEL), D_FF ** -0.5),
    }


def reference(x_prompt, x_sample, cache_mla_ckv, cache_mla_krope, state_ret, state_ssm, state_conv,
              ffn1_norm, ffn1_wgu, ffn1_wd, mix_norm, w_in, ret_norm, mla_q_norm, mla_kv_norm,
              mla_w_uq, mla_w_ukv, mla_q_gain, mla_k_gain, ssm_conv_w, ssm_conv_b, ssm_dt_bias,
              ssm_a_log, ssm_d, ssm_norm, w_out, ffn2_norm, ffn2_wgu, ffn2_wd):
    b_p, t_p, _ = x_prompt.shape
    t_s = x_sample.shape[1]
    past = cache_mla_ckv.shape[2]
    pos_p = jnp.arange(t_p, dtype=jnp.int32)
    past_pos = jnp.arange(past, dtype=jnp.int32)
    pos_s = past + jnp.arange(t_s, dtype=jnp.int32)
    yp, ys = x_prompt, x_sample
    new_p = [[], [], [], [], []]
    new_s = [[], [], [], [], []]
    for l in range(DEPTH):
        p = {
            'ffn1_norm': ffn1_norm[l], 'ffn1_wgu': ffn1_wgu[l], 'ffn1_wd': ffn1_wd[l],
            'mix_norm': mix_norm[l], 'w_in': w_in[l], 'ret_norm': ret_norm[l],
            'mla_q_norm': mla_q_norm[l], 'mla_kv_norm': mla_kv_norm[l], 'mla_w_uq': mla_w_uq[l],
            'mla_w_ukv': mla_w_ukv[l], 'mla_q_gain': mla_q_gain[l], 'mla_k_gain': mla_k_gain[l],
            'ssm_conv_w': ssm_conv_w[l], 'ssm_conv_b': ssm_conv_b[l], 'ssm_dt_bias': ssm_dt_bias[l],
            'ssm_a_log': ssm_a_log[l], 'ssm_d': ssm_d[l], 'ssm_norm': ssm_norm[l], 'w_out': w_out[l],
            'ffn2_norm': ffn2_norm[l], 'ffn2_wgu': ffn2_wgu[l], 'ffn2_wd': ffn2_wd[l],
        }
        yp, st_p = layer(yp, pos_p, None, None, None,
                         jnp.zeros((b_p, RET_HEADS, RET_DK, RET_DV), jnp.float32),
                         jnp.zeros((b_p, SSM_HEADS, SSM_HEADDIM, SSM_STATE), jnp.float32),
                         jnp.zeros((b_p, CONV_W - 1, CONV_DIM), x_prompt.dtype), p)
        ys, st_s = layer(ys, pos_s, cache_mla_ckv[l], cache_mla_krope[l], past_pos,
                         state_ret[l], state_ssm[l], state_conv[l], p)
        for i in range(5):
            new_p[i].append(st_p[i])
            new_s[i].append(st_s[i])
    ckv_p = jnp.stack(new_p[0])
    krope_p = jnp.stack(new_p[1])
    ret_p = jnp.stack(new_p[2])
    ssm_p = jnp.stack(new_p[3])
    conv_p = jnp.stack(new_p[4])
    ckv_s = jnp.stack(new_s[0])
    krope_s = jnp.stack(new_s[1])
    ret_s = jnp.stack(new_s[2])
    ssm_s = jnp.stack(new_s[3])
    conv_s = jnp.stack(new_s[4])
    return (yp, ys, ckv_p, krope_p, ret_p, ssm_p, conv_p, ckv_s, krope_s, ret_s, ssm_s, conv_s)
```

```python
import contextlib
import math
import os
import numpy as np
import ml_dtypes
import concourse.bass as bass
import concourse.mybir as mybir
from concourse.bass_utils import run_bass_kernel_spmd

F32 = mybir.dt.float32
BF16 = mybir.dt.bfloat16
AF = mybir.ActivationFunctionType
ALU = mybir.AluOpType

NCORES = 8
D = 1024
SEQ = 2048
T = 2080
DFF = 2816
NKF = DFF // 128
EPS = 1e-6
TILES = [(0, 512), (512, 512), (1024, 512), (1536, 512), (2048, 32)]
IN_COLS = 5296
RET0, MLA0, SSM0 = 0, 2048, 2720

_DBG_ENV = os.environ if os.environ.get("MK_DEBUG") else {}
ANNOTATE = bool(_DBG_ENV.get("MK_ANNOTATE"))
ENGS = ("sync", "tensor", "vector", "scalar", "gpsimd")
_SES = not _DBG_ENV.get("MK_NO_SES")
SAME_ENGINE_SYNC = {"vector": _SES, "scalar": _SES, "gpsimd": True, "tensor": False, "sync": False}


class Buf:
    __slots__ = ("name", "w", "r", "dsem", "dcnt", "excl")

    def __init__(self, name="", excl=False):
        self.name = name
        self.excl = excl
        self.w = None
        self.r = []
        self.dsem = None
        self.dcnt = 0


class Prog:
    def __init__(self):
        self.ops = {e: [] for e in ENGS}
        self.cnt = {e: 0 for e in ENGS}
        self.seen = {e: {} for e in ENGS}
        self.dma_sems = []
        self.dma_cnt = {}
        self.tag = ""

    def new_dma_sem(self):
        k = "d%d" % len(self.dma_sems)
        self.dma_sems.append(k)
        return k

    def _deps(self, eng, reads, writes, skip_waw=None):
        need = {}
        seen = self.seen[eng]

        def add(ev):
            if ev is None:
                return
            k, v = ev
            if k == eng and not SAME_ENGINE_SYNC[eng]:
                return
            if seen.get(k, 0) >= v:
                return
            if need.get(k, 0) < v:
                need[k] = v
        for b in reads:
            add(b.w)
            if b.excl:
                for ev in b.r:
                    if ev[0] != eng:
                        add(ev)
        for b in writes:
            if not (skip_waw is not None and b.w is not None and b.w[0] == skip_waw):
                add(b.w)
            for ev in b.r:
                add(ev)
        for k, v in need.items():
            seen[k] = v
        return list(need.items())

    def barrier(self, skip=()):
        for e in ENGS:
            if e in skip:
                continue
            waits = []
            for k in ENGS:
                v = self.cnt[k]
                if k != e and v > self.seen[e].get(k, 0):
                    waits.append((k, v))
                    self.seen[e][k] = v
            for k, v in self.dma_cnt.items():
                if v > self.seen[e].get(k, 0):
                    waits.append((k, v))
                    self.seen[e][k] = v
            self.ops[e].append((waits, None, None, ""))

    def op(self, eng, fn, reads=(), writes=(), inc=True):
        waits = self._deps(eng, reads, writes)
        self.ops[eng].append((waits, fn, ("E", eng) if inc else None, self.tag))
        if inc:
            self.cnt[eng] += 1
            ev = (eng, self.cnt[eng])
        else:
            ev = (eng, self.cnt[eng] + 1)
        for b in reads:
            b.r.append(ev)
            if len(b.r) > 48:
                b.r = b.r[-48:] if False else self._compact(b.r)
        for b in writes:
            b.w = ev
            b.r = []
        return ev

    @staticmethod
    def _compact(evs):
        best = {}
        for k, v in evs:
            if best.get(k, 0) < v:
                best[k] = v
        return list(best.items())

    def dma(self, eng, fn, reads=(), writes=(), sembuf=None):
        sb = sembuf if sembuf is not None else (writes[0] if writes else reads[0])
        if sb.dsem is None:
            sb.dsem = self.new_dma_sem()
        waits = self._deps(eng, reads, writes, skip_waw=sb.dsem)
        sb.dcnt += 16
        ev = (sb.dsem, sb.dcnt)
        self.dma_cnt[sb.dsem] = sb.dcnt
        self.ops[eng].append((waits, fn, ("D", sb.dsem), self.tag))
        for b in reads:
            b.r.append(ev)
        for b in writes:
            b.w = ev
            b.r = []
        return ev

    def wait_all(self, eng, bufs):
        waits = self._deps(eng, [], bufs)
        self.ops[eng].append((waits, None, None, ""))

    def check(self):
        sem = {}
        pc = {e: 0 for e in ENGS}
        total = sum(len(v) for v in self.ops.values())
        done = 0
        while done < total:
            prog = False
            for e in ENGS:
                ops = self.ops[e]
                while pc[e] < len(ops):
                    waits, fn, inc, _t = ops[pc[e]]
                    if any(sem.get(k, 0) < v for k, v in waits):
                        break
                    if inc is not None:
                        sem[inc[1]] = sem.get(inc[1], 0) + (1 if inc[0] == "E" else 16)
                    pc[e] += 1
                    done += 1
                    prog = True
            if not prog:
                msg = []
                for e in ENGS:
                    if pc[e] < len(self.ops[e]):
                        waits = self.ops[e][pc[e]][0]
                        msg.append((e, pc[e], [(k, v, sem.get(k, 0)) for k, v in waits if sem.get(k, 0) < v]))
                raise RuntimeError("DEADLOCK in recorded program: %s" % msg)
        return {e: len(v) for e, v in self.ops.items()}

    def emit(self, nc, st):
        print("[prog] ops per engine:", self.check(), "dma sems:", len(self.dma_sems))
        sems = {}
        for e in ENGS:
            sems[e] = st.enter_context(nc.semaphore("s_" + e))
        for k in self.dma_sems:
            sems[k] = st.enter_context(nc.semaphore("s_" + k))
        block = st.enter_context(nc.Block())
        ops = self.ops

        def body(ename):
            def run(eng):
                for waits, fn, inc, tag in ops[ename]:
                    for k, v in waits:
                        eng.wait_ge(sems[k], v)
                    if fn is None:
                        continue
                    ins = fn(eng)
                    if tag and ANNOTATE:
                        ins.annotate(tag)
                    if inc is not None:
                        ins.then_inc(sems[inc[1]], 1 if inc[0] == "E" else 16)
            return run

        block.sync(body("sync"))
        block.tensor(body("tensor"))
        block.vector(body("vector"))
        block.scalar(body("scalar"))
        block.gpsimd(body("gpsimd"))


class Arena:
    def __init__(self, ap, size):
        self.ap = ap
        self.size = size
        self.top = 0

    def alloc(self, ncols, dtype=BF16):
        n = ncols * (2 if dtype == F32 else 1)
        n = (n + 15) // 16 * 16
        a = self.top
        self.top += n
        assert self.top <= self.size, ("arena overflow", self.top, self.size)
        v = self.ap[:, a:a + ncols * (2 if dtype == F32 else 1)]
        if dtype == F32:
            v = v.bitcast(F32)
        return v

    def mark(self):
        return self.top

    def reset(self, m):
        self.top = m


class Pool:
    def __init__(self, slots):
        self.slots = slots
        self.i = 0

    def next(self):
        s = self.slots[self.i % len(self.slots)]
        self.i += 1
        return s


def pipeline(stages, depth):
    n = len(stages)
    for i in range(min(depth - 1, n)):
        stages[i][0](i % depth)
    for i in range(n):
        j = i + depth - 1
        if j < n:
            stages[j][0](j % depth)
        stages[i][1](i % depth)


PC = {}
_o = 0
for _n, _w in [("ffn1_norm", 8), ("mix_norm", 8), ("ffn2_norm", 8), ("ret_norm", 4), ("mla_q_norm", 3), ("mla_kv_norm", 2), ("mla_q_gain", 1), ("mla_k_gain", 1),
               ("conv_w", 48), ("conv_b", 12), ("ssm_d", 8), ("ssm_norm", 8)]:
    PC[_n] = (_o, _w)
    _o += _w
NPC = _o


def pack_pcols(inp, l):
    pc = np.zeros((128, NPC), np.float32)
    for n in ("ffn1_norm", "mix_norm", "ffn2_norm"):
        o, w = PC[n]
        pc[:, o:o + w] = inp[n][l].reshape(w, 128).T
    o, w = PC["ret_norm"]
    pc[:, o:o + w] = inp["ret_norm"][l].reshape(4, 128).T
    o, w = PC["mla_q_norm"]
    pc[:, o:o + w] = inp["mla_q_norm"][l].reshape(3, 128).T
    o, w = PC["mla_kv_norm"]
    pc[:, o:o + w] = inp["mla_kv_norm"][l].reshape(2, 128).T
    pc[0:96, PC["mla_q_gain"][0]] = inp["mla_q_gain"][l]
    pc[0:96, PC["mla_k_gain"][0]] = inp["mla_k_gain"][l]
    o, w = PC["conv_w"]
    pc[:, o:o + w] = inp["ssm_conv_w"][l].reshape(4, 12, 128).transpose(2, 1, 0).reshape(128, 48)
    o, w = PC["conv_b"]
    pc[:, o:o + w] = inp["ssm_conv_b"][l].reshape(12, 128).T
    o, w = PC["ssm_d"]
    pc[:, o:o + w] = np.repeat(inp["ssm_d"][l], 64).reshape(8, 128).T
    o, w = PC["ssm_norm"]
    pc[:, o:o + w] = inp["ssm_norm"][l].reshape(8, 128).T
    return pc


def ssm_tables():
    U = np.triu(np.ones((128, 128), np.float32))
    NEG = np.where(np.arange(128)[None, :] < np.arange(128)[:, None], -30000.0, 0.0).astype(np.float32)
    NEGt = np.zeros((128, 640), np.float32)
    NEGt[:, 0:512] = np.tile(NEG, (1, 4))
    NEGt[0:16, 512:640] = np.tile(NEG[0:16, 0:16], (1, 8))
    return U, NEGt.astype(ml_dtypes.bfloat16)


def mla_tables():
    pos = token_positions()
    theta = (1.0 / (10000.0 ** (np.arange(0, 32, 2, dtype=np.float32) / 32.0))).astype(np.float32)
    tab = np.zeros((128, 2, T), np.float64)
    for r in range(32):
        f = r % 16
        ang = (pos * theta[f]).astype(np.float32).astype(np.float64)
        tab[64 + r, 0] = np.cos(ang)
        tab[64 + r, 1] = np.sin(ang) * (-1.0 if r < 16 else 1.0)
    mats = np.zeros((128, 4, 128), np.float32)
    for m in range(32):
        mats[64 + (m + 16 if m < 16 else m - 16), 0, 64 + m] = 1.0
    mats[0:96, 1, 0:96] = 1.0
    mats[0, 2, 64:128] = 1.0
    mats[0, 3, 0:64] = -30000.0
    return tab.astype(ml_dtypes.bfloat16), mats.astype(ml_dtypes.bfloat16)


def token_positions():
    return np.concatenate([np.arange(2048), 1024 + np.arange(16), 1024 + np.arange(16)]).astype(np.float32)


RC = {"DmT": (0, 1024), "QD": (1024, 512), "KD": (1536, 512), "cdec": (2048, 4),
      "DmTs": (2052, 128), "QDs": (2180, 64), "KDs": (2244, 512), "cdecs": (2756, 4)}
NRC = 2760


def ret_tables():
    pos = token_positions()
    theta = (1.0 / (10000.0 ** np.linspace(0.0, 1.0, 32, dtype=np.float32))).astype(np.float32)
    p = np.arange(128)
    d = p % 64
    f = d % 32
    ang = (pos[None, :] * theta[f][:, None]).astype(np.float32).astype(np.float64)
    C = np.cos(ang)
    S = np.sin(ang) * np.where(d < 32, -1.0, 1.0)[:, None]
    tab = np.stack([C, S], axis=1).astype(ml_dtypes.bfloat16)
    Pm = np.zeros((128, 128), np.float32)
    for m in range(128):
        Pm[m + 32 if (m % 64) < 32 else m - 32, m] = 1.0
    BD = np.zeros((128, 128), np.float32)
    BD[0:64, 0:64] = 1.0
    BD[64:128, 64:128] = 1.0
    mats = np.stack([Pm, BD], axis=1).astype(ml_dtypes.bfloat16)
    g = 1.0 - 2.0 ** (-5.0 - np.arange(8, dtype=np.float64))
    rc = np.zeros((128, NRC), np.float64)

    def put(name, arr):
        o, w = RC[name]
        a = arr.reshape(arr.shape[0], -1)
        rc[:a.shape[0], o:o + w] = a
    for c, sfx in ((128, ""), (16, "s")):
        k = np.arange(c)[:, None, None]
        q = np.arange(c)[None, None, :]
        gh = g[None, :, None]
        put("DmT" + sfx, np.where(q >= k, gh ** np.maximum(q - k, 0), 0.0))
        gp = g[2 * np.arange(4)[None, :, None] + (p // 64)[:, None, None]]
        put("QD" + sfx, gp ** (np.arange(c)[None, None, :] + 1.0))
        gj = g[2 * np.arange(4)[None, :, None] + (p // 64)[None, None, :]]
        put("KD" + sfx, gj ** (c - 1.0 - np.arange(c)[:, None, None]))
        put("cdec" + sfx, (gp ** float(c))[:, :, 0])
    return tab, mats, rc.astype(np.float32)


def const_tables():
    ident = np.eye(128, dtype=np.float32)
    cb = {}
    cb["ident_b"] = ident.astype(ml_dtypes.bfloat16)
    cb["ones_b"] = np.ones((128, 128), ml_dtypes.bfloat16)
    return ident, cb


def build_program(dbg=None):
    nc = bass.Bass("TRN2", target_bir_lowering=False)
    dt = nc.dram_tensor
    xp = dt("xp", [SEQ, D], F32, kind="ExternalInput").ap()
    xs = dt("xs", [32, D], F32, kind="ExternalInput").ap()
    pcols = dt("pcols", [128, 2, NPC], F32, kind="ExternalInput").ap()
    c_ident = dt("c_ident", [128, 128], F32, kind="ExternalInput").ap()
    c_bf = dt("c_bf", [128, 2, 128], BF16, kind="ExternalInput").ap()
    wgu = [dt("ffn%d_wgu" % i, [2, D, 2 * DFF], F32, kind="ExternalInput").ap() for i in (1, 2)]
    wd = [dt("ffn%d_wd" % i, [2, DFF, D], F32, kind="ExternalInput").ap() for i in (1, 2)]
    w_in = dt("w_in", [2, D, IN_COLS], F32, kind="ExternalInput").ap()
    w_out = dt("w_out", [2, 2048, D], F32, kind="ExternalInput").ap()
    st_ret = dt("st_ret", [2, 2, 8, 64, 64], F32, kind="ExternalInput").ap()
    c_rtab = dt("c_rtab", [128, 2, T], BF16, kind="ExternalInput").ap()
    c_rmat = dt("c_rmat", [128, 2, 128], BF16, kind="ExternalInput").ap()
    c_rc = dt("c_rc", [128, NRC], F32, kind="ExternalInput").ap()
    o_ret_p = dt("o_ret_p", [2, 8, 64, 64], F32, kind="ExternalOutput").ap()
    o_ret_s = dt("o_ret_s", [2, 2, 8, 64, 64], F32, kind="ExternalOutput").ap()
    w_uq = dt("w_uq", [2, 384, 768], F32, kind="ExternalInput").ap()
    w_ukv = dt("w_ukv", [2, 256, 1024], F32, kind="ExternalInput").ap()
    c_ckv = dt("c_ckv", [2, 2, 1024, 256], F32, kind="ExternalInput").ap()
    c_kr = dt("c_kr", [2, 2, 1024, 32], F32, kind="ExternalInput").ap()
    c_mtab = dt("c_mtab", [128, 2, T], BF16, kind="ExternalInput").ap()
    c_mmat = dt("c_mmat", [128, 4, 128], BF16, kind="ExternalInput").ap()
    o_ckv_p = dt("o_ckv_p", [2, SEQ, 256], F32, kind="ExternalOutput").ap()
    o_kr_p = dt("o_kr_p", [2, SEQ, 32], F32, kind="ExternalOutput").ap()
    o_ckv_s = dt("o_ckv_s", [2, 32, 256], F32, kind="ExternalOutput").ap()
    o_kr_s = dt("o_kr_s", [2, 32, 32], F32, kind="ExternalOutput").ap()
    c_ssm = dt("c_ssm", [2, 2, 16, 64, 128], F32, kind="ExternalInput").ap()
    c_conv = dt("c_conv", [2, 2, 3, 1536], F32, kind="ExternalInput").ap()
    prow = dt("prow", [128, 2, 128], F32, kind="ExternalInput").ap()
    c_U = dt("c_U", [128, 128], F32, kind="ExternalInput").ap()
    c_NEG = dt("c_NEG", [128, 640], BF16, kind="ExternalInput").ap()
    o_ssm_p = dt("o_ssm_p", [2, 16, 64, 128], F32, kind="ExternalOutput").ap()
    o_ssm_s = dt("o_ssm_s", [2, 2, 16, 64, 128], F32, kind="ExternalOutput").ap()
    o_conv_p = dt("o_conv_p", [2, 3, 1536], F32, kind="ExternalOutput").ap()
    o_conv_s = dt("o_conv_s", [2, 2, 3, 1536], F32, kind="ExternalOutput").ap()
    dbg_out = {}
    yp = dt("yp", [SEQ, D], F32, kind="ExternalOutput").ap()
    ys = dt("ys", [32, D], F32, kind="ExternalOutput").ap()

    st = contextlib.ExitStack()
    with st:
        P = Prog()
        xT_t = st.enter_context(nc.sbuf_tensor("xT", [128, 8, T], F32))
        xT = xT_t[:]
        xB = [[Buf("x%d_%d" % (c, i)) for i in range(5)] for c in range(8)]
        WA_t = st.enter_context(nc.sbuf_tensor("WA", [128, 2, 4096], BF16))
        WA = [(WA_t[:, i, :], Buf("WA%d" % i)) for i in range(2)]
        cst_t = st.enter_context(nc.sbuf_tensor("cst", [128, 2 * NPC + 128], F32))
        pc_sb = cst_t[:, 0:2 * NPC].rearrange("p (l n) -> p l n", l=2)
        ident_f = cst_t[:, 2 * NPC:2 * NPC + 128]
        cbf_t = st.enter_context(nc.sbuf_tensor("cbf", [128, 2, 128], BF16))
        ident_b = cbf_t[:, 0, :]
        ones_b = cbf_t[:, 1, :]
        bC = Buf("consts")
        ARENA = 63800
        ar_t = st.enter_context(nc.sbuf_tensor("arena", [128, ARENA], BF16))
        AR = Arena(ar_t[:], ARENA)
        psum = []
        for i in range(8):
            pt = st.enter_context(nc.psum_tensor("ps%d" % i, [128, 512], F32))
            psum.append((pt[:], Buf("ps%d" % i, excl=True)))
        PS = Pool(psum[2:8])
        PSA = Pool(psum[0:2])
        outbufs = []

        def pcol(l, name, c):
            o, w = PC[name]
            return pc_sb[:, l, o + c:o + c + 1]

        P.dma("sync", lambda e: e.dma_start(out=pc_sb, in_=pcols), writes=[bC])
        bC2 = Buf("c2")
        P.dma("sync", lambda e: e.dma_start(out=ident_f, in_=c_ident), writes=[bC2])
        bC3 = Buf("c3")
        P.dma("sync", lambda e: e.dma_start(out=cbf_t[:], in_=c_bf), writes=[bC3])
        CONSTS = [bC, bC2, bC3]

        m0 = AR.mark()
        stg = [(AR.alloc(1024, F32), Buf("stg%d" % i)) for i in range(2)]
        for i in range(17):
            s_ap, s_b = stg[i % 2]
            if i < 16:
                rows = 128
                P.dma("sync", lambda e, s_ap=s_ap, i=i: e.dma_start(out=s_ap, in_=xp[i * 128:(i + 1) * 128, :]), writes=[s_b])
                c0 = i * 128
                ti = i // 4
            else:
                rows = 32
                P.dma("sync", lambda e, s_ap=s_ap: e.dma_start(out=s_ap[0:32, :], in_=xs), writes=[s_b])
                c0 = 2048
                ti = 4
            for hb in range(2):
                p_ap, p_b = PS.next()
                for cc in range(4):
                    c = hb * 4 + cc
                    P.op("tensor", lambda e, p_ap=p_ap, s_ap=s_ap, c=c, cc=cc, rows=rows: e.transpose(
                        out=p_ap[:, cc * rows:(cc + 1) * rows], in_=s_ap[0:rows, c * 128:(c + 1) * 128], identity=ident_f[0:rows, 0:rows]),
                        reads=[s_b] + CONSTS, writes=[p_b], inc=(cc == 3))
                eng = "scalar" if hb == 0 else "vector"
                src = p_ap[:, 0:4 * rows].rearrange("p (c t) -> p c t", c=4)
                dst = xT[:, hb * 4:hb * 4 + 4, c0:c0 + rows]
                if eng == "scalar":
                    P.op("scalar", lambda e, src=src, dst=dst: e.activation(out=dst, in_=src, func=AF.Copy),
                         reads=[p_b], writes=[xB[c][ti] for c in range(hb * 4, hb * 4 + 4)])
                else:
                    P.op("vector", lambda e, src=src, dst=dst: e.tensor_copy(out=dst, in_=src),
                         reads=[p_b], writes=[xB[c][ti] for c in range(hb * 4, hb * 4 + 4)])
        P.barrier(skip=("gpsimd",))
        AR.reset(m0)

        def rms_norm_fm(src_chunks, nch, gain_fn, dst_fn, n, inv_count, tmp, dq=None):
            sq, rs = tmp
            p_ap, p_b = PS.next()
            for c in range(nch):
                s_ap, s_bufs = src_chunks[c]
                q_ap, q_b = sq.next()
                P.op("scalar", lambda e, q_ap=q_ap, s_ap=s_ap: e.activation(out=q_ap[:, 0:n], in_=s_ap, func=AF.Square),
                     reads=s_bufs, writes=[q_b])
                P.op("tensor", lambda e, p_ap=p_ap, q_ap=q_ap, c=c: e.matmul(p_ap[:, 0:n], lhsT=ones_b, rhs=q_ap[:, 0:n], start=(c == 0), stop=(c == nch - 1)),
                     reads=[q_b] + CONSTS, writes=[p_b])
            def finish():
                r_ap, r_b = rs.next()
                P.op("scalar", lambda e: e.activation(out=r_ap[:, 0:n], in_=p_ap[:, 0:n], func=AF.Ln, bias=EPS, scale=inv_count),
                     reads=[p_b], writes=[r_b])
                P.op("scalar", lambda e: e.activation(out=r_ap[:, 0:n], in_=r_ap[:, 0:n], func=AF.Exp, scale=-0.5),
                     reads=[r_b], writes=[r_b])
                for c in range(nch):
                    s_ap, s_bufs = src_chunks[c]
                    d_ap, d_bufs = dst_fn(c)
                    g_ap = gain_fn(c)
                    P.op("vector", lambda e, d_ap=d_ap, s_ap=s_ap, g_ap=g_ap: e.scalar_tensor_tensor(
                        out=d_ap, in0=s_ap, scalar=g_ap, in1=r_ap[:, 0:n], op0=ALU.mult, op1=ALU.mult),
                        reads=s_bufs + [r_b] + CONSTS, writes=d_bufs)

            if dq is None:
                finish()
            else:
                while dq:
                    dq.pop()()
                dq.append(finish)

        def ffn(l, which):
            wgu_ap = wgu[0 if which == "ffn1" else 1]
            wd_ap = wd[0 if which == "ffn1" else 1]
            nname = which + "_norm"
            m = AR.mark()
            HB = 1056
            hT = AR.alloc(8 * HB).rearrange("p (c t) -> p c t", c=8)
            aT = AR.alloc(NKF * HB).rearrange("p (c t) -> p c t", c=NKF)
            WB = [(AR.alloc(NKF * 256).rearrange("p (k n) -> p k n", k=NKF), Buf("WB%d" % i)) for i in range(2)]
            sq = Pool([(AR.alloc(512), Buf("sq%d" % i)) for i in range(2)])
            rs = Pool([(AR.alloc(512, F32), Buf("rs%d" % i)) for i in range(2)])
            sg = Pool([(AR.alloc(512, F32), Buf("sg%d" % i)) for i in range(2)])
            hB = [[Buf("h%d_%d" % (c, ti)) for ti in range(2)] + [Buf("h%d_%d" % (c, ti)) for ti in range(3)] for c in range(8)]
            aB = [[Buf("a%d_%d" % (j, ti)) for ti in range(2)] + [Buf("a%d_%d" % (j, ti)) for ti in range(3)] for j in range(NKF)]
            for c in range(8):
                hB[c][2] = hB[c][0]; hB[c][3] = hB[c][1]
            for j in range(NKF):
                aB[j][2] = aB[j][0]; aB[j][3] = aB[j][1]
            def half_parts(half):
                tiles = [0, 1] if half == 0 else [2, 3, 4]
                hb0 = 0 if half == 0 else 1024

                def do_norm():
                    dq = []
                    for ti in tiles:
                        c0, n = TILES[ti]
                        rms_norm_fm([(xT[:, c, c0:c0 + n], [xB[c][ti]]) for c in range(8)], 8,
                                    lambda c: pcol(l, nname, c),
                                    lambda c, ti=ti, c0=c0, n=n: (hT[:, c, c0 - hb0:c0 - hb0 + n], [hB[c][ti]]),
                                    n, 1.0 / D, (sq, rs), dq=dq)
                    while dq:
                        dq.pop()()

                def do_gu():
                    P.tag = "ffn.gu"
                    stages = []
                    for g in range(11):
                        def load(slot, g=g):
                            w_ap, w_b = WA[slot]
                            wv = w_ap.rearrange("p (u k n) -> p u k n", u=2, k=8)
                            src = wgu_ap[l].rearrange("(k p) n -> p k n", p=128)
                            P.dma("gpsimd", lambda e: e.dma_start(out=wv[:, 0], in_=src[:, :, g * 256:(g + 1) * 256]), writes=[w_b])
                            P.dma("gpsimd", lambda e: e.dma_start(out=wv[:, 1], in_=src[:, :, DFF + g * 256:DFF + (g + 1) * 256]), writes=[w_b])

                        def comp(slot, g=g):
                            w_ap, w_b = WA[slot]
                            wv = w_ap.rearrange("p (u k n) -> p u k n", u=2, k=8)
                            for pr in range(2):
                                j = g * 2 + pr
                                for ti in tiles:
                                    c0, n = TILES[ti]
                                    cs = slice(c0 - hb0, c0 - hb0 + n)
                                    pg, pgb = PS.next()
                                    pu, pub = PS.next()
                                    for u, (pp, ppb) in enumerate(((pg, pgb), (pu, pub))):
                                        for k in range(8):
                                            P.op("tensor", lambda e, pp=pp, u=u, k=k, pr=pr, cs=cs, n=n: e.matmul(
                                                pp[:, 0:n], lhsT=wv[:, u, k, pr * 128:(pr + 1) * 128], rhs=hT[:, k, cs], start=(k == 0), stop=(k == 7)),
                                                reads=[w_b, hB[k][ti]], writes=[ppb], inc=(k == 7))
                                    s_ap, s_b = sg.next()
                                    P.op("scalar", lambda e, s_ap=s_ap, pg=pg, n=n: e.activation(out=s_ap[:, 0:n], in_=pg[:, 0:n], func=AF.Silu),
                                         reads=[pgb], writes=[s_b])
                                    P.op("vector", lambda e, s_ap=s_ap, pu=pu, j=j, cs=cs, n=n: e.tensor_tensor(
                                        out=aT[:, j, cs], in0=pu[:, 0:n], in1=s_ap[:, 0:n], op=ALU.mult),
                                        reads=[pub, s_b], writes=[aB[j][ti]])
                        stages.append((load, comp))
                    pipeline(stages, 2)

                def do_down():
                    P.tag = "ffn.down"
                    stages = []
                    for mb in range(4):
                        def load(slot, mb=mb):
                            w_ap, w_b = WB[slot]
                            src = wd_ap[l].rearrange("(k p) n -> p k n", p=128)
                            P.dma("gpsimd", lambda e: e.dma_start(out=w_ap, in_=src[:, :, mb * 256:(mb + 1) * 256]), writes=[w_b])

                        def comp(slot, mb=mb):
                            w_ap, w_b = WB[slot]
                            for mm in range(2):
                                mo = mb * 2 + mm
                                for ti in tiles:
                                    c0, n = TILES[ti]
                                    cs = slice(c0 - hb0, c0 - hb0 + n)
                                    pp, ppb = PS.next()
                                    for k in range(NKF):
                                        P.op("tensor", lambda e, pp=pp, k=k, mm=mm, cs=cs, n=n: e.matmul(
                                            pp[:, 0:n], lhsT=w_ap[:, k, mm * 128:(mm + 1) * 128], rhs=aT[:, k, cs], start=(k == 0), stop=(k == NKF - 1)),
                                            reads=[w_b, aB[k][ti]], writes=[ppb], inc=(k == NKF - 1))
                                    P.op("vector", lambda e, pp=pp, mo=mo, c0=c0, n=n: e.scalar_tensor_tensor(
                                        out=xT[:, mo, c0:c0 + n], in0=pp[:, 0:n], scalar=0.5, in1=xT[:, mo, c0:c0 + n], op0=ALU.mult, op1=ALU.add),
                                        reads=[ppb, xB[mo][ti]], writes=[xB[mo][ti]])
                        stages.append((load, comp))
                    pipeline(stages, 2)

                return do_norm, do_gu, do_down

            nA, gA, dA = half_parts(0)
            nB, gB, dB = half_parts(1)
            nA(); gA(); nB(); dA(); gB(); dB()
            P.barrier()
            AR.reset(m)

        def MM(out, lhsT, rhs, start, stop, reads, writes, inc=True, sgc=False):
            P.op("tensor", lambda e: e.matmul(out, lhsT=lhsT, rhs=rhs, start=start, stop=stop, skip_group_check=sgc), reads=reads, writes=writes, inc=inc)

        def TR(out, in_, ident, reads, writes, inc=True):
            P.op("tensor", lambda e: e.transpose(out=out, in_=in_, identity=ident), reads=reads, writes=writes, inc=inc)

        def ACT(out, in_, func, reads, writes, **kw):
            P.op("scalar", lambda e: e.activation(out=out, in_=in_, func=func, **kw), reads=reads, writes=writes)

        def TT(eng, out, in0, in1, op, reads, writes):
            if eng == "gpsimd":
                eng = _DBG_ENV.get("POOL_ENG", "gpsimd")
            P.op(eng, lambda e: e.tensor_tensor(out=out, in0=in0, in1=in1, op=op), reads=reads, writes=writes)

        def STT(out, in0, scalar, in1, op0, op1, reads, writes):
            P.op("vector", lambda e: e.scalar_tensor_tensor(out=out, in0=in0, scalar=scalar, in1=in1, op0=op0, op1=op1), reads=reads, writes=writes)

        def TS(eng, out, in0, s1, s2, op0, op1, reads, writes):
            P.op(eng, lambda e: e.tensor_scalar(out=out, in0=in0, scalar1=s1, scalar2=s2, op0=op0, op1=op1), reads=reads, writes=writes)

        def CP(eng, out, in_, reads, writes):
            if eng == "scalar":
                P.op("scalar", lambda e: e.activation(out=out, in_=in_, func=AF.Copy), reads=reads, writes=writes)
            else:
                P.op(eng, lambda e: e.tensor_copy(out=out, in_=in_), reads=reads, writes=writes)

        def MEMSET(eng, ap, val, writes):
            P.op(eng, lambda e: e.memset(ap, val), writes=writes)

        def LOAD(eng, out, in_, wbuf, reads=(), slow=False):
            P.dma(eng, lambda e: e.dma_start(out=out, in_=in_, allow_slow_non_contiguous=slow), reads=list(reads), writes=[wbuf])

        def STORE(out, in_, sbuf_buf, slow=False):
            ob = Buf("o")
            P.dma("sync", lambda e: e.dma_start(out=out, in_=in_, allow_slow_non_contiguous=slow), reads=[sbuf_buf], writes=[ob], sembuf=sbuf_buf)
            outbufs.append(ob)

        def dump(name, ap, bufs, dtype=F32):
            shp = list(ap.shape)
            t_ = dt("dbg_" + name, shp, dtype, kind="ExternalOutput").ap()
            for b in bufs[:1]:
                ob = Buf("o")
                P.dma("sync", lambda e: e.dma_start(out=t_, in_=ap), reads=list(bufs), writes=[ob], sembuf=Buf("dsem"))
                outbufs.append(ob)

        def xacc(l, wrow0, nk, mixT, mixB, tag):
            P.tag = "xacc." + tag
            for half in range(nk // 4):
                w_ap, w_b = WA[half % 2] if nk > 4 else WA[0]
                wv = w_ap.rearrange("p (k n) -> p k n", k=4)
                src = w_out[l][wrow0 + half * 512:wrow0 + (half + 1) * 512, :].rearrange("(k p) n -> p k n", p=128)
                LOAD("gpsimd", wv, src, w_b)
                for mo in range(8):
                    for ti in range(5):
                        c0, n = TILES[ti]
                        pp, ppb = PS.next()
                        for k in range(4):
                            MM(pp[:, 0:n], wv[:, k, mo * 128:(mo + 1) * 128], mixT[:, half * 4 + k, c0:c0 + n], k == 0, k == 3,
                               [w_b, mixB[half * 4 + k][ti]], [ppb], inc=(k == 3))
                        TT("vector", xT[:, mo, c0:c0 + n], pp[:, 0:n], xT[:, mo, c0:c0 + n], ALU.add, [ppb, xB[mo][ti]], [xB[mo][ti]])

        def mixer(l, stop_after=None):
            mM = AR.mark()
            hT_flat = AR.alloc(8 * T)
            hT = hT_flat.rearrange("p (c t) -> p c t", c=8)
            hB = [[Buf("mh%d_%d" % (c, ti)) for ti in range(5)] for c in range(8)]
            mt = AR.mark()
            sq = Pool([(AR.alloc(512), Buf("sq%d" % i)) for i in range(2)])
            rs = Pool([(AR.alloc(512, F32), Buf("rs%d" % i)) for i in range(2)])
            dq = []
            for ti in range(5):
                c0, n = TILES[ti]
                rms_norm_fm([(xT[:, c, c0:c0 + n], [xB[c][ti]]) for c in range(8)], 8,
                            lambda c: pcol(l, "mix_norm", c),
                            lambda c, ti=ti, c0=c0, n=n: (hT[:, c, c0:c0 + n], [hB[c][ti]]),
                            n, 1.0 / D, (sq, rs), dq=dq)
            while dq:
                dq.pop()()
            P.barrier()
            AR.reset(mt)
            if dbg not in ("mla", "ssm"):
                group_ret(l, hT, hB)
            if dbg == "ret":
                return
            if dbg != "mla":
                group_ssm(l, hT, hB)
            if dbg == "ssm":
                return
            group_mla(l, hT, hB, hT_flat)
            P.barrier()
            AR.reset(mM)

        def group_ret(l, hT, hB):
            mR = AR.mark()
            rtab = AR.alloc(2 * T).rearrange("p (a t) -> p a t", a=2)
            rmat = AR.alloc(256).rearrange("p (a t) -> p a t", a=2)
            rc = AR.alloc(NRC, F32)
            bRT, bRM, bRC = Buf("rtab"), Buf("rmat"), Buf("rc")
            LOAD("sync", rtab, c_rtab, bRT)
            LOAD("sync", rmat, c_rmat, bRM)
            LOAD("sync", rc, c_rc, bRC)
            Ctab, Stab = rtab[:, 0, :], rtab[:, 1, :]
            Pm, BD = rmat[:, 0, :], rmat[:, 1, :]

            def rcv(name, shape3=None):
                o, w = RC[name]
                v = rc[:, o:o + w]
                return v
            DmT = rcv("DmT").rearrange("p (h q) -> p h q", h=8)
            QD = rcv("QD").rearrange("p (h q) -> p h q", h=4)
            KD = rcv("KD").rearrange("p (h q) -> p h q", h=4)
            cdec = rcv("cdec")
            DmTs = rcv("DmTs").rearrange("p (h q) -> p h q", h=8)
            QDs = rcv("QDs").rearrange("p (h q) -> p h q", h=4)
            KDs = rcv("KDs").rearrange("p (h q) -> p h q", h=4)
            cdecs = rcv("cdecs")
            RCB = [bRT, bRM, bRC]

            mixT = AR.alloc(4 * T).rearrange("p (c t) -> p c t", c=4)
            mixB = [[Buf("rm%d_%d" % (c, ti)) for ti in range(5)] for c in range(4)]
            qT, kT, qdT, sgT = AR.alloc(T), AR.alloc(T), AR.alloc(T), AR.alloc(T)
            v_tm = AR.alloc(18 * 128).rearrange("p (i d) -> p i d", i=18)
            kd_tm = AR.alloc(18 * 128).rearrange("p (i d) -> p i d", i=18)
            S_bf = AR.alloc(18 * 64).rearrange("p (i d) -> p i d", i=18)
            S_f = [AR.alloc(64, F32) for _ in range(3)]
            rawp = Pool([(AR.alloc(512), Buf("raw%d" % i)) for i in range(2)])
            t1p = Pool([(AR.alloc(512, F32), Buf("t1_%d" % i)) for i in range(2)])
            t2p = Pool([(AR.alloc(512, F32), Buf("t2_%d" % i)) for i in range(2)])
            attp = Pool([(AR.alloc(128), Buf("att%d" % i)) for i in range(3)])
            sqp = Pool([(AR.alloc(512), Buf("rsq%d" % i)) for i in range(2)])
            rsp = Pool([(AR.alloc(512, F32), Buf("rrs%d" % i)) for i in range(2)])
            psb_view = lambda p_ap: p_ap.bitcast(BF16)

            qB = [Buf("q%d" % ti) for ti in range(5)]
            kB = [Buf("k%d" % ti) for ti in range(5)]
            qdB = [Buf("qd%d" % ti) for ti in range(5)]
            sgB = [Buf("sg%d" % ti) for ti in range(5)]
            vB = [Buf("v%d" % i) for i in range(18)]
            kdB = [Buf("kd%d" % i) for i in range(18)]
            SbB = [Buf("Sb%d" % i) for i in range(18)]
            SfB = [Buf("Sf%d" % i) for i in range(3)]
            for hp in range(4):
                if _DBG_ENV.get("RET_STOP") == "0":
                    break
                w_ap, w_b = WA[hp % 2]
                wv = w_ap.rearrange("p (u k n) -> p u k n", u=4, k=8)[:, :, :, 0:128] if False else w_ap[:, 0:4096].rearrange("p (u k n) -> p u k n", u=4, k=8)
                srcw = w_in[l].rearrange("(k p) n -> p k n", p=128)
                for u in range(4):
                    LOAD("gpsimd", wv[:, u], srcw[:, :, RET0 + 512 * u + 128 * hp:RET0 + 512 * u + 128 * hp + 128], w_b)
                for s_ in range(2):
                    LOAD("sync", S_f[1 + s_], st_ret[l, s_, 2 * hp:2 * hp + 2].rearrange("h k v -> (h k) v"), SfB[1 + s_])
                MEMSET("vector", S_f[0], 0.0, [SfB[0]])
                MEMSET("vector", S_bf[:, 0, :], 0.0, [SbB[0]])
                for s_ in range(2):
                    CP("scalar", S_bf[:, 16 + s_, :], S_f[1 + s_], [SfB[1 + s_]], [SbB[16 + s_]])
                if _DBG_ENV.get("RET_STOP") == "A1":
                    break
                P.tag = "ret.proj"
                pendB = None
                for ti in range(5):
                    c0, n = TILES[ti]
                    cs = slice(c0, c0 + n)
                    for u, (dst, dB, scale) in enumerate(((qT, qB, 1.0), (kT, kB, 0.125))):
                        pp, ppb = PS.next()
                        for k in range(8):
                            MM(pp[:, 0:n], wv[:, u, k, :], hT[:, k, cs], k == 0, k == 7, [w_b, hB[k][ti]], [ppb], inc=(k == 7))
                        r_ap, r_b = rawp.next()
                        CP("scalar", r_ap[:, 0:n], pp[:, 0:n], [ppb], [r_b])
                        if pendB is not None:
                            pendB()

                        def pendB(pp=pp, ppb=ppb, r_ap=r_ap, r_b=r_b, n=n, cs=cs, scale=scale, dst=dst, dB=dB, ti=ti, u=u):
                            p2, p2b = PS.next()
                            MM(p2[:, 0:n], Pm, r_ap[:, 0:n], True, True, [r_b] + RCB, [p2b])
                            a1, a1b = t1p.next()
                            a2, a2b = t2p.next()
                            STT(a1[:, 0:n], pp[:, 0:n], scale, Ctab[:, cs], ALU.mult, ALU.mult, [ppb] + RCB, [a1b])
                            STT(a2[:, 0:n], p2[:, 0:n], scale, Stab[:, cs], ALU.mult, ALU.mult, [p2b] + RCB, [a2b])
                            TT("vector", dst[:, cs], a1[:, 0:n], a2[:, 0:n], ALU.add, [a1b, a2b], [dB[ti]])
                            if u == 0:
                                if ti < 4:
                                    TT("vector", qdT[:, cs].rearrange("p (a b) -> p a b", b=128), qT[:, cs].rearrange("p (a b) -> p a b", b=128),
                                       QD[:, hp, :].unsqueeze(1).broadcast_to([128, 4, 128]), ALU.mult, [qB[ti]] + RCB, [qdB[ti]])
                                else:
                                    TT("vector", qdT[:, cs].rearrange("p (a b) -> p a b", b=16), qT[:, cs].rearrange("p (a b) -> p a b", b=16),
                                       QDs[:, hp, :].unsqueeze(1).broadcast_to([128, 2, 16]), ALU.mult, [qB[ti]] + RCB, [qdB[ti]])
                    pp, ppb = PS.next()
                    for k in range(8):
                        MM(pp[:, 0:n], wv[:, 3, k, :], hT[:, k, cs], k == 0, k == 7, [w_b, hB[k][ti]], [ppb], inc=(k == 7))
                    ACT(sgT[:, cs], pp[:, 0:n], AF.Silu, [ppb], [sgB[ti]])
                pendB()
                if _DBG_ENV.get("RET_STOP", "")[0:1] in ("A", "L"):
                    break
                P.tag = "ret.vk"
                for ti in range(4):
                    pp, ppb = PS.next()
                    for j in range(4):
                        i = ti * 4 + j
                        for k in range(8):
                            MM(pp[:, j * 128:(j + 1) * 128], hT[:, k, i * 128:(i + 1) * 128], wv[:, 2, k, :], k == 0, k == 7,
                               [w_b, hB[k][ti]], [ppb], inc=(k == 7 and j == 3))
                    CP("scalar", v_tm[:, ti * 4:ti * 4 + 4, :], pp[:, :].rearrange("p (a b) -> p a b", b=128), [ppb], vB[ti * 4:ti * 4 + 4])
                    pt, ptb = PS.next()
                    ptv = psb_view(pt)
                    for j in range(4):
                        i = ti * 4 + j
                        TR(ptv[:, j * 128:(j + 1) * 128], kT[:, i * 128:(i + 1) * 128], ident_b, [kB[ti]] + CONSTS, [ptb], inc=(j == 3))
                    TT("vector", kd_tm[:, ti * 4:ti * 4 + 4, :], ptv[:, 0:512].rearrange("p (a b) -> p a b", b=128),
                       KD[:, hp, :].unsqueeze(1).broadcast_to([128, 4, 128]), ALU.mult, [ptb] + RCB, kdB[ti * 4:ti * 4 + 4])
                pp, ppb = PS.next()
                pt, ptb = PS.next()
                ptv = psb_view(pt)
                for s_ in range(2):
                    sc = slice(2048 + 16 * s_, 2064 + 16 * s_)
                    for k in range(8):
                        MM(pp[0:16, s_ * 128:(s_ + 1) * 128], hT[:, k, sc], wv[:, 2, k, :], k == 0, k == 7, [w_b, hB[k][4]], [ppb], inc=(k == 7 and s_ == 1))
                    TR(ptv[0:16, s_ * 128:(s_ + 1) * 128], kT[:, sc], ident_b, [kB[4]] + CONSTS, [ptb], inc=(s_ == 1))
                CP("scalar", v_tm[0:16, 16:18, :], pp[0:16, 0:256].rearrange("p (a b) -> p a b", b=128), [ppb], vB[16:18])
                TT("vector", kd_tm[0:16, 16:18, :], ptv[0:16, 0:256].rearrange("p (a b) -> p a b", b=128),
                   KDs[0:16, hp, :].unsqueeze(1).broadcast_to([16, 2, 128]), ALU.mult, [ptb] + RCB, kdB[16:18])
                if _DBG_ENV.get("RET_STOP") == "B":
                    break
                P.tag = "ret.kv"
                kvps = []
                for ti in range(4):
                    pp, ppb = PS.next()
                    for j in range(4):
                        i = ti * 4 + j
                        MM(pp[:, j * 128:(j + 1) * 128], kd_tm[:, i, :], v_tm[:, i, :], True, True, [kdB[i], vB[i]], [ppb], inc=(j == 3))
                    kvps.append((pp, ppb))
                    for j in range(4):
                        i = ti * 4 + j
                        for h2 in range(2):
                            r = slice(64 * h2, 64 * h2 + 64)
                            STT(S_f[0][r, :], S_f[0][r, :], cdec[r, hp:hp + 1], pp[r, j * 128 + 64 * h2:j * 128 + 64 * h2 + 64], ALU.mult, ALU.add,
                                [ppb, SfB[0]] + RCB, [SfB[0]])
                        if i < 15:
                            CP("vector", S_bf[:, i + 1, :], S_f[0], [SfB[0]], [SbB[i + 1]])
                STORE(o_ret_p[l, 2 * hp:2 * hp + 2].rearrange("h k v -> (h k) v"), S_f[0], SfB[0])
                pp, ppb = PS.next()
                for s_ in range(2):
                    MM(pp[:, s_ * 128:(s_ + 1) * 128], kd_tm[0:16, 16 + s_, :], v_tm[0:16, 16 + s_, :], True, True, [kdB[16 + s_], vB[16 + s_]], [ppb], inc=(s_ == 1))
                for s_ in range(2):
                    for h2 in range(2):
                        r = slice(64 * h2, 64 * h2 + 64)
                        STT(S_f[1 + s_][r, :], S_f[1 + s_][r, :], cdecs[r, hp:hp + 1], pp[r, s_ * 128 + 64 * h2:s_ * 128 + 64 * h2 + 64], ALU.mult, ALU.add,
                            [ppb, SfB[1 + s_], SbB[16 + s_]] + RCB, [SfB[1 + s_]])
                    STORE(o_ret_s[l, s_, 2 * hp:2 * hp + 2].rearrange("h k v -> (h k) v"), S_f[1 + s_], SfB[1 + s_])
                if _DBG_ENV.get("RET_STOP") == "C":
                    break
                P.tag = "ret.attn"
                post_state = [None]
                for ti in range(5):
                    c0, n = TILES[ti]
                    po, pob = PSA.next()
                    pend = None
                    nunit = 0
                    nchunk = 4 if ti < 4 else 2
                    cl = 128 if ti < 4 else 16
                    for j in range(nchunk):
                        i = ti * 4 + j if ti < 4 else 16 + j
                        cc = slice(c0 + j * cl, c0 + (j + 1) * cl)
                        for h2 in range(2):
                            r = slice(64 * h2, 64 * h2 + 64)
                            hd = 2 * hp + h2
                            ps_, psb_ = PS.next()
                            MM(ps_[0:cl, 0:cl], kT[r, cc], qT[r, cc], True, True, [kB[ti], qB[ti]], [psb_])
                            at, atb = attp.next()
                            dm = DmT[0:cl, hd, 0:cl] if ti < 4 else DmTs[0:cl, hd, 0:cl]
                            TT("vector", at[0:cl, 0:cl], ps_[0:cl, 0:cl], dm, ALU.mult, [psb_] + RCB, [atb])
                            if pend is not None:
                                pend()
                            nunit += 1
                            if nunit == 3 and post_state[0] is not None:
                                post_state[0]()
                                post_state[0] = None

                            def pend(po=po, pob=pob, r=r, j=j, cl=cl, i=i, at=at, atb=atb, cc=cc, ti=ti):
                                MM(po[r, j * cl:(j + 1) * cl], v_tm[0:cl, i, r], at[0:cl, 0:cl], True, False, [vB[i], atb], [pob], inc=False)
                                MM(po[r, j * cl:(j + 1) * cl], S_bf[r, i, :], qdT[r, cc], False, True, [SbB[i], qdB[ti]], [pob])
                    pend()
                    pend = None
                    o_sb, o_sbb = t1p.next()
                    CP("scalar", o_sb[:, 0:n], po[:, 0:n], [pob], [o_sbb])
                    q_ap, q_b = sqp.next()
                    ACT(q_ap[:, 0:n], po[:, 0:n], AF.Square, [pob], [q_b])
                    p2, p2b = PS.next()
                    MM(p2[:, 0:n], BD, q_ap[:, 0:n], True, True, [q_b] + RCB, [p2b])

                    def post2(p2=p2, p2b=p2b, o_sb=o_sb, o_sbb=o_sbb, n=n, c0=c0, ti=ti, hp=hp):
                        r_ap, r_b = rsp.next()
                        ACT(r_ap[:, 0:n], p2[:, 0:n], AF.Ln, [p2b], [r_b], bias=EPS, scale=1.0 / 64)
                        ACT(r_ap[:, 0:n], r_ap[:, 0:n], AF.Exp, [r_b], [r_b], scale=-0.5)
                        a2, a2b = t2p.next()
                        STT(a2[:, 0:n], o_sb[:, 0:n], pcol(l, "ret_norm", hp), r_ap[:, 0:n], ALU.mult, ALU.mult, [o_sbb, r_b] + CONSTS, [a2b])
                        TT("vector", mixT[:, hp, c0:c0 + n], a2[:, 0:n], sgT[:, c0:c0 + n], ALU.mult, [a2b, sgB[ti]], [mixB[hp][ti]])
                    post_state[0] = post2
                if post_state[0] is not None:
                    post_state[0]()
                    post_state[0] = None
            if dbg == "ret":
                dump("retmix", mixT, [mixB[c][ti] for c in range(4) for ti in range(5)], BF16)
            xacc(l, 0, 4, mixT, mixB, "ret")
            P.barrier()
            AR.reset(mR)

        def group_mla(l, hT, hB, hT_region):
            mR = AR.mark()
            mtab = AR.alloc(2 * T).rearrange("p (a t) -> p a t", a=2)
            mmat = AR.alloc(4 * 128).rearrange("p (a t) -> p a t", a=4)
            bMT, bMM = Buf("mtab"), Buf("mmat")
            LOAD("sync", mtab, c_mtab, bMT)
            LOAD("sync", mmat, c_mmat, bMM)
            MCB = [bMT, bMM]
            Cm, Sm = mtab[:, 0, :], mtab[:, 1, :]
            Pm96, ones96, mrow = mmat[:, 0, :], mmat[:, 1, :], mmat[:, 2:4, :]
            mixT = AR.alloc(4 * T).rearrange("p (c t) -> p c t", c=4)
            mixB = [[Buf("mm%d_%d" % (c, ti)) for ti in range(5)] for c in range(4)]
            cqnT = AR.alloc(3 * T).rearrange("p (c t) -> p c t", c=3)
            cqB = [Buf("cq%d" % ti) for ti in range(5)]
            ckvT = AR.alloc(2 * 2048).rearrange("p (c t) -> p c t", c=2)
            ckvB = [Buf("ckv%d" % i) for i in range(4)]
            ckvN = AR.alloc(2 * 32).rearrange("p (c t) -> p c t", c=2)
            ckvNB = Buf("ckvN")
            krT = AR.alloc(2048)
            krB = [Buf("kr%d" % i) for i in range(4)]
            krN = AR.alloc(32)
            krNB = Buf("krN")
            qT = AR.alloc(2048)
            qB = [Buf("mq%d" % i) for i in range(4)]
            kT = AR.alloc(2048)
            kB = [Buf("mk%d" % i) for i in range(4)]
            vA = AR.alloc(16 * 128).rearrange("p (i d) -> p i d", i=16)
            vB = [Buf("mv%d" % i) for i in range(16)]
            sqp = Pool([(AR.alloc(512), Buf("msq%d" % i)) for i in range(2)])
            rsp = Pool([(AR.alloc(512, F32), Buf("mrs%d" % i)) for i in range(1)])
            xnp = Pool([(AR.alloc(512, F32), Buf("mxn%d" % i)) for i in range(2)])
            kf_t1 = AR.alloc(512, F32)
            t1p = Pool([(kf_t1, Buf("mt1%d" % i)) for i in range(1)])
            t2p = Pool([(AR.alloc(512, F32), Buf("mt2%d" % i)) for i in range(1)])
            xbp = Pool([(AR.alloc(512), Buf("mxb%d" % i)) for i in range(2)])
            pbp = Pool([(AR.alloc(512), Buf("mpb%d" % i)) for i in range(2)])
            recp = Pool([(AR.alloc(512, F32), Buf("mrc%d" % i)) for i in range(1)])
            cfp = Pool([(AR.alloc(2 * 512, F32).rearrange("p (c t) -> p c t", c=2), Buf("mcf%d" % i)) for i in range(1)])
            kfp = Pool([(kf_t1, Buf("mkf%d" % i)) for i in range(1)])
            cstg = Pool([(AR.alloc(288, F32), Buf("mcs%d" % i)) for i in range(2)])
            ostg = Pool([(AR.alloc(288, F32), Buf("mos%d" % i)) for i in range(2)])
            MEMSET("vector", vA[:, :, 64:128], 1.0, vB)
            growK = AR.alloc(96, F32)
            bGR = Buf("growK")
            LOAD("sync", growK, prow[:, l, 32:128], bGR)
            krg = AR.alloc(64, F32)
            krgB = Buf("krg")
            ssrP = AR.alloc(16, F32)
            ssrPB = Buf("ssrP")
            ssrS = AR.alloc(2 * 9, F32).rearrange("p (s i) -> p s i", s=2)
            ssrSB = Buf("ssrS")
            ssq = AR.alloc(16 * 8, F32).rearrange("p (i h) -> p i h", i=16)
            ksc = AR.alloc(16 * 8, F32).rearrange("p (i h) -> p i h", i=16)
            ssqB, kscB = Buf("ssq"), Buf("ksc")
            krRB = [Buf("krR0"), Buf("krR1")]

            P.tag = "mla.M1"
            srcw = w_in[l].rearrange("(k p) n -> p k n", p=128)
            wa, wab = WA[0]
            wb, wbb = WA[1]
            wav = wa[:, 0:3072].rearrange("p (k n) -> p k n", k=8)
            wbv = wb[:, 0:2304].rearrange("p (k n) -> p k n", k=8)
            LOAD("gpsimd", wav, srcw[:, :, MLA0:MLA0 + 384], wab)
            LOAD("gpsimd", wbv, srcw[:, :, MLA0 + 384:MLA0 + 672], wbb)
            for ti in range(5):
                c0, n = TILES[ti]
                cs = slice(c0, c0 + n)
                pcs = []
                for c in range(3):
                    pp, ppb = PS.next()
                    for k in range(8):
                        MM(pp[:, 0:n], wav[:, k, c * 128:(c + 1) * 128], hT[:, k, cs], k == 0, k == 7, [wab, hB[k][ti]], [ppb], inc=(k == 7))
                    pcs.append((pp, ppb))
                rms_norm_fm([(pp[:, 0:n], [ppb]) for pp, ppb in pcs], 3, lambda c: pcol(l, "mla_q_norm", c),
                            lambda c, ti=ti, cs=cs: (cqnT[:, c, cs], [cqB[ti]]), n, 1.0 / 384, (sqp, rsp))
                pcs = []
                for c in range(2):
                    pp, ppb = PS.next()
                    for k in range(8):
                        MM(pp[:, 0:n], wbv[:, k, c * 128:(c + 1) * 128], hT[:, k, cs], k == 0, k == 7, [wbb, hB[k][ti]], [ppb], inc=(k == 7))
                    pcs.append((pp, ppb))
                cf, cfb = cfp.next()
                rms_norm_fm([(pp[:, 0:n], [ppb]) for pp, ppb in pcs], 2, lambda c: pcol(l, "mla_kv_norm", c),
                            lambda c, cf=cf, cfb=cfb, n=n: (cf[:, c, 0:n], [cfb]), n, 1.0 / 256, (sqp, rsp))
                if ti < 4:
                    CP("scalar", ckvT[:, :, cs], cf[:, :, 0:n], [cfb], [ckvB[ti]])
                else:
                    CP("scalar", ckvN[:, :, 0:n], cf[:, :, 0:n], [cfb], [ckvNB])
                pp, ppb = PS.next()
                for k in range(8):
                    MM(pp[0:32, 0:n], wbv[:, k, 256:288], hT[:, k, cs], k == 0, k == 7, [wbb, hB[k][ti]], [ppb], inc=(k == 7))
                kf, kfb = kfp.next()
                CP("vector", kf[0:32, 0:n], pp[0:32, 0:n], [ppb], [kfb])
                if ti < 4:
                    CP("scalar", krT[64:96, cs], pp[0:32, 0:n], [ppb], [krB[ti]])
                else:
                    CP("scalar", krN[64:96, 0:n], pp[0:32, 0:n], [ppb], [krNB])
                nb = 4 if ti < 4 else 1
                bw = 128 if ti < 4 else 32
                for j in range(nb):
                    pt, ptb = PS.next()
                    for c in range(2):
                        TR(pt[0:bw, c * 128:(c + 1) * 128], cf[:, c, j * bw:(j + 1) * bw], ident_f, [cfb] + CONSTS, [ptb], inc=False)
                    TR(pt[0:bw, 256:288], kf[0:32, j * bw:(j + 1) * bw], ident_f[0:32, 0:32], [kfb] + CONSTS, [ptb])
                    og, ogb = ostg.next()
                    CP("vector", og[0:bw, :], pt[0:bw, 0:288], [ptb], [ogb])
                    if ti < 4:
                        jt = ti * 4 + j
                        TT("vector", krg[:, 0:32], og[:, 256:288], growK[:, 64:96], ALU.mult, [ogb, bGR], [krgB])
                        P.op("scalar", lambda e, jt=jt: e.activation(out=krg[:, 32:64], in_=krg[:, 0:32], func=AF.Square, accum_out=ssrP[:, jt:jt + 1]),
                             reads=[krgB], writes=[krgB, ssrPB])
                    else:
                        for s2 in range(2):
                            pt2, pt2b = PS.next()
                            TR(pt2[0:16, 0:32], kf[0:32, 16 * s2:16 * s2 + 16], ident_f[0:32, 0:32], [kfb] + CONSTS, [pt2b])
                            TT("vector", krg[0:16, 0:32], pt2[0:16, 0:32], growK[0:16, 64:96], ALU.mult, [pt2b, bGR], [krgB])
                            P.op("scalar", lambda e, s2=s2: e.activation(out=krg[0:16, 32:64], in_=krg[0:16, 0:32], func=AF.Square, accum_out=ssrS[0:16, s2, 8:9]),
                                 reads=[krgB], writes=[krgB, ssrSB])
                    if ti < 4:
                        r0 = c0 + j * 128
                        STORE(o_ckv_p[l, r0:r0 + 128, :], og[:, 0:256], ogb)
                        STORE(o_kr_p[l, r0:r0 + 128, :], og[:, 256:288], ogb)
                    else:
                        STORE(o_ckv_s[l], og[0:32, 0:256], ogb)
                        STORE(o_kr_s[l], og[0:32, 256:288], ogb)

            P.barrier()
            AR2 = Arena(hT_region, 8 * T)
            qT2 = [qT, AR2.alloc(2048)]
            kT2 = [kT, AR2.alloc(2048)]
            vA2 = [vA, AR2.alloc(16 * 128).rearrange("p (i d) -> p i d", i=16)]
            qB2 = [qB, [Buf("mq1_%d" % i) for i in range(4)]]
            kB2 = [kB, [Buf("mk1_%d" % i) for i in range(4)]]
            vB2 = [vB, [Buf("mv1_%d" % i) for i in range(16)]]
            MEMSET("vector", vA2[1][:, :, 64:128], 1.0, vB2[1])
            sqp = Pool(sqp.slots + [(AR2.alloc(512), Buf("msq2%d" % i)) for i in range(1)])
            rsp = Pool(rsp.slots + [(AR2.alloc(512, F32), Buf("mrs2%d" % i)) for i in range(1)])
            xnp = Pool(xnp.slots + [(AR2.alloc(512, F32), Buf("mxn2%d" % i)) for i in range(1)])
            t1p = Pool([(AR2.alloc(512, F32), Buf("mt12%d" % i)) for i in range(2)])
            t2p = Pool(t2p.slots + [(AR2.alloc(512, F32), Buf("mt22%d" % i)) for i in range(1)])
            xbp = Pool(xbp.slots + [(AR2.alloc(512), Buf("mxb2%d" % i)) for i in range(1)])
            pbp = Pool(pbp.slots + [(AR2.alloc(512), Buf("mpb2%d" % i)) for i in range(2)])
            recp = Pool(recp.slots + [(AR2.alloc(512, F32), Buf("mrc2%d" % i)) for i in range(1)])
            P.tag = "mla.w"
            wqv = wa[:, 0:2304].rearrange("p (k n) -> p k n", k=3)
            wkv = wb[:, 0:2048].rearrange("p (k n) -> p k n", k=2)
            LOAD("gpsimd", wqv, w_uq[l].rearrange("(k p) n -> p k n", p=128), wab)
            LOAD("gpsimd", wkv, w_ukv[l].rearrange("(k p) n -> p k n", p=128), wbb)
            wkv4 = wkv.rearrange("p k (h t d) -> p k h t d", h=8, t=2)
            for c in range(2):
                TT("vector", wkv4[:, c, :, 0, :], wkv4[:, c, :, 0, :], growK[:, 0:64].unsqueeze(1).broadcast_to([128, 8, 64]), ALU.mult, [wbb, bGR], [wbb])

            def normrope(ps, psb, n, gname, pos0, dst, dstb, psp):
                q_ap, q_b = sqp.next()
                ACT(q_ap[0:96, 0:n], ps[0:96, 0:n], AF.Square, [psb], [q_b])
                yield
                p2, p2b = psp.next()
                MM(p2[0:96, 0:n], ones96[0:96, 0:96], q_ap[0:96, 0:n], True, True, [q_b] + MCB, [p2b])
                yield
                r_ap, r_b = rsp.next()
                ACT(r_ap[0:96, 0:n], p2[0:96, 0:n], AF.Ln, [p2b], [r_b], bias=EPS, scale=1.0 / 96)
                ACT(r_ap[0:96, 0:n], r_ap[0:96, 0:n], AF.Exp, [r_b], [r_b], scale=-0.5)
                yield
                xn, xnb = xnp.next()
                STT(xn[0:96, 0:n], ps[0:96, 0:n], pcol(l, gname, 0)[0:96], r_ap[0:96, 0:n], ALU.mult, ALU.mult, [psb, r_b] + CONSTS, [xnb])
                yield
                CP("vector", dst[0:64, 0:n], xn[0:64, 0:n], [xnb], dstb)
                xb, xbb = xbp.next()
                CP("vector", xb[64:96, 0:n], xn[64:96, 0:n], [xnb], [xbb])
                yield
                p3, p3b = psp.next()
                MM(p3[64:96, 0:n], Pm96[64:96, 64:96], xb[64:96, 0:n], True, True, [xbb] + MCB, [p3b])
                yield
                a1, a1b = t1p.next()
                a2, a2b = t2p.next()
                TT("vector", a1[64:96, 0:n], xn[64:96, 0:n], Cm[64:96, pos0:pos0 + n], ALU.mult, [xnb] + MCB, [a1b])
                TT("vector", a2[64:96, 0:n], p3[64:96, 0:n], Sm[64:96, pos0:pos0 + n], ALU.mult, [p3b] + MCB, [a2b])
                yield
                TT("vector", dst[64:96, 0:n], a1[64:96, 0:n], a2[64:96, 0:n], ALU.add, [a1b, a2b], dstb)

            SC = 96.0 ** -0.5
            for st_ in range(3):
                if st_ == 0:
                    ktiles = [(i * 512, 512, i * 512, i) for i in range(4)]
                    nk = 2048
                    qcol0, nq = 0, 2048
                else:
                    s_ = st_ - 1
                    P.tag = "mla.past"
                    for i in range(8):
                        cg, cgb = cstg.next()
                        LOAD("sync", cg[:, 0:256], c_ckv[l, s_, i * 128:(i + 1) * 128, :], cgb)
                        LOAD("sync", cg[:, 256:288], c_kr[l, s_, i * 128:(i + 1) * 128, :], cgb)
                        pt, ptb = PS.next()
                        for c in range(2):
                            TR(pt[:, c * 128:(c + 1) * 128], cg[:, c * 128:(c + 1) * 128], ident_f, [cgb] + CONSTS, [ptb], inc=False)
                        TR(pt[0:32, 256:384], cg[:, 256:288], ident_f, [cgb] + CONSTS, [ptb])
                        CP("scalar", ckvT[:, :, i * 128:(i + 1) * 128], pt[:, 0:256].rearrange("p (c t) -> p c t", c=2), [ptb], [ckvB[i // 4]])
                        CP("vector", krT[64:96, i * 128:(i + 1) * 128], pt[0:32, 256:384], [ptb], [krB[i // 4]])
                        TT("vector", krg[:, 0:32], cg[:, 256:288], growK[:, 64:96], ALU.mult, [cgb, bGR], [krgB])
                        P.op("scalar", lambda e, i=i, s_=s_: e.activation(out=krg[:, 32:64], in_=krg[:, 0:32], func=AF.Square, accum_out=ssrS[:, s_, i:i + 1]),
                             reads=[krgB], writes=[krgB, ssrSB])
                    CP("scalar", ckvT[:, :, 1024:1040], ckvN[:, :, 16 * s_:16 * s_ + 16], [ckvNB], [ckvB[2]])
                    CP("scalar", krT[64:96, 1024:1040], krN[64:96, 16 * s_:16 * s_ + 16], [krNB], [krB[2]])
                    ktiles = [(0, 512, 0, 0), (512, 512, 512, 1), (1024, 16, 2048, 2)]
                    nk = 1040
                    qcol0, nq = 2048 + 16 * s_, 16
                P.tag = "mla.kset"
                ntile = 16 if st_ == 0 else 9
                for jt in range(ntile):
                    kn = 128 if (st_ == 0 or jt < 8) else 16
                    pk, pkb = PS.next()
                    for c in range(2):
                        MM(pk[0:kn, :], ckvT[:, c, jt * 128:jt * 128 + kn], wkv4[:, c, :, 0, :], c == 0, c == 1, [wbb, ckvB[jt // 4]], [pkb], inc=(c == 1))
                    q_ap, q_b = xnp.next()
                    ACT(q_ap[0:kn, :], pk[0:kn, :], AF.Square, [pkb], [q_b])
                    P.op("vector", lambda e, q_ap=q_ap, kn=kn, jt=jt: e.tensor_reduce(out=ssq[0:kn, jt, :], in_=q_ap[0:kn, :].rearrange("p (h d) -> p h d", h=8),
                                                                                   axis=mybir.AxisListType.X, op=ALU.add), reads=[q_b], writes=[ssqB])
                parts = [(128, 0, 16)] if st_ == 0 else [(128, 0, 8), (16, 8, 9)]
                for (pr, t0_, t1_) in parts:
                    ssr_v = ssrP[0:pr, t0_:t1_] if st_ == 0 else ssrS[0:pr, st_ - 1, t0_:t1_]
                    kv_ = ksc[0:pr, t0_:t1_, :]
                    TT("vector", kv_, ssq[0:pr, t0_:t1_, :], ssr_v.unsqueeze(2).broadcast_to([pr, t1_ - t0_, 8]), ALU.add, [ssqB, ssrPB, ssrSB], [kscB])
                    ACT(kv_, kv_, AF.Ln, [kscB], [kscB], bias=EPS, scale=1.0 / 96)
                    ACT(kv_, kv_, AF.Exp, [kscB], [kscB], scale=-0.5, bias=math.log(SC))
                for (k0, n, pos0, bi) in ktiles:
                    xn, xnb = xnp.next()
                    TS("vector", xn[64:96, 0:n], krT[64:96, k0:k0 + n], pcol(l, "mla_k_gain", 0)[64:96], None, ALU.mult, ALU.bypass, [krB[bi]] + CONSTS, [xnb])
                    xb, xbb = xbp.next()
                    CP("scalar", xb[64:96, 0:n], xn[64:96, 0:n], [xnb], [xbb])
                    p3, p3b = PS.next()
                    MM(p3[64:96, 0:n], Pm96[64:96, 64:96], xb[64:96, 0:n], True, True, [xbb] + MCB, [p3b])
                    a1, a1b = t1p.next()
                    a2, a2b = t2p.next()
                    TT("vector", a1[64:96, 0:n], xn[64:96, 0:n], Cm[64:96, pos0:pos0 + n], ALU.mult, [xnb] + MCB, [a1b])
                    TT("vector", a2[64:96, 0:n], p3[64:96, 0:n], Sm[64:96, pos0:pos0 + n], ALU.mult, [p3b] + MCB, [a2b])
                    for b in range(2):
                        TT("vector", kT2[b][64:96, k0:k0 + n], a1[64:96, 0:n], a2[64:96, 0:n], ALU.add, [a1b, a2b], [krRB[b]])
                NFILL = int(_DBG_ENV.get("MK_FILL", "0"))
                PSP = Pool(psum[5:8] if NFILL == 0 else psum[5:7])
                fillB = Buf("fill")

                def filler():
                    for _ in range(NFILL):
                        MM(psum[7][0][:, 0:512], ident_b, cqnT[:, 0, 0:512], True, True, [], [fillB], inc=False)
                PSC = Pool(psum[2:5])
                def prep_items(h, st_=st_, ktiles=ktiles, qcol0=qcol0):
                    b = h % 2
                    qT_, kT_, vA_ = qT2[b], kT2[b], vA2[b]
                    items = []

                    def q_item(ti):
                        P.tag = "mla.q"
                        if st_ == 0:
                            c0 = ti * 512
                            pp, ppb = PSP.next()
                            for k in range(3):
                                MM(pp[0:96, 0:512], wqv[:, k, h * 96:(h + 1) * 96], cqnT[:, k, c0:c0 + 512], k == 0, k == 2, [wab, cqB[ti]], [ppb], inc=(k == 2))
                            yield
                            for _ in normrope(pp, ppb, 512, "mla_q_gain", c0, qT_[:, c0:c0 + 512], [qB2[b][ti]], PSP):
                                yield
                        else:
                            pp, ppb = PSP.next()
                            for k in range(3):
                                MM(pp[0:96, 0:16], wqv[:, k, h * 96:(h + 1) * 96], cqnT[:, k, qcol0:qcol0 + 16], k == 0, k == 2, [wab, cqB[4]], [ppb], inc=(k == 2))
                            yield
                            for _ in normrope(pp, ppb, 16, "mla_q_gain", 2048, qT_[:, 0:16], [qB2[b][0]], PSP):
                                yield

                    def k_item(kt):
                        P.tag = "mla.kv"
                        (k0, n, pos0, bi) = kt
                        pp, ppb = PSP.next()
                        for c in range(2):
                            MM(pp[0:64, 0:n], wkv4[:, c, h, 0, :], ckvT[:, c, k0:k0 + n], c == 0, c == 1, [wbb, ckvB[bi]], [ppb], inc=(c == 1))
                        yield
                        P.tag = "mla.kv"
                        CP("vector", kT_[0:64, k0:k0 + n], pp[0:64, 0:n], [ppb], [kB2[b][bi]])

                    def v_item(kt):
                        P.tag = "mla.kv"
                        (k0, n, pos0, bi) = kt
                        pv, pvb = PSP.next()
                        nt = (n + 127) // 128
                        for j in range(nt):
                            kn = min(128, n - j * 128)
                            for c in range(2):
                                MM(pv[0:kn, j * 64:(j + 1) * 64], ckvT[:, c, k0 + j * 128:k0 + j * 128 + kn], wkv4[:, c, h, 1, :], c == 0, c == 1,
                                   [wbb, ckvB[bi]], [pvb], inc=(c == 1 and j == nt - 1))
                        yield
                        P.tag = "mla.kv"
                        t0 = k0 // 128
                        kn0 = min(128, n)
                        CP("vector", vA_[0:kn0, t0:t0 + nt, 0:64], pv[0:kn0, 0:nt * 64].rearrange("p (a b) -> p a b", b=64), [pvb], vB2[b][t0:t0 + nt])
                    for ti in range(4 if st_ == 0 else 1):
                        items.append(q_item(ti))
                    for kt in ktiles:
                        items.append(k_item(kt))
                        items.append(v_item(kt))
                    return items

                def step(nxt):
                    while nxt:
                        try:
                            next(nxt[0])
                            return
                        except StopIteration:
                            nxt.pop(0)

                fin_state = {"f": None}

                def attention(h, nxt, st_=st_, qcol0=qcol0):
                    b = h % 2
                    qT_, kT_, vA_ = qT2[b], kT2[b], vA2[b]
                    it = 0
                    r = slice(64 * (h % 2), 64 * (h % 2) + 64)
                    if st_ == 0:
                        for g in range(4):
                            acc, accb = PSA.next()
                            last = 4 * g + 3
                            pendq = []
                            PV_DEPTH = int(_DBG_ENV.get("MK_PVD", "2"))
                            for j in range(last + 1):
                                P.tag = "mla.attn"
                                d_ = j - 4 * g
                                lo = 128 * d_ if d_ > 0 else 0
                                n = 512 - lo
                                sc, scb = PSC.next()
                                MM(sc[:, 0:n], kT_[0:96, j * 128:(j + 1) * 128], qT_[0:96, 512 * g + lo:512 * g + 512], True, d_ < 0,
                                   [kB2[b][j // 4], krRB[b], qB2[b][g]], [scb], inc=(d_ < 0))
                                if d_ >= 0:
                                    MM(sc[:, 0:128], mrow[0:1, 0, :], mrow[0:1, 1, :], False, True, MCB, [scb])
                                pb, pbb = pbp.next()
                                ACT(pb[:, 0:n], sc[:, 0:n], AF.Exp, [scb, kscB], [pbb], scale=ksc[:, j, h:h + 1])
                                pendq.append(lambda acc=acc, accb=accb, lo=lo, n=n, j=j, pb=pb, pbb=pbb, last=last:
                                             MM(acc[:, lo:512], vA_[:, j, :], pb[:, 0:n], j == 0, j == last, [vB2[b][j], pbb], [accb], inc=(j == last)))
                                if len(pendq) > PV_DEPTH:
                                    pendq.pop(0)()
                                it += 1
                                step(nxt)
                                if it % 2 == 0:
                                    step(nxt)
                            while pendq:
                                pendq.pop(0)()

                            def fin(acc=acc, accb=accb, r=r, h=h, g=g):
                                P.tag = "mla.attn"
                                rc_, rcb = recp.next()
                                ACT(rc_[64:128, :], acc[64:128, :], AF.Ln, [accb], [rcb])
                                ACT(rc_[64:128, :], rc_[64:128, :], AF.Exp, [rcb], [rcb], scale=-1.0)
                                TT("vector", mixT[r, h // 2, 512 * g:512 * g + 512], acc[0:64, :], rc_[64:128, :], ALU.mult, [accb, rcb], [mixB[h // 2][g]])
                            fin()
                    else:
                        acc, accb = PSA.next()
                        pend = None
                        for j in range(9):
                            P.tag = "mla.attn"
                            kn = 128 if j < 8 else 16
                            sc, scb = PSC.next()
                            MM(sc[0:kn, 0:16], kT_[0:96, j * 128:j * 128 + kn], qT_[0:96, 0:16], True, True, [kB2[b][j // 4], krRB[b], qB2[b][0]], [scb])
                            pb, pbb = pbp.next()
                            ACT(pb[0:kn, 0:16], sc[0:kn, 0:16], AF.Exp, [scb, kscB], [pbb], scale=ksc[0:kn, j, h:h + 1])
                            if pend is not None:
                                pend()
                            pend = (lambda acc=acc, accb=accb, kn=kn, j=j, pb=pb, pbb=pbb:
                                    MM(acc[:, 0:16], vA_[0:kn, j, :], pb[0:kn, 0:16], j == 0, j == 8, [vB2[b][j], pbb], [accb], inc=(j == 8)))
                            step(nxt)
                            step(nxt)
                        pend()
                        P.tag = "mla.attn"
                        rc_, rcb = recp.next()
                        ACT(rc_[64:128, 0:16], acc[64:128, 0:16], AF.Ln, [accb], [rcb])
                        ACT(rc_[64:128, 0:16], rc_[64:128, 0:16], AF.Exp, [rcb], [rcb], scale=-1.0)
                        TT("vector", mixT[r, h // 2, qcol0:qcol0 + 16], acc[0:64, 0:16], rc_[64:128, 0:16], ALU.mult, [accb, rcb], [mixB[h // 2][4]])
                    while nxt:
                        step(nxt)

                for g_ in prep_items(0):
                    for _ in g_:
                        pass
                for h in range(8):
                    attention(h, prep_items(h + 1) if h < 7 else [])
                if fin_state["f"] is not None:
                    fin_state["f"]()
                    fin_state["f"] = None
            if dbg == "mla":
                dump("mlamix", mixT, [mixB[c][ti] for c in range(4) for ti in range(5)], BF16)
            xacc(l, 512, 4, mixT, mixB, "mla")
            P.barrier()
            AR.reset(mR)

        def group_ssm(l, hT, hB):
            mR = AR.mark()
            U = AR.alloc(128, F32)
            NEG = AR.alloc(640)
            ones_f = AR.alloc(128, F32)
            MEMSET("vector", ones_f, 1.0, [Buf("onesf")])
            prow_sb = AR.alloc(128, F32)
            bSC = Buf("ssmc")
            LOAD("sync", U, c_U, bSC)
            LOAD("sync", NEG, c_NEG, bSC)
            LOAD("sync", prow_sb, prow[:, l, :], bSC)
            SCB = [bSC]
            dtv = AR.alloc(18 * 16, F32).rearrange("p (i h) -> p i h", i=18)
            dtA = AR.alloc(18 * 16, F32).rearrange("p (i h) -> p i h", i=18)
            nacum = AR.alloc(18 * 16, F32).rearrange("p (i h) -> p i h", i=18)
            ea = AR.alloc(16, F32)
            bDT = Buf("dt")
            convout = AR.alloc(3 * 12 * 3, F32).rearrange("p (q c r) -> p q c r", q=3, c=12)
            bCO = Buf("convout")
            srcw = w_in[l].rearrange("(k p) n -> p k n", p=128)

            P.tag = "ssm.dt"
            wa, wab = WA[0]
            wdt = wa[:, 0:128].rearrange("p (k n) -> p k n", k=8)
            LOAD("gpsimd", wdt, srcw[:, :, SSM0 + 2560:SSM0 + 2576], wab)
            pd, pdb = PSA.next()
            for i in range(18):
                if i < 16:
                    cols, rows, ti = slice(i * 128, (i + 1) * 128), 128, i // 4
                else:
                    cols, rows, ti = slice(2048 + 16 * (i - 16), 2064 + 16 * (i - 16)), 16, 4
                for k in range(8):
                    MM(pd[0:rows, i * 16:(i + 1) * 16], hT[:, k, cols], wdt[:, k, :], k == 0, k == 7, [wab, hB[k][ti]], [pdb], inc=(k == 7 and i == 17))
            pdv = pd[:, 0:288].rearrange("p (i h) -> p i h", i=18)
            ACT(ea, prow_sb[:, 16:32], AF.Exp, SCB, [bDT])
            for (pr, sl_, ns) in ((128, slice(0, 16), 16), (16, slice(16, 18), 2)):
                TT("vector", dtv[0:pr, sl_, :], pdv[0:pr, sl_, :], prow_sb[0:pr, 0:16].unsqueeze(1).broadcast_to([pr, ns, 16]), ALU.add, [pdb] + SCB, [bDT])
                ACT(dtv[0:pr, sl_, :], dtv[0:pr, sl_, :], AF.Exp, [bDT], [bDT])
                ACT(dtv[0:pr, sl_, :], dtv[0:pr, sl_, :], AF.Ln, [bDT], [bDT], bias=1.0)
                STT(dtA[0:pr, sl_, :], dtv[0:pr, sl_, :], -1.0, ea[0:pr].unsqueeze(1).broadcast_to([pr, ns, 16]), ALU.mult, ALU.mult, [bDT], [bDT])
            pc_, pcb = PS.next()
            MM(pc_[:, 0:256], U, dtA[:, 0:16, :].rearrange("p i h -> p (i h)"), True, True, [bDT] + SCB, [pcb])
            MM(pc_[0:16, 256:288], U[0:16, 0:16], dtA[0:16, 16:18, :].rearrange("p i h -> p (i h)"), True, True, [bDT] + SCB, [pcb])
            P.op("scalar", lambda e: e.activation(out=nacum[:, 0:16, :].rearrange("p i h -> p (i h)"), in_=pc_[:, 0:256], func=AF.Copy, scale=-1.0), reads=[pcb], writes=[bDT])
            P.op("scalar", lambda e: e.activation(out=nacum[0:16, 16:18, :].rearrange("p i h -> p (i h)"), in_=pc_[0:16, 256:288], func=AF.Copy, scale=-1.0), reads=[pcb], writes=[bDT])

            mG = AR.mark()
            for g in range(2):
                AR.reset(mG)
                xsT = AR.alloc(4 * T).rearrange("p (c t) -> p c t", c=4)
                zsT = AR.alloc(4 * T).rearrange("p (c t) -> p c t", c=4)
                BT, CT = AR.alloc(T), AR.alloc(T)
                xsB = [[Buf("xs%d_%d" % (c, ti)) for ti in range(5)] for c in range(4)]
                zB = [[Buf("zs%d_%d" % (c, ti)) for ti in range(5)] for c in range(4)]
                BB = [Buf("B%d" % ti) for ti in range(5)]
                CB = [Buf("C%d" % ti) for ti in range(5)]
                B_tm = AR.alloc(18 * 128).rearrange("p (i d) -> p i d", i=18)
                BtB = [Buf("Bt%d" % i) for i in range(18)]
                mPh = AR.mark()
                NR = 4
                Rp = [(AR.alloc(520, F32), Buf("R%d" % i)) for i in range(NR)]
                accp = Pool([(AR.alloc(512, F32), Buf("ca%d" % i)) for i in range(4)])
                P.tag = "ssm.z"
                wz, wzb = WA[1]
                wzv = wz.rearrange("p (k n) -> p k n", k=8)
                LOAD("gpsimd", wzv, srcw[:, :, SSM0 + 512 * g:SSM0 + 512 * g + 512], wzb)
                zunits = []
                for c in range(4):
                    for ti in range(5):
                        def zunit(c=c, ti=ti):
                            P.tag = "ssm.z"
                            c0, n = TILES[ti]
                            pp, ppb = PS.next()
                            for k in range(8):
                                MM(pp[:, 0:n], wzv[:, k, c * 128:(c + 1) * 128], hT[:, k, c0:c0 + n], k == 0, k == 7, [wzb, hB[k][ti]], [ppb], inc=(k == 7))
                            ACT(zsT[:, c, c0:c0 + n], pp[:, 0:n], AF.Silu, [ppb], [zB[c][ti]])
                            P.tag = "ssm.conv"
                        zunits.append(zunit)
                P.tag = "ssm.conv"
                wx, wxb = WA[0]
                wxv = wx.rearrange("p (k n) -> p k n", k=8)
                LOAD("gpsimd", wxv, srcw[:, :, SSM0 + 1024 + 512 * g:SSM0 + 1024 + 512 * g + 512], wxb)
                wbc, wbcb = WA[1]
                wbcv = wbc[:, 0:2048].rearrange("p (k n) -> p k n", k=8)
                ri = 0
                conv_pend = [None]
                for q in range(6):
                    if q == 4:
                        while zunits:
                            zunits.pop(0)()
                        LOAD("gpsimd", wbcv[:, :, 0:128], srcw[:, :, SSM0 + 2048 + 128 * g:SSM0 + 2048 + 128 * g + 128], wbcb)
                        LOAD("gpsimd", wbcv[:, :, 128:256], srcw[:, :, SSM0 + 2304 + 128 * g:SSM0 + 2304 + 128 * g + 128], wbcb)
                    if q < 4:
                        lw = lambda k, q=q: wxv[:, k, q * 128:(q + 1) * 128]
                        lwb = wxb
                        cc = 4 * g + q
                        dstf = lambda cs, q=q: xsT[:, q, cs]
                        dB = xsB[q]
                    else:
                        lw = lambda k, q=q: wbcv[:, k, (q - 4) * 128:(q - 3) * 128]
                        lwb = wbcb
                        cc = 8 + 2 * (q - 4) + g
                        dstf = (lambda cs: BT[:, cs]) if q == 4 else (lambda cs: CT[:, cs])
                        dB = BB if q == 4 else CB
                    segs = [(ti, TILES[ti][0], 512, 0) for ti in range(4)] + [(4, 2048, 16, 1), (4, 2064, 16, 2)]
                    for (ti, c0, n, sq_) in segs:
                        pp, ppb = PS.next()
                        for k in range(8):
                            MM(pp[:, 0:n], lw(k), hT[:, k, c0:c0 + n], k == 0, k == 7, [lwb, hB[k][ti]], [ppb], inc=(k == 7))
                        R, Rb = Rp[ri % NR]
                        Rprev, Rpb = Rp[(ri - 1) % NR]
                        ri += 1
                        if sq_ == 0 and ti == 0:
                            MEMSET("vector", R[:, 0:3], 0.0, [Rb])
                        elif sq_ == 0:
                            CP("vector", R[:, 0:3], Rprev[:, 512:515], [Rpb], [Rb])
                        else:
                            LOAD("sync", R[:, 0:3], c_conv[l, sq_ - 1].rearrange("r c -> c r")[cc * 128:(cc + 1) * 128, :], Rb, slow=True)
                        CP("scalar", R[:, 3:3 + n], pp[:, 0:n], [ppb], [Rb])
                        if (sq_ == 0 and ti == 3) or sq_ > 0:
                            CP("vector", convout[:, sq_, cc, :], R[:, n:n + 3], [Rb], [bCO])
                        a_, ab = accp.next()
                        ACT(a_[:, 0:n], pp[:, 0:n], AF.Copy, [ppb] + CONSTS, [ab], scale=pcol(l, "conv_w", cc * 4 + 3))
                        for w_ in range(0, 3):
                            STT(a_[:, 0:n], R[:, w_:w_ + n], pcol(l, "conv_w", cc * 4 + w_), a_[:, 0:n], ALU.mult, ALU.add, [Rb, ab] + CONSTS, [ab])
                        if conv_pend[0] is not None:
                            conv_pend[0]()
                        conv_pend[0] = (lambda d_=dstf(slice(c0, c0 + n)), a_=a_, n=n, ab=ab, db_=dB[ti], cc=cc:
                                        ACT(d_, a_[:, 0:n], AF.Silu, [ab] + CONSTS, [db_], bias=pcol(l, "conv_b", cc)))
                        if zunits:
                            zunits.pop(0)()
                conv_pend[0]()
                conv_pend[0] = None
                P.tag = "ssm.Btm"
                for ti in range(4):
                    pt, ptb = PS.next()
                    ptv = pt.bitcast(BF16)
                    for j in range(4):
                        i = ti * 4 + j
                        TR(ptv[:, j * 128:(j + 1) * 128], BT[:, i * 128:(i + 1) * 128], ident_b, [BB[ti]] + CONSTS, [ptb], inc=(j == 3))
                    CP("scalar", B_tm[:, ti * 4:ti * 4 + 4, :], ptv[:, 0:512].rearrange("p (a b) -> p a b", b=128), [ptb], BtB[ti * 4:ti * 4 + 4])
                pt, ptb = PS.next()
                ptv = pt.bitcast(BF16)
                for s_ in range(2):
                    TR(ptv[0:16, s_ * 128:(s_ + 1) * 128], BT[:, 2048 + 16 * s_:2064 + 16 * s_], ident_b, [BB[4]] + CONSTS, [ptb], inc=(s_ == 1))
                CP("scalar", B_tm[0:16, 16:18, :], ptv[0:16, 0:256].rearrange("p (a b) -> p a b", b=128), [ptb], BtB[16:18])

                if dbg == "ssm" and _DBG_ENV.get("SSM_DUMP"):
                    dump("xs%d" % g, xsT, [xsB[c][ti] for c in range(4) for ti in range(5)], BF16)
                    dump("zs%d" % g, zsT, [zB[c][ti] for c in range(4) for ti in range(5)], BF16)
                    dump("B%d" % g, BT, BB, BF16)
                    dump("C%d" % g, CT, CB, BF16)
                P.tag = "ssm.scan"
                P.barrier()
                AR.reset(mPh)
                xdtp = Pool([(AR.alloc(512).rearrange("p (h d) -> p h d", h=8), Buf("xdt%d" % i)) for i in range(2)])
                xwp = Pool([(AR.alloc(512).rearrange("p (h d) -> p h d", h=8), Buf("xw%d" % i)) for i in range(2)])
                UdA = AR.alloc(1024, F32)
                UdAB = Buf("UdA")
                Ea = AR.alloc(1024, F32)
                EaB = Buf("Ea")
                Da = AR.alloc(1024, F32)
                DaB = Buf("Da")
                scp = Pool([(AR.alloc(1024), Buf("sc%d" % i)) for i in range(2)])
                cdp = Pool([(AR.alloc(1024), Buf("cd%d" % i)) for i in range(2)])
                hs = AR.alloc(512, F32).rearrange("p (h d) -> p h d", h=8)
                hsB = Buf("hs")
                hsbf = [(AR.alloc(512), Buf("hsbf%d" % i)) for i in range(2)]
                ydp = Pool([(AR.alloc(512, F32), Buf("yd%d" % i)) for i in range(1)])
                hstg = Pool([(AR.alloc(128, F32), Buf("hst%d" % i)) for i in range(2)])
                PSB = Pool(psum[2:5])
                PSD = Pool(psum[5:8])

                def state_out(dst_ap):
                    for c in range(4):
                        pt, ptb = PSB.next()
                        TR(pt[:, 0:128], hs[:, 2 * c:2 * c + 2, :].rearrange("p h d -> p (h d)"), ident_f, [hsB] + CONSTS, [ptb])
                        sg_, sgb = hstg.next()
                        CP("scalar", sg_, pt[:, 0:128], [ptb], [sgb])
                        STORE(dst_ap[2 * c:2 * c + 2].rearrange("h p n -> (h p) n"), sg_, sgb)

                Eap = Pool([(Ea, EaB), (AR.alloc(1024, F32), Buf("Ea1"))])
                hstate = {"cur": 0}

                def phaseA(i):
                    if i < 16:
                        cl, cols, ti = 128, slice(i * 128, (i + 1) * 128), i // 4
                    else:
                        cl, cols, ti = 16, slice(2048 + 16 * (i - 16), 2064 + 16 * (i - 16)), 4
                    W = 8 * cl
                    gb, gbb = PSA.next()
                    MM(gb[0:cl, 0:cl], BT[:, cols], CT[:, cols], True, True, [BB[ti], CB[ti]], [gbb])
                    pt, ptb = PSB.next()
                    ptv = pt.bitcast(BF16)
                    for c in range(4):
                        TR(ptv[0:cl, c * 128:(c + 1) * 128], xsT[:, c, cols], ident_b, [xsB[c][ti]] + CONSTS, [ptb], inc=(c == 3))
                    xdt, xdtb = xdtp.next()
                    TT("vector", xdt[0:cl], ptv[0:cl, 0:512].rearrange("p (h d) -> p h d", h=8),
                       dtv[0:cl, i, 8 * g:8 * g + 8].unsqueeze(2).broadcast_to([cl, 8, 64]), ALU.mult, [ptb, bDT], [xdtb])
                    nb = 2 if cl == 128 else 1
                    bw = W // nb
                    bcs = [PSD.next() for _ in range(nb)]
                    for h in range(8):
                        bc, bcb = bcs[(h * cl) // bw]
                        o_ = (h * cl) % bw
                        MM(bc[:, o_:o_ + cl], dtA[0:cl, i, 8 * g + h:8 * g + h + 1].broadcast_to([cl, 128]), U[0:cl, 0:cl], o_ == 0, True, [bDT] + SCB, [bcb],
                           inc=(h % (8 // nb) == (8 // nb) - 1), sgc=True)
                    Ea_, EaB_ = Eap.next()
                    for hb_, (bc, bcb) in enumerate(bcs):
                        ACT(Ea_[:, hb_ * bw:(hb_ + 1) * bw], bc[:, 0:bw], AF.Exp, [bcb], [EaB_])
                    negt = NEG[0:cl, 0:512] if cl == 128 else NEG[0:cl, 512:640]
                    for hb_, (bc, bcb) in enumerate(bcs):
                        MM(bc[0:cl, 0:bw], ident_b[0:cl, 0:cl], negt, False, True, SCB + CONSTS + [EaB_], [bcb], sgc=True)
                    Da3 = Da[0:cl, 0:W].rearrange("p (h t) -> p h t", h=8)
                    Ea3 = Ea_[:, 0:W].rearrange("p (h t) -> p h t", h=8)
                    for h in range(8):
                        bc, bcb = bcs[(h * cl) // bw]
                        o_ = (h * cl) % bw
                        ACT(Da3[:, h, :], bc[0:cl, o_:o_ + cl], AF.Exp, [bcb, bDT], [DaB], bias=nacum[0:cl, i, 8 * g + h:8 * g + h + 1])
                    return dict(cl=cl, cols=cols, ti=ti, W=W, xdt=xdt, xdtb=xdtb, Ea3=Ea3, EaB=EaB_, Da3=Da3, gb=gb, gbb=gbb)

                def phaseA2(cx):
                    cl, cols, ti, W = cx["cl"], cx["cols"], cx["ti"], cx["W"]
                    sc_, scb_ = scp.next()
                    sc3 = sc_[0:cl, 0:W].rearrange("p (h t) -> p h t", h=8)
                    TT("vector", sc3, cx["gb"][0:cl, 0:cl].unsqueeze(1).broadcast_to([cl, 8, cl]), cx["Da3"], ALU.mult, [cx["gbb"], DaB], [scb_])
                    cd_, cdb_ = cdp.next()
                    cd3 = cd_[:, 0:W].rearrange("p (h t) -> p h t", h=8)
                    TT("vector", cd3, CT[:, cols].unsqueeze(1).broadcast_to([128, 8, cl]), cx["Ea3"], ALU.mult, [CB[ti], cx["EaB"]], [cdb_])
                    xw, xwb = xwp.next()
                    TT("vector", xw[0:cl], cx["xdt"][0:cl], cx["Da3"][:, :, cl - 1:cl].broadcast_to([cl, 8, 64]), ALU.mult, [cx["xdtb"], DaB], [xwb])
                    cx.update(sc3=sc3, scb=scb_, cd3=cd3, cdb=cdb_, xw=xw, xwb=xwb)

                def phaseB(i, cx):
                    cl, cols, ti = cx["cl"], cx["cols"], cx["ti"]
                    if i == 0:
                        MEMSET("vector", hs, 0.0, [hsB])
                        MEMSET("vector", hsbf[hstate["cur"]][0], 0.0, [hsbf[hstate["cur"]][1]])
                    if i >= 16:
                        for c in range(4):
                            sg_, sgb = hstg.next()
                            LOAD("sync", sg_, c_ssm[l, i - 16, 8 * g + 2 * c:8 * g + 2 * c + 2].rearrange("h p n -> (h p) n"), sgb)
                            pt, ptb = PSB.next()
                            TR(pt[:, 0:128], sg_, ident_f, [sgb] + CONSTS, [ptb])
                            CP("vector", hs[:, 2 * c:2 * c + 2, :].rearrange("p h d -> p (h d)"), pt[:, 0:128], [ptb], [hsB])
                        hstate["cur"] ^= 1
                        CP("scalar", hsbf[hstate["cur"]][0], hs.rearrange("p h d -> p (h d)"), [hsB], [hsbf[hstate["cur"]][1]])
                    hb_ap, hb_b = hsbf[hstate["cur"]]
                    yb, ybb = PSA.next()
                    for h in range(8):
                        r = slice(64 * (h % 2), 64 * (h % 2) + 64)
                        yc = slice((h // 2) * cl, (h // 2 + 1) * cl)
                        MM(yb[r, yc], cx["xdt"][0:cl, h, :], cx["sc3"][:, h, :], True, False, [cx["xdtb"], cx["scb"]], [ybb], inc=False)
                        MM(yb[r, yc], hb_ap[:, h * 64:(h + 1) * 64], cx["cd3"][:, h, :], False, True, [hb_b, cx["cdb"]], [ybb], inc=(h == 7))
                    kv, kvb = PSB.next()
                    MM(kv[:, :], B_tm[0:cl, i, :], cx["xw"][0:cl].rearrange("p h d -> p (h d)"), True, True, [BtB[i], cx["xwb"]], [kvb])
                    TT("vector", hs, hs, cx["Ea3"][:, :, cl - 1:cl].broadcast_to([128, 8, 64]), ALU.mult, [hsB, cx["EaB"]], [hsB])
                    TT("vector", hs, hs, kv[:, :].rearrange("p (h d) -> p h d", h=8), ALU.add, [hsB, kvb], [hsB])
                    if i < 15:
                        hstate["cur"] ^= 1
                        CP("scalar", hsbf[hstate["cur"]][0], hs.rearrange("p h d -> p (h d)"), [hsB], [hsbf[hstate["cur"]][1]])
                    if i == 15:
                        state_out(o_ssm_p[l, 8 * g:8 * g + 8])
                    if i >= 16:
                        state_out(o_ssm_s[l, i - 16, 8 * g:8 * g + 8])
                    yd, ydb = ydp.next()
                    for c in range(4):
                        STT(yd[:, c * cl:(c + 1) * cl], xsT[:, c, cols], pcol(l, "ssm_d", 4 * g + c), yb[:, c * cl:(c + 1) * cl], ALU.mult, ALU.add,
                            [xsB[c][ti], ybb] + CONSTS, [ydb])
                    TT("gpsimd", zsT[:, :, cols], zsT[:, :, cols], yd[:, 0:4 * cl].rearrange("p (c t) -> p c t", c=4), ALU.mult,
                       [ydb] + [zB[c][ti] for c in range(4)], [zB[c][ti] for c in range(4)])

                cxs = {0: phaseA(0)}
                phaseA2(cxs[0])
                for i in range(18):
                    if i + 1 < 18:
                        cxs[i + 1] = phaseA(i + 1)
                    phaseB(i, cxs.pop(i))
                    if i + 1 < 18:
                        phaseA2(cxs[i + 1])
                P.barrier()
                AR.reset(mPh)
                sqp = Pool([(AR.alloc(512), Buf("ssq%d" % i)) for i in range(2)])
                rsp = Pool([(AR.alloc(512, F32), Buf("srs%d" % i)) for i in range(2)])
                P.tag = "ssm.norm"
                dq = []
                for ti in range(5):
                    c0, n = TILES[ti]
                    rms_norm_fm([(zsT[:, c, c0:c0 + n], [zB[c][ti]]) for c in range(4)], 4, lambda c, g=g: pcol(l, "ssm_norm", 4 * g + c),
                                lambda c, ti=ti, c0=c0, n=n: (zsT[:, c, c0:c0 + n], [zB[c][ti]]), n, 1.0 / 512, (sqp, rsp), dq=dq)
                while dq:
                    dq.pop()()
                if dbg == "ssm":
                    dump("ssmmix%d" % g, zsT, [zB[c][ti] for c in range(4) for ti in range(5)], BF16)
                xacc(l, 1024 + 512 * g, 4, zsT, zB, "ssm")
                P.barrier()
            for cc in range(12):
                STORE(o_conv_p[l][:, cc * 128:(cc + 1) * 128].rearrange("r p -> p r"), convout[:, 0, cc, :], bCO, slow=True)
                for s_ in range(2):
                    STORE(o_conv_s[l, s_][:, cc * 128:(cc + 1) * 128].rearrange("r p -> p r"), convout[:, 1 + s_, cc, :], bCO, slow=True)
            P.barrier()
            AR.reset(mR)

        NL = 2
        plan = dbg or "full"
        for l in range(NL):
            ffn(l, "ffn1")
            if plan == "ffn1":
                break
            mixer(l)
            if plan in ("ret", "mla", "ssm", "mix"):
                break
            ffn(l, "ffn2")

        m0 = AR.mark()
        ostg = [(AR.alloc(1024, F32), Buf("ostg%d" % i)) for i in range(2)]
        for i in range(17):
            s_ap, s_b = ostg[i % 2]
            if i < 16:
                rows, c0, ti = 128, i * 128, i // 4
            else:
                rows, c0, ti = 32, 2048, 4
            for hb in range(2):
                p_ap, p_b = PS.next()
                for cc in range(4):
                    c = hb * 4 + cc
                    P.op("tensor", lambda e, p_ap=p_ap, c=c, cc=cc, rows=rows, c0=c0: e.transpose(
                        out=p_ap[0:rows, cc * 128:(cc + 1) * 128], in_=xT[:, c, c0:c0 + rows], identity=ident_f),
                        reads=[xB[c][ti]] + CONSTS, writes=[p_b], inc=(cc == 3))
                if hb == 0:
                    P.op("scalar", lambda e, p_ap=p_ap, s_ap=s_ap, rows=rows: e.activation(out=s_ap[0:rows, 0:512], in_=p_ap[0:rows, :], func=AF.Copy),
                         reads=[p_b], writes=[s_b])
                else:
                    P.op("vector", lambda e, p_ap=p_ap, s_ap=s_ap, rows=rows: e.tensor_copy(out=s_ap[0:rows, 512:1024], in_=p_ap[0:rows, :]),
                         reads=[p_b], writes=[s_b])
            ob = Buf("out%d" % i)
            if i < 16:
                P.dma("sync", lambda e, s_ap=s_ap, i=i: e.dma_start(out=yp[i * 128:(i + 1) * 128, :], in_=s_ap), reads=[s_b], writes=[ob], sembuf=s_b)
            else:
                P.dma("sync", lambda e, s_ap=s_ap: e.dma_start(out=ys, in_=s_ap[0:32, :]), reads=[s_b], writes=[ob], sembuf=s_b)
            outbufs.append(ob)
        AR.reset(m0)

        P.wait_all("sync", outbufs)
        P.emit(nc, st)
    return nc


_NC_CACHE = {}


def make_in_maps(inp):
    ident, cb = const_tables()
    pcs = np.stack([pack_pcols(inp, l) for l in range(2)], axis=1)
    c_bf = np.stack([cb["ident_b"], cb["ones_b"]], axis=1)
    rtab, rmat, rc = ret_tables()
    mtab, mmat = mla_tables()
    U_np, NEG_np = ssm_tables()
    prow_np = np.ascontiguousarray(np.broadcast_to(
        np.concatenate([inp["ssm_dt_bias"], inp["ssm_a_log"], inp["mla_k_gain"]], axis=1)[None], (128, 2, 128))).astype(np.float32)
    maps = []
    for c in range(NCORES):
        m = {
            "xp": np.ascontiguousarray(inp["x_prompt"][c]),
            "xs": np.ascontiguousarray(inp["x_sample"][2 * c:2 * c + 2].reshape(32, D)),
            "pcols": pcs, "c_ident": ident, "c_bf": c_bf,
            "w_in": inp["w_in"], "w_out": inp["w_out"],
            "st_ret": np.ascontiguousarray(inp["state_ret"][:, 2 * c:2 * c + 2]),
            "c_rtab": rtab, "c_rmat": rmat, "c_rc": rc,
            "w_uq": inp["mla_w_uq"], "w_ukv": inp["mla_w_ukv"],
            "c_ckv": np.ascontiguousarray(inp["cache_mla_ckv"][:, 2 * c:2 * c + 2]),
            "c_kr": np.ascontiguousarray(inp["cache_mla_krope"][:, 2 * c:2 * c + 2]),
            "c_mtab": mtab, "c_mmat": mmat,
            "c_ssm": np.ascontiguousarray(inp["state_ssm"][:, 2 * c:2 * c + 2]),
            "c_conv": np.ascontiguousarray(inp["state_conv"][:, 2 * c:2 * c + 2]),
            "prow": prow_np, "c_U": U_np, "c_NEG": NEG_np,
            "ffn1_wgu": inp["ffn1_wgu"], "ffn1_wd": inp["ffn1_wd"],
            "ffn2_wgu": inp["ffn2_wgu"], "ffn2_wd": inp["ffn2_wd"],
        }
        maps.append(m)
    return maps


def kernel(**inputs):
    inp = {k: np.asarray(v) for k, v in inputs.items()}
    if "nc" not in _NC_CACHE:
        _NC_CACHE["nc"] = build_program()
    nc = _NC_CACHE["nc"]
    maps = make_in_maps(inp)
    res = run_bass_kernel_spmd(nc, maps, core_ids=list(range(NCORES)))
    r = res.results
    g = lambda c, k: np.asarray(r[c][k])
    y_p = np.stack([g(c, "yp") for c in range(NCORES)], axis=0)
    y_s = np.concatenate([g(c, "ys").reshape(2, 16, D) for c in range(NCORES)], axis=0)
    ckv_p = np.stack([g(c, "o_ckv_p") for c in range(NCORES)], axis=1)
    kr_p = np.stack([g(c, "o_kr_p") for c in range(NCORES)], axis=1)
    ret_p = np.stack([g(c, "o_ret_p") for c in range(NCORES)], axis=1)
    ssm_p = np.stack([g(c, "o_ssm_p") for c in range(NCORES)], axis=1)
    conv_p = np.stack([g(c, "o_conv_p") for c in range(NCORES)], axis=1)
    ckv_s = np.concatenate([g(c, "o_ckv_s").reshape(2, 2, 16, 256) for c in range(NCORES)], axis=1)
    kr_s = np.concatenate([g(c, "o_kr_s").reshape(2, 2, 16, 32) for c in range(NCORES)], axis=1)
    ret_s = np.concatenate([g(c, "o_ret_s") for c in range(NCORES)], axis=1)
    ssm_s = np.concatenate([g(c, "o_ssm_s") for c in range(NCORES)], axis=1)
    conv_s = np.concatenate([g(c, "o_conv_s") for c in range(NCORES)], axis=1)
    outs = (y_p, y_s, ckv_p, kr_p, ret_p, ssm_p, conv_p, ckv_s, kr_s, ret_s, ssm_s, conv_s)
    return tuple(np.ascontiguousarray(o, dtype=np.float32) for o in outs)
```

```python
import contextlib
import math
import os
import numpy as np
import ml_dtypes
import concourse.bass as bass
import concourse.mybir as mybir
from concourse.bass_utils import run_bass_kernel_spmd

F32 = mybir.dt.float32
BF16 = mybir.dt.bfloat16
AF = mybir.ActivationFunctionType
ALU = mybir.AluOpType

NCORES = 8
D = 1024
SEQ = 2048
T = 2080
DFF = 2816
NKF = DFF // 128
EPS = 1e-6
TILES = [(0, 512), (512, 512), (1024, 512), (1536, 512), (2048, 32)]
IN_COLS = 5296
RET0, MLA0, SSM0 = 0, 2048, 2720

_DBG_ENV = os.environ if os.environ.get("MK_DEBUG") else {}
ANNOTATE = bool(_DBG_ENV.get("MK_ANNOTATE"))
ENGS = ("sync", "tensor", "vector", "scalar", "gpsimd")
_SES = not _DBG_ENV.get("MK_NO_SES")
SAME_ENGINE_SYNC = {"vector": _SES, "scalar": _SES, "gpsimd": True, "tensor": False, "sync": False}


class Buf:
    __slots__ = ("name", "w", "r", "dsem", "dcnt", "excl")

    def __init__(self, name="", excl=False):
        self.name = name
        self.excl = excl
        self.w = None
        self.r = []
        self.dsem = None
        self.dcnt = 0


class Prog:
    def __init__(self):
        self.ops = {e: [] for e in ENGS}
        self.cnt = {e: 0 for e in ENGS}
        self.seen = {e: {} for e in ENGS}
        self.dma_sems = []
        self.dma_cnt = {}
        self.tag = ""

    def new_dma_sem(self):
        k = "d%d" % len(self.dma_sems)
        self.dma_sems.append(k)
        return k

    def _deps(self, eng, reads, writes, skip_waw=None):
        need = {}
        seen = self.seen[eng]

        def add(ev):
            if ev is None:
                return
            k, v = ev
            if k == eng and not SAME_ENGINE_SYNC[eng]:
                return
            if seen.get(k, 0) >= v:
                return
            if need.get(k, 0) < v:
                need[k] = v
        for b in reads:
            add(b.w)
            if b.excl:
                for ev in b.r:
                    if ev[0] != eng:
                        add(ev)
        for b in writes:
            if not (skip_waw is not None and b.w is not None and b.w[0] == skip_waw):
                add(b.w)
            for ev in b.r:
                add(ev)
        for k, v in need.items():
            seen[k] = v
        return list(need.items())

    def barrier(self, skip=()):
        for e in ENGS:
            if e in skip:
                continue
            waits = []
            for k in ENGS:
                v = self.cnt[k]
                if k != e and v > self.seen[e].get(k, 0):
                    waits.append((k, v))
                    self.seen[e][k] = v
            for k, v in self.dma_cnt.items():
                if v > self.seen[e].get(k, 0):
                    waits.append((k, v))
                    self.seen[e][k] = v
            self.ops[e].append((waits, None, None, ""))

    def op(self, eng, fn, reads=(), writes=(), inc=True):
        waits = self._deps(eng, reads, writes)
        self.ops[eng].append((waits, fn, ("E", eng) if inc else None, self.tag))
        if inc:
            self.cnt[eng] += 1
            ev = (eng, self.cnt[eng])
        else:
            ev = (eng, self.cnt[eng] + 1)
        for b in reads:
            b.r.append(ev)
            if len(b.r) > 48:
                b.r = b.r[-48:] if False else self._compact(b.r)
        for b in writes:
            b.w = ev
            b.r = []
        return ev

    @staticmethod
    def _compact(evs):
        best = {}
        for k, v in evs:
            if best.get(k, 0) < v:
                best[k] = v
        return list(best.items())

    def dma(self, eng, fn, reads=(), writes=(), sembuf=None):
        sb = sembuf if sembuf is not None else (writes[0] if writes else reads[0])
        if sb.dsem is None:
            sb.dsem = self.new_dma_sem()
        waits = self._deps(eng, reads, writes, skip_waw=sb.dsem)
        sb.dcnt += 16
        ev = (sb.dsem, sb.dcnt)
        self.dma_cnt[sb.dsem] = sb.dcnt
        self.ops[eng].append((waits, fn, ("D", sb.dsem), self.tag))
        for b in reads:
            b.r.append(ev)
        for b in writes:
            b.w = ev
            b.r = []
        return ev

    def wait_all(self, eng, bufs):
        waits = self._deps(eng, [], bufs)
        self.ops[eng].append((waits, None, None, ""))

    def check(self):
        sem = {}
        pc = {e: 0 for e in ENGS}
        total = sum(len(v) for v in self.ops.values())
        done = 0
        while done < total:
            prog = False
            for e in ENGS:
                ops = self.ops[e]
                while pc[e] < len(ops):
                    waits, fn, inc, _t = ops[pc[e]]
                    if any(sem.get(k, 0) < v for k, v in waits):
                        break
                    if inc is not None:
                        sem[inc[1]] = sem.get(inc[1], 0) + (1 if inc[0] == "E" else 16)
                    pc[e] += 1
                    done += 1
                    prog = True
            if not prog:
                msg = []
                for e in ENGS:
                    if pc[e] < len(self.ops[e]):
                        waits = self.ops[e][pc[e]][0]
                        msg.append((e, pc[e], [(k, v, sem.get(k, 0)) for k, v in waits if sem.get(k, 0) < v]))
                raise RuntimeError("DEADLOCK in recorded program: %s" % msg)
        return {e: len(v) for e, v in self.ops.items()}

    def emit(self, nc, st):
        print("[prog] ops per engine:", self.check(), "dma sems:", len(self.dma_sems))
        sems = {}
        for e in ENGS:
            sems[e] = st.enter_context(nc.semaphore("s_" + e))
        for k in self.dma_sems:
            sems[k] = st.enter_context(nc.semaphore("s_" + k))
        block = st.enter_context(nc.Block())
        ops = self.ops

        def body(ename):
            def run(eng):
                for waits, fn, inc, tag in ops[ename]:
                    for k, v in waits:
                        eng.wait_ge(sems[k], v)
                    if fn is None:
                        continue
                    ins = fn(eng)
                    if tag and ANNOTATE:
                        ins.annotate(tag)
                    if inc is not None:
                        ins.then_inc(sems[inc[1]], 1 if inc[0] == "E" else 16)
            return run

        block.sync(body("sync"))
        block.tensor(body("tensor"))
        block.vector(body("vector"))
        block.scalar(body("scalar"))
        block.gpsimd(body("gpsimd"))


class Arena:
    def __init__(self, ap, size):
        self.ap = ap
        self.size = size
        self.top = 0

    def alloc(self, ncols, dtype=BF16):
        n = ncols * (2 if dtype == F32 else 1)
        n = (n + 15) // 16 * 16
        a = self.top
        self.top += n
        assert self.top <= self.size, ("arena overflow", self.top, self.size)
        v = self.ap[:, a:a + ncols * (2 if dtype == F32 else 1)]
        if dtype == F32:
            v = v.bitcast(F32)
        return v

    def mark(self):
        return self.top

    def reset(self, m):
        self.top = m


class Pool:
    def __init__(self, slots):
        self.slots = slots
        self.i = 0

    def next(self):
        s = self.slots[self.i % len(self.slots)]
        self.i += 1
        return s


def pipeline(stages, depth):
    n = len(stages)
    for i in range(min(depth - 1, n)):
        stages[i][0](i % depth)
    for i in range(n):
        j = i + depth - 1
        if j < n:
            stages[j][0](j % depth)
        stages[i][1](i % depth)


PC = {}
_o = 0
for _n, _w in [("ffn1_norm", 8), ("mix_norm", 8), ("ffn2_norm", 8), ("ret_norm", 4), ("mla_q_norm", 3), ("mla_kv_norm", 2), ("mla_q_gain", 1), ("mla_k_gain", 1),
               ("conv_w", 48), ("conv_b", 12), ("ssm_d", 8), ("ssm_norm", 8)]:
    PC[_n] = (_o, _w)
    _o += _w
NPC = _o


def pack_pcols(inp, l):
    pc = np.zeros((128, NPC), np.float32)
    for n in ("ffn1_norm", "mix_norm", "ffn2_norm"):
        o, w = PC[n]
        pc[:, o:o + w] = inp[n][l].reshape(w, 128).T
    o, w = PC["ret_norm"]
    pc[:, o:o + w] = inp["ret_norm"][l].reshape(4, 128).T
    o, w = PC["mla_q_norm"]
    pc[:, o:o + w] = inp["mla_q_norm"][l].reshape(3, 128).T
    o, w = PC["mla_kv_norm"]
    pc[:, o:o + w] = inp["mla_kv_norm"][l].reshape(2, 128).T
    pc[0:96, PC["mla_q_gain"][0]] = inp["mla_q_gain"][l]
    pc[0:96, PC["mla_k_gain"][0]] = inp["mla_k_gain"][l]
    o, w = PC["conv_w"]
    pc[:, o:o + w] = inp["ssm_conv_w"][l].reshape(4, 12, 128).transpose(2, 1, 0).reshape(128, 48)
    o, w = PC["conv_b"]
    pc[:, o:o + w] = inp["ssm_conv_b"][l].reshape(12, 128).T
    o, w = PC["ssm_d"]
    pc[:, o:o + w] = np.repeat(inp["ssm_d"][l], 64).reshape(8, 128).T
    o, w = PC["ssm_norm"]
    pc[:, o:o + w] = inp["ssm_norm"][l].reshape(8, 128).T
    return pc


def ssm_tables():
    U = np.triu(np.ones((128, 128), np.float32))
    NEG = np.where(np.arange(128)[None, :] < np.arange(128)[:, None], -30000.0, 0.0).astype(np.float32)
    NEGt = np.zeros((128, 640), np.float32)
    NEGt[:, 0:512] = np.tile(NEG, (1, 4))
    NEGt[0:16, 512:640] = np.tile(NEG[0:16, 0:16], (1, 8))
    return U, NEGt.astype(ml_dtypes.bfloat16)


def mla_tables():
    pos = token_positions()
    theta = (1.0 / (10000.0 ** (np.arange(0, 32, 2, dtype=np.float32) / 32.0))).astype(np.float32)
    tab = np.zeros((128, 2, T), np.float64)
    for r in range(32):
        f = r % 16
        ang = (pos * theta[f]).astype(np.float32).astype(np.float64)
        tab[64 + r, 0] = np.cos(ang)
        tab[64 + r, 1] = np.sin(ang) * (-1.0 if r < 16 else 1.0)
    mats = np.zeros((128, 4, 128), np.float32)
    for m in range(32):
        mats[64 + (m + 16 if m < 16 else m - 16), 0, 64 + m] = 1.0
    mats[0:96, 1, 0:96] = 1.0
    mats[0, 2, 64:128] = 1.0
    mats[0, 3, 0:64] = -30000.0
    return tab.astype(ml_dtypes.bfloat16), mats.astype(ml_dtypes.bfloat16)


def token_positions():
    return np.concatenate([np.arange(2048), 1024 + np.arange(16), 1024 + np.arange(16)]).astype(np.float32)


RC = {"DmT": (0, 1024), "QD": (1024, 512), "KD": (1536, 512), "cdec": (2048, 4),
      "DmTs": (2052, 128), "QDs": (2180, 64), "KDs": (2244, 512), "cdecs": (2756, 4)}
NRC = 2760


def ret_tables():
    pos = token_positions()
    theta = (1.0 / (10000.0 ** np.linspace(0.0, 1.0, 32, dtype=np.float32))).astype(np.float32)
    p = np.arange(128)
    d = p % 64
    f = d % 32
    ang = (pos[None, :] * theta[f][:, None]).astype(np.float32).astype(np.float64)
    C = np.cos(ang)
    S = np.sin(ang) * np.where(d < 32, -1.0, 1.0)[:, None]
    tab = np.stack([C, S], axis=1).astype(ml_dtypes.bfloat16)
    Pm = np.zeros((128, 128), np.float32)
    for m in range(128):
        Pm[m + 32 if (m % 64) < 32 else m - 32, m] = 1.0
    BD = np.zeros((128, 128), np.float32)
    BD[0:64, 0:64] = 1.0
    BD[64:128, 64:128] = 1.0
    mats = np.stack([Pm, BD], axis=1).astype(ml_dtypes.bfloat16)
    g = 1.0 - 2.0 ** (-5.0 - np.arange(8, dtype=np.float64))
    rc = np.zeros((128, NRC), np.float64)

    def put(name, arr):
        o, w = RC[name]
        a = arr.reshape(arr.shape[0], -1)
        rc[:a.shape[0], o:o + w] = a
    for c, sfx in ((128, ""), (16, "s")):
        k = np.arange(c)[:, None, None]
        q = np.arange(c)[None, None, :]
        gh = g[None, :, None]
        put("DmT" + sfx, np.where(q >= k, gh ** np.maximum(q - k, 0), 0.0))
        gp = g[2 * np.arange(4)[None, :, None] + (p // 64)[:, None, None]]
        put("QD" + sfx, gp ** (np.arange(c)[None, None, :] + 1.0))
        gj = g[2 * np.arange(4)[None, :, None] + (p // 64)[None, None, :]]
        put("KD" + sfx, gj ** (c - 1.0 - np.arange(c)[:, None, None]))
        put("cdec" + sfx, (gp ** float(c))[:, :, 0])
    return tab, mats, rc.astype(np.float32)


def const_tables():
    ident = np.eye(128, dtype=np.float32)
    cb = {}
    cb["ident_b"] = ident.astype(ml_dtypes.bfloat16)
    cb["ones_b"] = np.ones((128, 128), ml_dtypes.bfloat16)
    return ident, cb


def build_program(dbg=None):
    nc = bass.Bass("TRN2", target_bir_lowering=False)
    dt = nc.dram_tensor
    xp = dt("xp", [SEQ, D], F32, kind="ExternalInput").ap()
    xs = dt("xs", [32, D], F32, kind="ExternalInput").ap()
    pcols = dt("pcols", [128, 2, NPC], F32, kind="ExternalInput").ap()
    c_ident = dt("c_ident", [128, 128], F32, kind="ExternalInput").ap()
    c_bf = dt("c_bf", [128, 2, 128], BF16, kind="ExternalInput").ap()
    wgu = [dt("ffn%d_wgu" % i, [2, D, 2 * DFF], F32, kind="ExternalInput").ap() for i in (1, 2)]
    wd = [dt("ffn%d_wd" % i, [2, DFF, D], F32, kind="ExternalInput").ap() for i in (1, 2)]
    w_in = dt("w_in", [2, D, IN_COLS], F32, kind="ExternalInput").ap()
    w_out = dt("w_out", [2, 2048, D], F32, kind="ExternalInput").ap()
    st_ret = dt("st_ret", [2, 2, 8, 64, 64], F32, kind="ExternalInput").ap()
    c_rtab = dt("c_rtab", [128, 2, T], BF16, kind="ExternalInput").ap()
    c_rmat = dt("c_rmat", [128, 2, 128], BF16, kind="ExternalInput").ap()
    c_rc = dt("c_rc", [128, NRC], F32, kind="ExternalInput").ap()
    o_ret_p = dt("o_ret_p", [2, 8, 64, 64], F32, kind="ExternalOutput").ap()
    o_ret_s = dt("o_ret_s", [2, 2, 8, 64, 64], F32, kind="ExternalOutput").ap()
    w_uq = dt("w_uq", [2, 384, 768], F32, kind="ExternalInput").ap()
    w_ukv = dt("w_ukv", [2, 256, 1024], F32, kind="ExternalInput").ap()
    c_ckv = dt("c_ckv", [2, 2, 1024, 256], F32, kind="ExternalInput").ap()
    c_kr = dt("c_kr", [2, 2, 1024, 32], F32, kind="ExternalInput").ap()
    c_mtab = dt("c_mtab", [128, 2, T], BF16, kind="ExternalInput").ap()
    c_mmat = dt("c_mmat", [128, 4, 128], BF16, kind="ExternalInput").ap()
    o_ckv_p = dt("o_ckv_p", [2, SEQ, 256], F32, kind="ExternalOutput").ap()
    o_kr_p = dt("o_kr_p", [2, SEQ, 32], F32, kind="ExternalOutput").ap()
    o_ckv_s = dt("o_ckv_s", [2, 32, 256], F32, kind="ExternalOutput").ap()
    o_kr_s = dt("o_kr_s", [2, 32, 32], F32, kind="ExternalOutput").ap()
    c_ssm = dt("c_ssm", [2, 2, 16, 64, 128], F32, kind="ExternalInput").ap()
    c_conv = dt("c_conv", [2, 2, 3, 1536], F32, kind="ExternalInput").ap()
    prow = dt("prow", [128, 2, 128], F32, kind="ExternalInput").ap()
    c_U = dt("c_U", [128, 128], F32, kind="ExternalInput").ap()
    c_NEG = dt("c_NEG", [128, 640], BF16, kind="ExternalInput").ap()
    o_ssm_p = dt("o_ssm_p", [2, 16, 64, 128], F32, kind="ExternalOutput").ap()
    o_ssm_s = dt("o_ssm_s", [2, 2, 16, 64, 128], F32, kind="ExternalOutput").ap()
    o_conv_p = dt("o_conv_p", [2, 3, 1536], F32, kind="ExternalOutput").ap()
    o_conv_s = dt("o_conv_s", [2, 2, 3, 1536], F32, kind="ExternalOutput").ap()
    dbg_out = {}
    yp = dt("yp", [SEQ, D], F32, kind="ExternalOutput").ap()
    ys = dt("ys", [32, D], F32, kind="ExternalOutput").ap()

    st = contextlib.ExitStack()
    with st:
        P = Prog()
        xT_t = st.enter_context(nc.sbuf_tensor("xT", [128, 8, T], F32))
        xT = xT_t[:]
        xB = [[Buf("x%d_%d" % (c, i)) for i in range(5)] for c in range(8)]
        WA_t = st.enter_context(nc.sbuf_tensor("WA", [128, 2, 4096], BF16))
        WA = [(WA_t[:, i, :], Buf("WA%d" % i)) for i in range(2)]
        cst_t = st.enter_context(nc.sbuf_tensor("cst", [128, 2 * NPC + 128], F32))
        pc_sb = cst_t[:, 0:2 * NPC].rearrange("p (l n) -> p l n", l=2)
        ident_f = cst_t[:, 2 * NPC:2 * NPC + 128]
        cbf_t = st.enter_context(nc.sbuf_tensor("cbf", [128, 2, 128], BF16))
        ident_b = cbf_t[:, 0, :]
        ones_b = cbf_t[:, 1, :]
        bC = Buf("consts")
        ARENA = 63800
        ar_t = st.enter_context(nc.sbuf_tensor("arena", [128, ARENA], BF16))
        AR = Arena(ar_t[:], ARENA)
        psum = []
        for i in range(8):
            pt = st.enter_context(nc.psum_tensor("ps%d" % i, [128, 512], F32))
            psum.append((pt[:], Buf("ps%d" % i, excl=True)))
        PS = Pool(psum[2:8])
        PSA = Pool(psum[0:2])
        outbufs = []

        def pcol(l, name, c):
            o, w = PC[name]
            return pc_sb[:, l, o + c:o + c + 1]

        P.dma("sync", lambda e: e.dma_start(out=pc_sb, in_=pcols), writes=[bC])
        bC2 = Buf("c2")
        P.dma("sync", lambda e: e.dma_start(out=ident_f, in_=c_ident), writes=[bC2])
        bC3 = Buf("c3")
        P.dma("sync", lambda e: e.dma_start(out=cbf_t[:], in_=c_bf), writes=[bC3])
        CONSTS = [bC, bC2, bC3]

        m0 = AR.mark()
        stg = [(AR.alloc(1024, F32), Buf("stg%d" % i)) for i in range(2)]
        for i in range(17):
            s_ap, s_b = stg[i % 2]
            if i < 16:
                rows = 128
                P.dma("sync", lambda e, s_ap=s_ap, i=i: e.dma_start(out=s_ap, in_=xp[i * 128:(i + 1) * 128, :]), writes=[s_b])
                c0 = i * 128
                ti = i // 4
            else:
                rows = 32
                P.dma("sync", lambda e, s_ap=s_ap: e.dma_start(out=s_ap[0:32, :], in_=xs), writes=[s_b])
                c0 = 2048
                ti = 4
            for hb in range(2):
                p_ap, p_b = PS.next()
                for cc in range(4):
                    c = hb * 4 + cc
                    P.op("tensor", lambda e, p_ap=p_ap, s_ap=s_ap, c=c, cc=cc, rows=rows: e.transpose(
                        out=p_ap[:, cc * rows:(cc + 1) * rows], in_=s_ap[0:rows, c * 128:(c + 1) * 128], identity=ident_f[0:rows, 0:rows]),
                        reads=[s_b] + CONSTS, writes=[p_b], inc=(cc == 3))
                eng = "scalar" if hb == 0 else "vector"
                src = p_ap[:, 0:4 * rows].rearrange("p (c t) -> p c t", c=4)
                dst = xT[:, hb * 4:hb * 4 + 4, c0:c0 + rows]
                if eng == "scalar":
                    P.op("scalar", lambda e, src=src, dst=dst: e.activation(out=dst, in_=src, func=AF.Copy),
                         reads=[p_b], writes=[xB[c][ti] for c in range(hb * 4, hb * 4 + 4)])
                else:
                    P.op("vector", lambda e, src=src, dst=dst: e.tensor_copy(out=dst, in_=src),
                         reads=[p_b], writes=[xB[c][ti] for c in range(hb * 4, hb * 4 + 4)])
        P.barrier(skip=("gpsimd",))
        AR.reset(m0)

        def rms_norm_fm(src_chunks, nch, gain_fn, dst_fn, n, inv_count, tmp, dq=None):
            sq, rs = tmp
            p_ap, p_b = PS.next()
            for c in range(nch):
                s_ap, s_bufs = src_chunks[c]
                q_ap, q_b = sq.next()
                P.op("scalar", lambda e, q_ap=q_ap, s_ap=s_ap: e.activation(out=q_ap[:, 0:n], in_=s_ap, func=AF.Square),
                     reads=s_bufs, writes=[q_b])
                P.op("tensor", lambda e, p_ap=p_ap, q_ap=q_ap, c=c: e.matmul(p_ap[:, 0:n], lhsT=ones_b, rhs=q_ap[:, 0:n], start=(c == 0), stop=(c == nch - 1)),
                     reads=[q_b] + CONSTS, writes=[p_b])
            def finish():
                r_ap, r_b = rs.next()
                P.op("scalar", lambda e: e.activation(out=r_ap[:, 0:n], in_=p_ap[:, 0:n], func=AF.Ln, bias=EPS, scale=inv_count),
                     reads=[p_b], writes=[r_b])
                P.op("scalar", lambda e: e.activation(out=r_ap[:, 0:n], in_=r_ap[:, 0:n], func=AF.Exp, scale=-0.5),
                     reads=[r_b], writes=[r_b])
                for c in range(nch):
                    s_ap, s_bufs = src_chunks[c]
                    d_ap, d_bufs = dst_fn(c)
                    g_ap = gain_fn(c)
                    P.op("vector", lambda e, d_ap=d_ap, s_ap=s_ap, g_ap=g_ap: e.scalar_tensor_tensor(
                        out=d_ap, in0=s_ap, scalar=g_ap, in1=r_ap[:, 0:n], op0=ALU.mult, op1=ALU.mult),
                        reads=s_bufs + [r_b] + CONSTS, writes=d_bufs)

            if dq is None:
                finish()
            else:
                while dq:
                    dq.pop()()
                dq.append(finish)

        def ffn(l, which):
            wgu_ap = wgu[0 if which == "ffn1" else 1]
            wd_ap = wd[0 if which == "ffn1" else 1]
            nname = which + "_norm"
            m = AR.mark()
            HB = 1056
            hT = AR.alloc(8 * HB).rearrange("p (c t) -> p c t", c=8)
            aT = AR.alloc(NKF * HB).rearrange("p (c t) -> p c t", c=NKF)
            WB = [(AR.alloc(NKF * 256).rearrange("p (k n) -> p k n", k=NKF), Buf("WB%d" % i)) for i in range(2)]
            sq = Pool([(AR.alloc(512), Buf("sq%d" % i)) for i in range(2)])
            rs = Pool([(AR.alloc(512, F32), Buf("rs%d" % i)) for i in range(2)])
            sg = Pool([(AR.alloc(512, F32), Buf("sg%d" % i)) for i in range(2)])
            hB = [[Buf("h%d_%d" % (c, ti)) for ti in range(2)] + [Buf("h%d_%d" % (c, ti)) for ti in range(3)] for c in range(8)]
            aB = [[Buf("a%d_%d" % (j, ti)) for ti in range(2)] + [Buf("a%d_%d" % (j, ti)) for ti in range(3)] for j in range(NKF)]
            for c in range(8):
                hB[c][2] = hB[c][0]; hB[c][3] = hB[c][1]
            for j in range(NKF):
                aB[j][2] = aB[j][0]; aB[j][3] = aB[j][1]
            def half_parts(half):
                tiles = [0, 1] if half == 0 else [2, 3, 4]
                hb0 = 0 if half == 0 else 1024

                def do_norm():
                    dq = []
                    for ti in tiles:
                        c0, n = TILES[ti]
                        rms_norm_fm([(xT[:, c, c0:c0 + n], [xB[c][ti]]) for c in range(8)], 8,
                                    lambda c: pcol(l, nname, c),
                                    lambda c, ti=ti, c0=c0, n=n: (hT[:, c, c0 - hb0:c0 - hb0 + n], [hB[c][ti]]),
                                    n, 1.0 / D, (sq, rs), dq=dq)
                    while dq:
                        dq.pop()()

                def do_gu():
                    P.tag = "ffn.gu"
                    stages = []
                    for g in range(11):
                        def load(slot, g=g):
                            w_ap, w_b = WA[slot]
                            wv = w_ap.rearrange("p (u k n) -> p u k n", u=2, k=8)
                            src = wgu_ap[l].rearrange("(k p) n -> p k n", p=128)
                            P.dma("gpsimd", lambda e: e.dma_start(out=wv[:, 0], in_=src[:, :, g * 256:(g + 1) * 256]), writes=[w_b])
                            P.dma("gpsimd", lambda e: e.dma_start(out=wv[:, 1], in_=src[:, :, DFF + g * 256:DFF + (g + 1) * 256]), writes=[w_b])

                        def comp(slot, g=g):
                            w_ap, w_b = WA[slot]
                            wv = w_ap.rearrange("p (u k n) -> p u k n", u=2, k=8)
                            for pr in range(2):
                                j = g * 2 + pr
                                for ti in tiles:
                                    c0, n = TILES[ti]
                                    cs = slice(c0 - hb0, c0 - hb0 + n)
                                    pg, pgb = PS.next()
                                    pu, pub = PS.next()
                                    for u, (pp, ppb) in enumerate(((pg, pgb), (pu, pub))):
                                        for k in range(8):
                                            P.op("tensor", lambda e, pp=pp, u=u, k=k, pr=pr, cs=cs, n=n: e.matmul(
                                                pp[:, 0:n], lhsT=wv[:, u, k, pr * 128:(pr + 1) * 128], rhs=hT[:, k, cs], start=(k == 0), stop=(k == 7)),
                                                reads=[w_b, hB[k][ti]], writes=[ppb], inc=(k == 7))
                                    s_ap, s_b = sg.next()
                                    P.op("scalar", lambda e, s_ap=s_ap, pg=pg, n=n: e.activation(out=s_ap[:, 0:n], in_=pg[:, 0:n], func=AF.Silu),
                                         reads=[pgb], writes=[s_b])
                                    P.op("vector", lambda e, s_ap=s_ap, pu=pu, j=j, cs=cs, n=n: e.tensor_tensor(
                                        out=aT[:, j, cs], in0=pu[:, 0:n], in1=s_ap[:, 0:n], op=ALU.mult),
                                        reads=[pub, s_b], writes=[aB[j][ti]])
                        stages.append((load, comp))
                    pipeline(stages, 2)

                def do_down():
                    P.tag = "ffn.down"
                    stages = []
                    for mb in range(4):
                        def load(slot, mb=mb):
                            w_ap, w_b = WB[slot]
                            src = wd_ap[l].rearrange("(k p) n -> p k n", p=128)
                            P.dma("gpsimd", lambda e: e.dma_start(out=w_ap, in_=src[:, :, mb * 256:(mb + 1) * 256]), writes=[w_b])

                        def comp(slot, mb=mb):
                            w_ap, w_b = WB[slot]
                            for mm in range(2):
                                mo = mb * 2 + mm
                                for ti in tiles:
                                    c0, n = TILES[ti]
                                    cs = slice(c0 - hb0, c0 - hb0 + n)
                                    pp, ppb = PS.next()
                                    for k in range(NKF):
                                        P.op("tensor", lambda e, pp=pp, k=k, mm=mm, cs=cs, n=n: e.matmul(
                                            pp[:, 0:n], lhsT=w_ap[:, k, mm * 128:(mm + 1) * 128], rhs=aT[:, k, cs], start=(k == 0), stop=(k == NKF - 1)),
                                            reads=[w_b, aB[k][ti]], writes=[ppb], inc=(k == NKF - 1))
                                    P.op("vector", lambda e, pp=pp, mo=mo, c0=c0, n=n: e.scalar_tensor_tensor(
                                        out=xT[:, mo, c0:c0 + n], in0=pp[:, 0:n], scalar=0.5, in1=xT[:, mo, c0:c0 + n], op0=ALU.mult, op1=ALU.add),
                                        reads=[ppb, xB[mo][ti]], writes=[xB[mo][ti]])
                        stages.append((load, comp))
                    pipeline(stages, 2)

                return do_norm, do_gu, do_down

            nA, gA, dA = half_parts(0)
            nB, gB, dB = half_parts(1)
            nA(); gA(); nB(); dA(); gB(); dB()
            P.barrier()
            AR.reset(m)

        def MM(out, lhsT, rhs, start, stop, reads, writes, inc=True, sgc=False):
            P.op("tensor", lambda e: e.matmul(out, lhsT=lhsT, rhs=rhs, start=start, stop=stop, skip_group_check=sgc), reads=reads, writes=writes, inc=inc)

        def TR(out, in_, ident, reads, writes, inc=True):
            P.op("tensor", lambda e: e.transpose(out=out, in_=in_, identity=ident), reads=reads, writes=writes, inc=inc)

        def ACT(out, in_, func, reads, writes, **kw):
            P.op("scalar", lambda e: e.activation(out=out, in_=in_, func=func, **kw), reads=reads, writes=writes)

        def TT(eng, out, in0, in1, op, reads, writes):
            if eng == "gpsimd":
                eng = _DBG_ENV.get("POOL_ENG", "gpsimd")
            P.op(eng, lambda e: e.tensor_tensor(out=out, in0=in0, in1=in1, op=op), reads=reads, writes=writes)

        def STT(out, in0, scalar, in1, op0, op1, reads, writes):
            P.op("vector", lambda e: e.scalar_tensor_tensor(out=out, in0=in0, scalar=scalar, in1=in1, op0=op0, op1=op1), reads=reads, writes=writes)

        def TS(eng, out, in0, s1, s2, op0, op1, reads, writes):
            P.op(eng, lambda e: e.tensor_scalar(out=out, in0=in0, scalar1=s1, scalar2=s2, op0=op0, op1=op1), reads=reads, writes=writes)

        def CP(eng, out, in_, reads, writes):
            if eng == "scalar":
                P.op("scalar", lambda e: e.activation(out=out, in_=in_, func=AF.Copy), reads=reads, writes=writes)
            else:
                P.op(eng, lambda e: e.tensor_copy(out=out, in_=in_), reads=reads, writes=writes)

        def MEMSET(eng, ap, val, writes):
            P.op(eng, lambda e: e.memset(ap, val), writes=writes)

        def LOAD(eng, out, in_, wbuf, reads=(), slow=False):
            P.dma(eng, lambda e: e.dma_start(out=out, in_=in_, allow_slow_non_contiguous=slow), reads=list(reads), writes=[wbuf])

        def STORE(out, in_, sbuf_buf, slow=False):
            ob = Buf("o")
            P.dma("sync", lambda e: e.dma_start(out=out, in_=in_, allow_slow_non_contiguous=slow), reads=[sbuf_buf], writes=[ob], sembuf=sbuf_buf)
            outbufs.append(ob)

        def dump(name, ap, bufs, dtype=F32):
            shp = list(ap.shape)
            t_ = dt("dbg_" + name, shp, dtype, kind="ExternalOutput").ap()
            for b in bufs[:1]:
                ob = Buf("o")
                P.dma("sync", lambda e: e.dma_start(out=t_, in_=ap), reads=list(bufs), writes=[ob], sembuf=Buf("dsem"))
                outbufs.append(ob)

        def xacc_prefetch(l, wrow0):
            w_ap, w_b = WA[0]
            wv = w_ap.rearrange("p (k n) -> p k n", k=4)
            LOAD("gpsimd", wv, w_out[l][wrow0:wrow0 + 512, :].rearrange("(k p) n -> p k n", p=128), w_b)

        def xacc(l, wrow0, nk, mixT, mixB, tag, preloaded=False):
            P.tag = "xacc." + tag
            for half in range(nk // 4):
                w_ap, w_b = WA[half % 2] if nk > 4 else WA[0]
                wv = w_ap.rearrange("p (k n) -> p k n", k=4)
                src = w_out[l][wrow0 + half * 512:wrow0 + (half + 1) * 512, :].rearrange("(k p) n -> p k n", p=128)
                if not preloaded:
                    LOAD("gpsimd", wv, src, w_b)
                for mo in range(8):
                    for ti in range(5):
                        c0, n = TILES[ti]
                        pp, ppb = PS.next()
                        for k in range(4):
                            MM(pp[:, 0:n], wv[:, k, mo * 128:(mo + 1) * 128], mixT[:, half * 4 + k, c0:c0 + n], k == 0, k == 3,
                               [w_b, mixB[half * 4 + k][ti]], [ppb], inc=(k == 3))
                        TT("vector", xT[:, mo, c0:c0 + n], pp[:, 0:n], xT[:, mo, c0:c0 + n], ALU.add, [ppb, xB[mo][ti]], [xB[mo][ti]])

        def mixer(l, stop_after=None):
            mM = AR.mark()
            hT_flat = AR.alloc(8 * T)
            hT = hT_flat.rearrange("p (c t) -> p c t", c=8)
            hB = [[Buf("mh%d_%d" % (c, ti)) for ti in range(5)] for c in range(8)]
            mt = AR.mark()
            sq = Pool([(AR.alloc(512), Buf("sq%d" % i)) for i in range(2)])
            rs = Pool([(AR.alloc(512, F32), Buf("rs%d" % i)) for i in range(2)])
            dq = []
            for ti in range(5):
                c0, n = TILES[ti]
                rms_norm_fm([(xT[:, c, c0:c0 + n], [xB[c][ti]]) for c in range(8)], 8,
                            lambda c: pcol(l, "mix_norm", c),
                            lambda c, ti=ti, c0=c0, n=n: (hT[:, c, c0:c0 + n], [hB[c][ti]]),
                            n, 1.0 / D, (sq, rs), dq=dq)
            while dq:
                dq.pop()()
            P.barrier()
            AR.reset(mt)
            if dbg not in ("mla", "ssm"):
                group_ret(l, hT, hB)
            if dbg == "ret":
                return
            if dbg != "mla":
                group_ssm(l, hT, hB)
            if dbg == "ssm":
                return
            group_mla(l, hT, hB, hT_flat)
            P.barrier()
            AR.reset(mM)

        def group_ret(l, hT, hB):
            mR = AR.mark()
            rtab = AR.alloc(2 * T).rearrange("p (a t) -> p a t", a=2)
            rmat = AR.alloc(256).rearrange("p (a t) -> p a t", a=2)
            rc = AR.alloc(NRC, F32)
            bRT, bRM, bRC = Buf("rtab"), Buf("rmat"), Buf("rc")
            LOAD("sync", rtab, c_rtab, bRT)
            LOAD("sync", rmat, c_rmat, bRM)
            LOAD("sync", rc, c_rc, bRC)
            Ctab, Stab = rtab[:, 0, :], rtab[:, 1, :]
            Pm, BD = rmat[:, 0, :], rmat[:, 1, :]

            def rcv(name, shape3=None):
                o, w = RC[name]
                v = rc[:, o:o + w]
                return v
            DmT = rcv("DmT").rearrange("p (h q) -> p h q", h=8)
            QD = rcv("QD").rearrange("p (h q) -> p h q", h=4)
            KD = rcv("KD").rearrange("p (h q) -> p h q", h=4)
            cdec = rcv("cdec")
            DmTs = rcv("DmTs").rearrange("p (h q) -> p h q", h=8)
            QDs = rcv("QDs").rearrange("p (h q) -> p h q", h=4)
            KDs = rcv("KDs").rearrange("p (h q) -> p h q", h=4)
            cdecs = rcv("cdecs")
            RCB = [bRT, bRM, bRC]

            mixT = AR.alloc(4 * T).rearrange("p (c t) -> p c t", c=4)
            mixB = [[Buf("rm%d_%d" % (c, ti)) for ti in range(5)] for c in range(4)]
            qT, kT, qdT, sgT = AR.alloc(T), AR.alloc(T), AR.alloc(T), AR.alloc(T)
            v_tm = AR.alloc(18 * 128).rearrange("p (i d) -> p i d", i=18)
            kd_tm = AR.alloc(18 * 128).rearrange("p (i d) -> p i d", i=18)
            S_bf = AR.alloc(18 * 64).rearrange("p (i d) -> p i d", i=18)
            S_f = [AR.alloc(64, F32) for _ in range(3)]
            rawp = Pool([(AR.alloc(512), Buf("raw%d" % i)) for i in range(2)])
            t1p = Pool([(AR.alloc(512, F32), Buf("t1_%d" % i)) for i in range(2)])
            t2p = Pool([(AR.alloc(512, F32), Buf("t2_%d" % i)) for i in range(2)])
            attp = Pool([(AR.alloc(128), Buf("att%d" % i)) for i in range(3)])
            sqp = Pool([(AR.alloc(512), Buf("rsq%d" % i)) for i in range(2)])
            rsp = Pool([(AR.alloc(512, F32), Buf("rrs%d" % i)) for i in range(2)])
            psb_view = lambda p_ap: p_ap.bitcast(BF16)

            qB = [Buf("q%d" % ti) for ti in range(5)]
            kB = [Buf("k%d" % ti) for ti in range(5)]
            qdB = [Buf("qd%d" % ti) for ti in range(5)]
            sgB = [Buf("sg%d" % ti) for ti in range(5)]
            vB = [Buf("v%d" % i) for i in range(18)]
            kdB = [Buf("kd%d" % i) for i in range(18)]
            SbB = [Buf("Sb%d" % i) for i in range(18)]
            SfB = [Buf("Sf%d" % i) for i in range(3)]
            for hp in range(4):
                if _DBG_ENV.get("RET_STOP") == "0":
                    break
                w_ap, w_b = WA[hp % 2]
                wv = w_ap.rearrange("p (u k n) -> p u k n", u=4, k=8)[:, :, :, 0:128] if False else w_ap[:, 0:4096].rearrange("p (u k n) -> p u k n", u=4, k=8)
                srcw = w_in[l].rearrange("(k p) n -> p k n", p=128)
                for u in range(4):
                    LOAD("gpsimd", wv[:, u], srcw[:, :, RET0 + 512 * u + 128 * hp:RET0 + 512 * u + 128 * hp + 128], w_b)
                if hp == 3:
                    xacc_prefetch(l, 0)
                for s_ in range(2):
                    LOAD("sync", S_f[1 + s_], st_ret[l, s_, 2 * hp:2 * hp + 2].rearrange("h k v -> (h k) v"), SfB[1 + s_])
                MEMSET("vector", S_f[0], 0.0, [SfB[0]])
                MEMSET("vector", S_bf[:, 0, :], 0.0, [SbB[0]])
                for s_ in range(2):
                    CP("scalar", S_bf[:, 16 + s_, :], S_f[1 + s_], [SfB[1 + s_]], [SbB[16 + s_]])
                if _DBG_ENV.get("RET_STOP") == "A1":
                    break
                P.tag = "ret.proj"
                pendB = None
                for ti in range(5):
                    c0, n = TILES[ti]
                    cs = slice(c0, c0 + n)
                    for u, (dst, dB, scale) in enumerate(((qT, qB, 1.0), (kT, kB, 0.125))):
                        pp, ppb = PS.next()
                        for k in range(8):
                            MM(pp[:, 0:n], wv[:, u, k, :], hT[:, k, cs], k == 0, k == 7, [w_b, hB[k][ti]], [ppb], inc=(k == 7))
                        r_ap, r_b = rawp.next()
                        CP("scalar", r_ap[:, 0:n], pp[:, 0:n], [ppb], [r_b])
                        if pendB is not None:
                            pendB()

                        def pendB(pp=pp, ppb=ppb, r_ap=r_ap, r_b=r_b, n=n, cs=cs, scale=scale, dst=dst, dB=dB, ti=ti, u=u):
                            p2, p2b = PS.next()
                            MM(p2[:, 0:n], Pm, r_ap[:, 0:n], True, True, [r_b] + RCB, [p2b])
                            a1, a1b = t1p.next()
                            a2, a2b = t2p.next()
                            STT(a1[:, 0:n], pp[:, 0:n], scale, Ctab[:, cs], ALU.mult, ALU.mult, [ppb] + RCB, [a1b])
                            STT(a2[:, 0:n], p2[:, 0:n], scale, Stab[:, cs], ALU.mult, ALU.mult, [p2b] + RCB, [a2b])
                            TT("vector", dst[:, cs], a1[:, 0:n], a2[:, 0:n], ALU.add, [a1b, a2b], [dB[ti]])
                            if u == 0:
                                if ti < 4:
                                    TT("vector", qdT[:, cs].rearrange("p (a b) -> p a b", b=128), qT[:, cs].rearrange("p (a b) -> p a b", b=128),
                                       QD[:, hp, :].unsqueeze(1).broadcast_to([128, 4, 128]), ALU.mult, [qB[ti]] + RCB, [qdB[ti]])
                                else:
                                    TT("vector", qdT[:, cs].rearrange("p (a b) -> p a b", b=16), qT[:, cs].rearrange("p (a b) -> p a b", b=16),
                                       QDs[:, hp, :].unsqueeze(1).broadcast_to([128, 2, 16]), ALU.mult, [qB[ti]] + RCB, [qdB[ti]])
                    pp, ppb = PS.next()
                    for k in range(8):
                        MM(pp[:, 0:n], wv[:, 3, k, :], hT[:, k, cs], k == 0, k == 7, [w_b, hB[k][ti]], [ppb], inc=(k == 7))
                    ACT(sgT[:, cs], pp[:, 0:n], AF.Silu, [ppb], [sgB[ti]])
                pendB()
                if _DBG_ENV.get("RET_STOP", "")[0:1] in ("A", "L"):
                    break
                P.tag = "ret.vk"
                for ti in range(4):
                    pp, ppb = PS.next()
                    for j in range(4):
                        i = ti * 4 + j
                        for k in range(8):
                            MM(pp[:, j * 128:(j + 1) * 128], hT[:, k, i * 128:(i + 1) * 128], wv[:, 2, k, :], k == 0, k == 7,
                               [w_b, hB[k][ti]], [ppb], inc=(k == 7 and j == 3))
                    CP("scalar", v_tm[:, ti * 4:ti * 4 + 4, :], pp[:, :].rearrange("p (a b) -> p a b", b=128), [ppb], vB[ti * 4:ti * 4 + 4])
                    pt, ptb = PS.next()
                    ptv = psb_view(pt)
                    for j in range(4):
                        i = ti * 4 + j
                        TR(ptv[:, j * 128:(j + 1) * 128], kT[:, i * 128:(i + 1) * 128], ident_b, [kB[ti]] + CONSTS, [ptb], inc=(j == 3))
                    TT("vector", kd_tm[:, ti * 4:ti * 4 + 4, :], ptv[:, 0:512].rearrange("p (a b) -> p a b", b=128),
                       KD[:, hp, :].unsqueeze(1).broadcast_to([128, 4, 128]), ALU.mult, [ptb] + RCB, kdB[ti * 4:ti * 4 + 4])
                pp, ppb = PS.next()
                pt, ptb = PS.next()
                ptv = psb_view(pt)
                for s_ in range(2):
                    sc = slice(2048 + 16 * s_, 2064 + 16 * s_)
                    for k in range(8):
                        MM(pp[0:16, s_ * 128:(s_ + 1) * 128], hT[:, k, sc], wv[:, 2, k, :], k == 0, k == 7, [w_b, hB[k][4]], [ppb], inc=(k == 7 and s_ == 1))
                    TR(ptv[0:16, s_ * 128:(s_ + 1) * 128], kT[:, sc], ident_b, [kB[4]] + CONSTS, [ptb], inc=(s_ == 1))
                CP("scalar", v_tm[0:16, 16:18, :], pp[0:16, 0:256].rearrange("p (a b) -> p a b", b=128), [ppb], vB[16:18])
                TT("vector", kd_tm[0:16, 16:18, :], ptv[0:16, 0:256].rearrange("p (a b) -> p a b", b=128),
                   KDs[0:16, hp, :].unsqueeze(1).broadcast_to([16, 2, 128]), ALU.mult, [ptb] + RCB, kdB[16:18])
                if _DBG_ENV.get("RET_STOP") == "B":
                    break
                P.tag = "ret.kv"
                kvps = []
                for ti in range(4):
                    pp, ppb = PS.next()
                    for j in range(4):
                        i = ti * 4 + j
                        MM(pp[:, j * 128:(j + 1) * 128], kd_tm[:, i, :], v_tm[:, i, :], True, True, [kdB[i], vB[i]], [ppb], inc=(j == 3))
                    kvps.append((pp, ppb))
                    for j in range(4):
                        i = ti * 4 + j
                        for h2 in range(2):
                            r = slice(64 * h2, 64 * h2 + 64)
                            STT(S_f[0][r, :], S_f[0][r, :], cdec[r, hp:hp + 1], pp[r, j * 128 + 64 * h2:j * 128 + 64 * h2 + 64], ALU.mult, ALU.add,
                                [ppb, SfB[0]] + RCB, [SfB[0]])
                        if i < 15:
                            CP("vector", S_bf[:, i + 1, :], S_f[0], [SfB[0]], [SbB[i + 1]])
                STORE(o_ret_p[l, 2 * hp:2 * hp + 2].rearrange("h k v -> (h k) v"), S_f[0], SfB[0])
                pp, ppb = PS.next()
                for s_ in range(2):
                    MM(pp[:, s_ * 128:(s_ + 1) * 128], kd_tm[0:16, 16 + s_, :], v_tm[0:16, 16 + s_, :], True, True, [kdB[16 + s_], vB[16 + s_]], [ppb], inc=(s_ == 1))
                for s_ in range(2):
                    for h2 in range(2):
                        r = slice(64 * h2, 64 * h2 + 64)
                        STT(S_f[1 + s_][r, :], S_f[1 + s_][r, :], cdecs[r, hp:hp + 1], pp[r, s_ * 128 + 64 * h2:s_ * 128 + 64 * h2 + 64], ALU.mult, ALU.add,
                            [ppb, SfB[1 + s_], SbB[16 + s_]] + RCB, [SfB[1 + s_]])
                    STORE(o_ret_s[l, s_, 2 * hp:2 * hp + 2].rearrange("h k v -> (h k) v"), S_f[1 + s_], SfB[1 + s_])
                if _DBG_ENV.get("RET_STOP") == "C":
                    break
                P.tag = "ret.attn"
                post_state = [None]
                for ti in range(5):
                    c0, n = TILES[ti]
                    po, pob = PSA.next()
                    pend = None
                    nunit = 0
                    nchunk = 4 if ti < 4 else 2
                    cl = 128 if ti < 4 else 16
                    for j in range(nchunk):
                        i = ti * 4 + j if ti < 4 else 16 + j
                        cc = slice(c0 + j * cl, c0 + (j + 1) * cl)
                        for h2 in range(2):
                            r = slice(64 * h2, 64 * h2 + 64)
                            hd = 2 * hp + h2
                            ps_, psb_ = PS.next()
                            MM(ps_[0:cl, 0:cl], kT[r, cc], qT[r, cc], True, True, [kB[ti], qB[ti]], [psb_])
                            at, atb = attp.next()
                            dm = DmT[0:cl, hd, 0:cl] if ti < 4 else DmTs[0:cl, hd, 0:cl]
                            TT("vector", at[0:cl, 0:cl], ps_[0:cl, 0:cl], dm, ALU.mult, [psb_] + RCB, [atb])
                            if pend is not None:
                                pend()
                            nunit += 1
                            if nunit == 3 and post_state[0] is not None:
                                post_state[0]()
                                post_state[0] = None

                            def pend(po=po, pob=pob, r=r, j=j, cl=cl, i=i, at=at, atb=atb, cc=cc, ti=ti):
                                MM(po[r, j * cl:(j + 1) * cl], v_tm[0:cl, i, r], at[0:cl, 0:cl], True, False, [vB[i], atb], [pob], inc=False)
                                MM(po[r, j * cl:(j + 1) * cl], S_bf[r, i, :], qdT[r, cc], False, True, [SbB[i], qdB[ti]], [pob])
                    pend()
                    pend = None
                    o_sb, o_sbb = t1p.next()
                    CP("scalar", o_sb[:, 0:n], po[:, 0:n], [pob], [o_sbb])
                    q_ap, q_b = sqp.next()
                    ACT(q_ap[:, 0:n], po[:, 0:n], AF.Square, [pob], [q_b])
                    p2, p2b = PS.next()
                    MM(p2[:, 0:n], BD, q_ap[:, 0:n], True, True, [q_b] + RCB, [p2b])

                    def post2(p2=p2, p2b=p2b, o_sb=o_sb, o_sbb=o_sbb, n=n, c0=c0, ti=ti, hp=hp):
                        r_ap, r_b = rsp.next()
                        ACT(r_ap[:, 0:n], p2[:, 0:n], AF.Ln, [p2b], [r_b], bias=EPS, scale=1.0 / 64)
                        ACT(r_ap[:, 0:n], r_ap[:, 0:n], AF.Exp, [r_b], [r_b], scale=-0.5)
                        a2, a2b = t2p.next()
                        STT(a2[:, 0:n], o_sb[:, 0:n], pcol(l, "ret_norm", hp), r_ap[:, 0:n], ALU.mult, ALU.mult, [o_sbb, r_b] + CONSTS, [a2b])
                        TT("vector", mixT[:, hp, c0:c0 + n], a2[:, 0:n], sgT[:, c0:c0 + n], ALU.mult, [a2b, sgB[ti]], [mixB[hp][ti]])
                    post_state[0] = post2
                if post_state[0] is not None:
                    post_state[0]()
                    post_state[0] = None
            if dbg == "ret":
                dump("retmix", mixT, [mixB[c][ti] for c in range(4) for ti in range(5)], BF16)
            xacc(l, 0, 4, mixT, mixB, "ret", preloaded=True)
            P.barrier()
            AR.reset(mR)

        def group_mla(l, hT, hB, hT_region):
            mR = AR.mark()
            mtab = AR.alloc(2 * T).rearrange("p (a t) -> p a t", a=2)
            mmat = AR.alloc(4 * 128).rearrange("p (a t) -> p a t", a=4)
            bMT, bMM = Buf("mtab"), Buf("mmat")
            LOAD("sync", mtab, c_mtab, bMT)
            LOAD("sync", mmat, c_mmat, bMM)
            MCB = [bMT, bMM]
            Cm, Sm = mtab[:, 0, :], mtab[:, 1, :]
            Pm96, ones96, mrow = mmat[:, 0, :], mmat[:, 1, :], mmat[:, 2:4, :]
            mixT = AR.alloc(4 * T).rearrange("p (c t) -> p c t", c=4)
            mixB = [[Buf("mm%d_%d" % (c, ti)) for ti in range(5)] for c in range(4)]
            cqnT = AR.alloc(3 * T).rearrange("p (c t) -> p c t", c=3)
            cqB = [Buf("cq%d" % ti) for ti in range(5)]
            ckvT = AR.alloc(2 * 2048).rearrange("p (c t) -> p c t", c=2)
            ckvB = [Buf("ckv%d" % i) for i in range(4)]
            ckvN = AR.alloc(2 * 32).rearrange("p (c t) -> p c t", c=2)
            ckvNB = Buf("ckvN")
            krT = AR.alloc(2048)
            krB = [Buf("kr%d" % i) for i in range(4)]
            krN = AR.alloc(32)
            krNB = Buf("krN")
            qT = AR.alloc(2048)
            qB = [Buf("mq%d" % i) for i in range(4)]
            kT = AR.alloc(2048)
            kB = [Buf("mk%d" % i) for i in range(4)]
            vA = AR.alloc(16 * 128).rearrange("p (i d) -> p i d", i=16)
            vB = [Buf("mv%d" % i) for i in range(16)]
            sqp = Pool([(AR.alloc(512), Buf("msq%d" % i)) for i in range(2)])
            rsp = Pool([(AR.alloc(512, F32), Buf("mrs%d" % i)) for i in range(1)])
            xnp = Pool([(AR.alloc(512, F32), Buf("mxn%d" % i)) for i in range(2)])
            kf_t1 = AR.alloc(512, F32)
            t1p = Pool([(kf_t1, Buf("mt1%d" % i)) for i in range(1)])
            t2p = Pool([(AR.alloc(512, F32), Buf("mt2%d" % i)) for i in range(1)])
            xbp = Pool([(AR.alloc(512), Buf("mxb%d" % i)) for i in range(2)])
            pbp = Pool([(AR.alloc(512), Buf("mpb%d" % i)) for i in range(2)])
            recp = Pool([(AR.alloc(512, F32), Buf("mrc%d" % i)) for i in range(1)])
            cfp = Pool([(AR.alloc(2 * 512, F32).rearrange("p (c t) -> p c t", c=2), Buf("mcf%d" % i)) for i in range(1)])
            kfp = Pool([(kf_t1, Buf("mkf%d" % i)) for i in range(1)])
            cstg = Pool([(AR.alloc(288, F32), Buf("mcs%d" % i)) for i in range(2)])
            ostg = Pool([(AR.alloc(288, F32), Buf("mos%d" % i)) for i in range(2)])
            MEMSET("vector", vA[:, :, 64:128], 1.0, vB)
            growK = AR.alloc(96, F32)
            bGR = Buf("growK")
            LOAD("sync", growK, prow[:, l, 32:128], bGR)
            krg = AR.alloc(64, F32)
            krgB = Buf("krg")
            ssrP = AR.alloc(16, F32)
            ssrPB = Buf("ssrP")
            ssrS = AR.alloc(2 * 9, F32).rearrange("p (s i) -> p s i", s=2)
            ssrSB = Buf("ssrS")
            ssq = AR.alloc(16 * 8, F32).rearrange("p (i h) -> p i h", i=16)
            ksc = AR.alloc(16 * 8, F32).rearrange("p (i h) -> p i h", i=16)
            ssqB, kscB = Buf("ssq"), Buf("ksc")
            krRB = [Buf("krR0"), Buf("krR1")]

            P.tag = "mla.M1"
            srcw = w_in[l].rearrange("(k p) n -> p k n", p=128)
            wa, wab = WA[0]
            wb, wbb = WA[1]
            wav = wa[:, 0:3072].rearrange("p (k n) -> p k n", k=8)
            wbv = wb[:, 0:2304].rearrange("p (k n) -> p k n", k=8)
            LOAD("gpsimd", wav, srcw[:, :, MLA0:MLA0 + 384], wab)
            LOAD("gpsimd", wbv, srcw[:, :, MLA0 + 384:MLA0 + 672], wbb)
            for ti in range(5):
                c0, n = TILES[ti]
                cs = slice(c0, c0 + n)
                pcs = []
                for c in range(3):
                    pp, ppb = PS.next()
                    for k in range(8):
                        MM(pp[:, 0:n], wav[:, k, c * 128:(c + 1) * 128], hT[:, k, cs], k == 0, k == 7, [wab, hB[k][ti]], [ppb], inc=(k == 7))
                    pcs.append((pp, ppb))
                rms_norm_fm([(pp[:, 0:n], [ppb]) for pp, ppb in pcs], 3, lambda c: pcol(l, "mla_q_norm", c),
                            lambda c, ti=ti, cs=cs: (cqnT[:, c, cs], [cqB[ti]]), n, 1.0 / 384, (sqp, rsp))
                pcs = []
                for c in range(2):
                    pp, ppb = PS.next()
                    for k in range(8):
                        MM(pp[:, 0:n], wbv[:, k, c * 128:(c + 1) * 128], hT[:, k, cs], k == 0, k == 7, [wbb, hB[k][ti]], [ppb], inc=(k == 7))
                    pcs.append((pp, ppb))
                cf, cfb = cfp.next()
                rms_norm_fm([(pp[:, 0:n], [ppb]) for pp, ppb in pcs], 2, lambda c: pcol(l, "mla_kv_norm", c),
                            lambda c, cf=cf, cfb=cfb, n=n: (cf[:, c, 0:n], [cfb]), n, 1.0 / 256, (sqp, rsp))
                if ti < 4:
                    CP("scalar", ckvT[:, :, cs], cf[:, :, 0:n], [cfb], [ckvB[ti]])
                else:
                    CP("scalar", ckvN[:, :, 0:n], cf[:, :, 0:n], [cfb], [ckvNB])
                pp, ppb = PS.next()
                for k in range(8):
                    MM(pp[0:32, 0:n], wbv[:, k, 256:288], hT[:, k, cs], k == 0, k == 7, [wbb, hB[k][ti]], [ppb], inc=(k == 7))
                kf, kfb = kfp.next()
                CP("vector", kf[0:32, 0:n], pp[0:32, 0:n], [ppb], [kfb])
                if ti < 4:
                    CP("scalar", krT[64:96, cs], pp[0:32, 0:n], [ppb], [krB[ti]])
                else:
                    CP("scalar", krN[64:96, 0:n], pp[0:32, 0:n], [ppb], [krNB])
                nb = 4 if ti < 4 else 1
                bw = 128 if ti < 4 else 32
                for j in range(nb):
                    pt, ptb = PS.next()
                    for c in range(2):
                        TR(pt[0:bw, c * 128:(c + 1) * 128], cf[:, c, j * bw:(j + 1) * bw], ident_f, [cfb] + CONSTS, [ptb], inc=False)
                    TR(pt[0:bw, 256:288], kf[0:32, j * bw:(j + 1) * bw], ident_f[0:32, 0:32], [kfb] + CONSTS, [ptb])
                    og, ogb = ostg.next()
                    CP("vector", og[0:bw, :], pt[0:bw, 0:288], [ptb], [ogb])
                    if ti < 4:
                        jt = ti * 4 + j
                        TT("vector", krg[:, 0:32], og[:, 256:288], growK[:, 64:96], ALU.mult, [ogb, bGR], [krgB])
                        P.op("scalar", lambda e, jt=jt: e.activation(out=krg[:, 32:64], in_=krg[:, 0:32], func=AF.Square, accum_out=ssrP[:, jt:jt + 1]),
                             reads=[krgB], writes=[krgB, ssrPB])
                    else:
                        for s2 in range(2):
                            pt2, pt2b = PS.next()
                            TR(pt2[0:16, 0:32], kf[0:32, 16 * s2:16 * s2 + 16], ident_f[0:32, 0:32], [kfb] + CONSTS, [pt2b])
                            TT("vector", krg[0:16, 0:32], pt2[0:16, 0:32], growK[0:16, 64:96], ALU.mult, [pt2b, bGR], [krgB])
                            P.op("scalar", lambda e, s2=s2: e.activation(out=krg[0:16, 32:64], in_=krg[0:16, 0:32], func=AF.Square, accum_out=ssrS[0:16, s2, 8:9]),
                                 reads=[krgB], writes=[krgB, ssrSB])
                    if ti < 4:
                        r0 = c0 + j * 128
                        STORE(o_ckv_p[l, r0:r0 + 128, :], og[:, 0:256], ogb)
                        STORE(o_kr_p[l, r0:r0 + 128, :], og[:, 256:288], ogb)
                    else:
                        STORE(o_ckv_s[l], og[0:32, 0:256], ogb)
                        STORE(o_kr_s[l], og[0:32, 256:288], ogb)

            P.barrier()
            AR2 = Arena(hT_region, 8 * T)
            qT2 = [qT, AR2.alloc(2048)]
            kT2 = [kT, AR2.alloc(2048)]
            vA2 = [vA, AR2.alloc(16 * 128).rearrange("p (i d) -> p i d", i=16)]
            qB2 = [qB, [Buf("mq1_%d" % i) for i in range(4)]]
            kB2 = [kB, [Buf("mk1_%d" % i) for i in range(4)]]
            vB2 = [vB, [Buf("mv1_%d" % i) for i in range(16)]]
            MEMSET("vector", vA2[1][:, :, 64:128], 1.0, vB2[1])
            sqp = Pool(sqp.slots + [(AR2.alloc(512), Buf("msq2%d" % i)) for i in range(1)])
            rsp = Pool(rsp.slots + [(AR2.alloc(512, F32), Buf("mrs2%d" % i)) for i in range(1)])
            xnp = Pool(xnp.slots + [(AR2.alloc(512, F32), Buf("mxn2%d" % i)) for i in range(1)])
            t1p = Pool([(AR2.alloc(512, F32), Buf("mt12%d" % i)) for i in range(2)])
            t2p = Pool(t2p.slots + [(AR2.alloc(512, F32), Buf("mt22%d" % i)) for i in range(1)])
            xbp = Pool(xbp.slots + [(AR2.alloc(512), Buf("mxb2%d" % i)) for i in range(1)])
            pbp = Pool(pbp.slots + [(AR2.alloc(512), Buf("mpb2%d" % i)) for i in range(2)])
            recp = Pool(recp.slots + [(AR2.alloc(512, F32), Buf("mrc2%d" % i)) for i in range(1)])
            P.tag = "mla.w"
            wqv = wa[:, 0:2304].rearrange("p (k n) -> p k n", k=3)
            wkv = wb[:, 0:2048].rearrange("p (k n) -> p k n", k=2)
            LOAD("gpsimd", wqv, w_uq[l].rearrange("(k p) n -> p k n", p=128), wab)
            LOAD("gpsimd", wkv, w_ukv[l].rearrange("(k p) n -> p k n", p=128), wbb)
            wkv4 = wkv.rearrange("p k (h t d) -> p k h t d", h=8, t=2)
            for c in range(2):
                TT("vector", wkv4[:, c, :, 0, :], wkv4[:, c, :, 0, :], growK[:, 0:64].unsqueeze(1).broadcast_to([128, 8, 64]), ALU.mult, [wbb, bGR], [wbb])

            def normrope(ps, psb, n, gname, pos0, dst, dstb, psp):
                q_ap, q_b = sqp.next()
                ACT(q_ap[0:96, 0:n], ps[0:96, 0:n], AF.Square, [psb], [q_b])
                yield
                p2, p2b = psp.next()
                MM(p2[0:96, 0:n], ones96[0:96, 0:96], q_ap[0:96, 0:n], True, True, [q_b] + MCB, [p2b])
                yield
                r_ap, r_b = rsp.next()
                ACT(r_ap[0:96, 0:n], p2[0:96, 0:n], AF.Ln, [p2b], [r_b], bias=EPS, scale=1.0 / 96)
                ACT(r_ap[0:96, 0:n], r_ap[0:96, 0:n], AF.Exp, [r_b], [r_b], scale=-0.5)
                yield
                xn, xnb = xnp.next()
                STT(xn[0:96, 0:n], ps[0:96, 0:n], pcol(l, gname, 0)[0:96], r_ap[0:96, 0:n], ALU.mult, ALU.mult, [psb, r_b] + CONSTS, [xnb])
                yield
                CP("vector", dst[0:64, 0:n], xn[0:64, 0:n], [xnb], dstb)
                xb, xbb = xbp.next()
                CP("vector", xb[64:96, 0:n], xn[64:96, 0:n], [xnb], [xbb])
                yield
                p3, p3b = psp.next()
                MM(p3[64:96, 0:n], Pm96[64:96, 64:96], xb[64:96, 0:n], True, True, [xbb] + MCB, [p3b])
                yield
                a1, a1b = t1p.next()
                a2, a2b = t2p.next()
                TT("vector", a1[64:96, 0:n], xn[64:96, 0:n], Cm[64:96, pos0:pos0 + n], ALU.mult, [xnb] + MCB, [a1b])
                TT("vector", a2[64:96, 0:n], p3[64:96, 0:n], Sm[64:96, pos0:pos0 + n], ALU.mult, [p3b] + MCB, [a2b])
                yield
                TT("vector", dst[64:96, 0:n], a1[64:96, 0:n], a2[64:96, 0:n], ALU.add, [a1b, a2b], dstb)

            SC = 96.0 ** -0.5
            for st_ in range(3):
                if st_ == 0:
                    ktiles = [(i * 512, 512, i * 512, i) for i in range(4)]
                    nk = 2048
                    qcol0, nq = 0, 2048
                else:
                    s_ = st_ - 1
                    P.tag = "mla.past"
                    for i in range(8):
                        cg, cgb = cstg.next()
                        LOAD("sync", cg[:, 0:256], c_ckv[l, s_, i * 128:(i + 1) * 128, :], cgb)
                        LOAD("sync", cg[:, 256:288], c_kr[l, s_, i * 128:(i + 1) * 128, :], cgb)
                        pt, ptb = PS.next()
                        for c in range(2):
                            TR(pt[:, c * 128:(c + 1) * 128], cg[:, c * 128:(c + 1) * 128], ident_f, [cgb] + CONSTS, [ptb], inc=False)
                        TR(pt[0:32, 256:384], cg[:, 256:288], ident_f, [cgb] + CONSTS, [ptb])
                        CP("scalar", ckvT[:, :, i * 128:(i + 1) * 128], pt[:, 0:256].rearrange("p (c t) -> p c t", c=2), [ptb], [ckvB[i // 4]])
                        CP("vector", krT[64:96, i * 128:(i + 1) * 128], pt[0:32, 256:384], [ptb], [krB[i // 4]])
                        TT("vector", krg[:, 0:32], cg[:, 256:288], growK[:, 64:96], ALU.mult, [cgb, bGR], [krgB])
                        P.op("scalar", lambda e, i=i, s_=s_: e.activation(out=krg[:, 32:64], in_=krg[:, 0:32], func=AF.Square, accum_out=ssrS[:, s_, i:i + 1]),
                             reads=[krgB], writes=[krgB, ssrSB])
                    CP("scalar", ckvT[:, :, 1024:1040], ckvN[:, :, 16 * s_:16 * s_ + 16], [ckvNB], [ckvB[2]])
                    CP("scalar", krT[64:96, 1024:1040], krN[64:96, 16 * s_:16 * s_ + 16], [krNB], [krB[2]])
                    ktiles = [(0, 512, 0, 0), (512, 512, 512, 1), (1024, 16, 2048, 2)]
                    nk = 1040
                    qcol0, nq = 2048 + 16 * s_, 16
                P.tag = "mla.kset"
                ntile = 16 if st_ == 0 else 9
                for jt in range(ntile):
                    kn = 128 if (st_ == 0 or jt < 8) else 16
                    pk, pkb = PS.next()
                    for c in range(2):
                        MM(pk[0:kn, :], ckvT[:, c, jt * 128:jt * 128 + kn], wkv4[:, c, :, 0, :], c == 0, c == 1, [wbb, ckvB[jt // 4]], [pkb], inc=(c == 1))
                    q_ap, q_b = xnp.next()
                    ACT(q_ap[0:kn, :], pk[0:kn, :], AF.Square, [pkb], [q_b])
                    P.op("vector", lambda e, q_ap=q_ap, kn=kn, jt=jt: e.tensor_reduce(out=ssq[0:kn, jt, :], in_=q_ap[0:kn, :].rearrange("p (h d) -> p h d", h=8),
                                                                                   axis=mybir.AxisListType.X, op=ALU.add), reads=[q_b], writes=[ssqB])
                parts = [(128, 0, 16)] if st_ == 0 else [(128, 0, 8), (16, 8, 9)]
                for (pr, t0_, t1_) in parts:
                    ssr_v = ssrP[0:pr, t0_:t1_] if st_ == 0 else ssrS[0:pr, st_ - 1, t0_:t1_]
                    kv_ = ksc[0:pr, t0_:t1_, :]
                    TT("vector", kv_, ssq[0:pr, t0_:t1_, :], ssr_v.unsqueeze(2).broadcast_to([pr, t1_ - t0_, 8]), ALU.add, [ssqB, ssrPB, ssrSB], [kscB])
                    ACT(kv_, kv_, AF.Ln, [kscB], [kscB], bias=EPS, scale=1.0 / 96)
                    ACT(kv_, kv_, AF.Exp, [kscB], [kscB], scale=-0.5, bias=math.log(SC))
                for (k0, n, pos0, bi) in ktiles:
                    xn, xnb = xnp.next()
                    TS("vector", xn[64:96, 0:n], krT[64:96, k0:k0 + n], pcol(l, "mla_k_gain", 0)[64:96], None, ALU.mult, ALU.bypass, [krB[bi]] + CONSTS, [xnb])
                    xb, xbb = xbp.next()
                    CP("scalar", xb[64:96, 0:n], xn[64:96, 0:n], [xnb], [xbb])
                    p3, p3b = PS.next()
                    MM(p3[64:96, 0:n], Pm96[64:96, 64:96], xb[64:96, 0:n], True, True, [xbb] + MCB, [p3b])
                    a1, a1b = t1p.next()
                    a2, a2b = t2p.next()
                    TT("vector", a1[64:96, 0:n], xn[64:96, 0:n], Cm[64:96, pos0:pos0 + n], ALU.mult, [xnb] + MCB, [a1b])
                    TT("vector", a2[64:96, 0:n], p3[64:96, 0:n], Sm[64:96, pos0:pos0 + n], ALU.mult, [p3b] + MCB, [a2b])
                    for b in range(2):
                        TT("vector", kT2[b][64:96, k0:k0 + n], a1[64:96, 0:n], a2[64:96, 0:n], ALU.add, [a1b, a2b], [krRB[b]])
                NFILL = int(_DBG_ENV.get("MK_FILL", "0"))
                PSP = Pool(psum[5:8] if NFILL == 0 else psum[5:7])
                fillB = Buf("fill")

                def filler():
                    for _ in range(NFILL):
                        MM(psum[7][0][:, 0:512], ident_b, cqnT[:, 0, 0:512], True, True, [], [fillB], inc=False)
                PSC = Pool(psum[2:5])
                def prep_items(h, st_=st_, ktiles=ktiles, qcol0=qcol0):
                    b = h % 2
                    qT_, kT_, vA_ = qT2[b], kT2[b], vA2[b]
                    items = []

                    def q_item(ti):
                        P.tag = "mla.q"
                        if st_ == 0:
                            c0 = ti * 512
                            pp, ppb = PSP.next()
                            for k in range(3):
                                MM(pp[0:96, 0:512], wqv[:, k, h * 96:(h + 1) * 96], cqnT[:, k, c0:c0 + 512], k == 0, k == 2, [wab, cqB[ti]], [ppb], inc=(k == 2))
                            yield
                            for _ in normrope(pp, ppb, 512, "mla_q_gain", c0, qT_[:, c0:c0 + 512], [qB2[b][ti]], PSP):
                                yield
                        else:
                            pp, ppb = PSP.next()
                            for k in range(3):
                                MM(pp[0:96, 0:16], wqv[:, k, h * 96:(h + 1) * 96], cqnT[:, k, qcol0:qcol0 + 16], k == 0, k == 2, [wab, cqB[4]], [ppb], inc=(k == 2))
                            yield
                            for _ in normrope(pp, ppb, 16, "mla_q_gain", 2048, qT_[:, 0:16], [qB2[b][0]], PSP):
                                yield

                    def k_item(kt):
                        P.tag = "mla.kv"
                        (k0, n, pos0, bi) = kt
                        pp, ppb = PSP.next()
                        for c in range(2):
                            MM(pp[0:64, 0:n], wkv4[:, c, h, 0, :], ckvT[:, c, k0:k0 + n], c == 0, c == 1, [wbb, ckvB[bi]], [ppb], inc=(c == 1))
                        yield
                        P.tag = "mla.kv"
                        CP("vector", kT_[0:64, k0:k0 + n], pp[0:64, 0:n], [ppb], [kB2[b][bi]])

                    def v_item(kt):
                        P.tag = "mla.kv"
                        (k0, n, pos0, bi) = kt
                        pv, pvb = PSP.next()
                        nt = (n + 127) // 128
                        for j in range(nt):
                            kn = min(128, n - j * 128)
                            for c in range(2):
                                MM(pv[0:kn, j * 64:(j + 1) * 64], ckvT[:, c, k0 + j * 128:k0 + j * 128 + kn], wkv4[:, c, h, 1, :], c == 0, c == 1,
                                   [wbb, ckvB[bi]], [pvb], inc=(c == 1 and j == nt - 1))
                        yield
                        P.tag = "mla.kv"
                        t0 = k0 // 128
                        kn0 = min(128, n)
                        CP("vector", vA_[0:kn0, t0:t0 + nt, 0:64], pv[0:kn0, 0:nt * 64].rearrange("p (a b) -> p a b", b=64), [pvb], vB2[b][t0:t0 + nt])
                    for ti in range(4 if st_ == 0 else 1):
                        items.append(q_item(ti))
                    for kt in ktiles:
                        items.append(k_item(kt))
                        items.append(v_item(kt))
                    return items

                def step(nxt):
                    while nxt:
                        try:
                            next(nxt[0])
                            return
                        except StopIteration:
                            nxt.pop(0)

                fin_state = {"f": None}

                def attention(h, nxt, st_=st_, qcol0=qcol0):
                    b = h % 2
                    qT_, kT_, vA_ = qT2[b], kT2[b], vA2[b]
                    it = 0
                    r = slice(64 * (h % 2), 64 * (h % 2) + 64)
                    if st_ == 0:
                        for g in range(4):
                            acc, accb = PSA.next()
                            last = 4 * g + 3
                            pendq = []
                            PV_DEPTH = int(_DBG_ENV.get("MK_PVD", "2"))
                            for j in range(last + 1):
                                P.tag = "mla.attn"
                                d_ = j - 4 * g
                                lo = 128 * d_ if d_ > 0 else 0
                                n = 512 - lo
                                sc, scb = PSC.next()
                                MM(sc[:, 0:n], kT_[0:96, j * 128:(j + 1) * 128], qT_[0:96, 512 * g + lo:512 * g + 512], True, d_ < 0,
                                   [kB2[b][j // 4], krRB[b], qB2[b][g]], [scb], inc=(d_ < 0))
                                if d_ >= 0:
                                    MM(sc[:, 0:128], mrow[0:1, 0, :], mrow[0:1, 1, :], False, True, MCB, [scb])
                                pb, pbb = pbp.next()
                                ACT(pb[:, 0:n], sc[:, 0:n], AF.Exp, [scb, kscB], [pbb], scale=ksc[:, j, h:h + 1])
                                pendq.append(lambda acc=acc, accb=accb, lo=lo, n=n, j=j, pb=pb, pbb=pbb, last=last:
                                             MM(acc[:, lo:512], vA_[:, j, :], pb[:, 0:n], j == 0, j == last, [vB2[b][j], pbb], [accb], inc=(j == last)))
                                if len(pendq) > PV_DEPTH:
                                    pendq.pop(0)()
                                it += 1
                                step(nxt)
                                if it % 2 == 0:
                                    step(nxt)
                            while pendq:
                                pendq.pop(0)()

                            def fin(acc=acc, accb=accb, r=r, h=h, g=g):
                                P.tag = "mla.attn"
                                rc_, rcb = recp.next()
                                ACT(rc_[64:128, :], acc[64:128, :], AF.Ln, [accb], [rcb])
                                ACT(rc_[64:128, :], rc_[64:128, :], AF.Exp, [rcb], [rcb], scale=-1.0)
                                TT("vector", mixT[r, h // 2, 512 * g:512 * g + 512], acc[0:64, :], rc_[64:128, :], ALU.mult, [accb, rcb], [mixB[h // 2][g]])
                            fin()
                    else:
                        acc, accb = PSA.next()
                        pend = None
                        for j in range(9):
                            P.tag = "mla.attn"
                            kn = 128 if j < 8 else 16
                            sc, scb = PSC.next()
                            MM(sc[0:kn, 0:16], kT_[0:96, j * 128:j * 128 + kn], qT_[0:96, 0:16], True, True, [kB2[b][j // 4], krRB[b], qB2[b][0]], [scb])
                            pb, pbb = pbp.next()
                            ACT(pb[0:kn, 0:16], sc[0:kn, 0:16], AF.Exp, [scb, kscB], [pbb], scale=ksc[0:kn, j, h:h + 1])
                            if pend is not None:
                                pend()
                            pend = (lambda acc=acc, accb=accb, kn=kn, j=j, pb=pb, pbb=pbb:
                                    MM(acc[:, 0:16], vA_[0:kn, j, :], pb[0:kn, 0:16], j == 0, j == 8, [vB2[b][j], pbb], [accb], inc=(j == 8)))
                            step(nxt)
                            step(nxt)
                        pend()
                        P.tag = "mla.attn"
                        rc_, rcb = recp.next()
                        ACT(rc_[64:128, 0:16], acc[64:128, 0:16], AF.Ln, [accb], [rcb])
                        ACT(rc_[64:128, 0:16], rc_[64:128, 0:16], AF.Exp, [rcb], [rcb], scale=-1.0)
                        TT("vector", mixT[r, h // 2, qcol0:qcol0 + 16], acc[0:64, 0:16], rc_[64:128, 0:16], ALU.mult, [accb, rcb], [mixB[h // 2][4]])
                    while nxt:
                        step(nxt)

                for g_ in prep_items(0):
                    for _ in g_:
                        pass
                for h in range(8):
                    attention(h, prep_items(h + 1) if h < 7 else [])
                if fin_state["f"] is not None:
                    fin_state["f"]()
                    fin_state["f"] = None
            if dbg == "mla":
                dump("mlamix", mixT, [mixB[c][ti] for c in range(4) for ti in range(5)], BF16)
            xacc(l, 512, 4, mixT, mixB, "mla")
            P.barrier()
            AR.reset(mR)

        def group_ssm(l, hT, hB):
            mR = AR.mark()
            U = AR.alloc(128, F32)
            NEG = AR.alloc(640)
            ones_f = AR.alloc(128, F32)
            MEMSET("vector", ones_f, 1.0, [Buf("onesf")])
            prow_sb = AR.alloc(128, F32)
            bSC = Buf("ssmc")
            LOAD("sync", U, c_U, bSC)
            LOAD("sync", NEG, c_NEG, bSC)
            LOAD("sync", prow_sb, prow[:, l, :], bSC)
            SCB = [bSC]
            dtv = AR.alloc(18 * 16, F32).rearrange("p (i h) -> p i h", i=18)
            dtA = AR.alloc(18 * 16, F32).rearrange("p (i h) -> p i h", i=18)
            nacum = AR.alloc(18 * 16, F32).rearrange("p (i h) -> p i h", i=18)
            ea = AR.alloc(16, F32)
            bDT = Buf("dt")
            convout = AR.alloc(3 * 12 * 3, F32).rearrange("p (q c r) -> p q c r", q=3, c=12)
            bCO = Buf("convout")
            srcw = w_in[l].rearrange("(k p) n -> p k n", p=128)

            P.tag = "ssm.dt"
            wa, wab = WA[0]
            wdt = wa[:, 0:128].rearrange("p (k n) -> p k n", k=8)
            LOAD("gpsimd", wdt, srcw[:, :, SSM0 + 2560:SSM0 + 2576], wab)
            pd, pdb = PSA.next()
            for i in range(18):
                if i < 16:
                    cols, rows, ti = slice(i * 128, (i + 1) * 128), 128, i // 4
                else:
                    cols, rows, ti = slice(2048 + 16 * (i - 16), 2064 + 16 * (i - 16)), 16, 4
                for k in range(8):
                    MM(pd[0:rows, i * 16:(i + 1) * 16], hT[:, k, cols], wdt[:, k, :], k == 0, k == 7, [wab, hB[k][ti]], [pdb], inc=(k == 7 and i == 17))
            pdv = pd[:, 0:288].rearrange("p (i h) -> p i h", i=18)
            ACT(ea, prow_sb[:, 16:32], AF.Exp, SCB, [bDT])
            for (pr, sl_, ns) in ((128, slice(0, 16), 16), (16, slice(16, 18), 2)):
                TT("vector", dtv[0:pr, sl_, :], pdv[0:pr, sl_, :], prow_sb[0:pr, 0:16].unsqueeze(1).broadcast_to([pr, ns, 16]), ALU.add, [pdb] + SCB, [bDT])
                ACT(dtv[0:pr, sl_, :], dtv[0:pr, sl_, :], AF.Exp, [bDT], [bDT])
                ACT(dtv[0:pr, sl_, :], dtv[0:pr, sl_, :], AF.Ln, [bDT], [bDT], bias=1.0)
                STT(dtA[0:pr, sl_, :], dtv[0:pr, sl_, :], -1.0, ea[0:pr].unsqueeze(1).broadcast_to([pr, ns, 16]), ALU.mult, ALU.mult, [bDT], [bDT])
            pc_, pcb = PS.next()
            MM(pc_[:, 0:256], U, dtA[:, 0:16, :].rearrange("p i h -> p (i h)"), True, True, [bDT] + SCB, [pcb])
            MM(pc_[0:16, 256:288], U[0:16, 0:16], dtA[0:16, 16:18, :].rearrange("p i h -> p (i h)"), True, True, [bDT] + SCB, [pcb])
            P.op("scalar", lambda e: e.activation(out=nacum[:, 0:16, :].rearrange("p i h -> p (i h)"), in_=pc_[:, 0:256], func=AF.Copy, scale=-1.0), reads=[pcb], writes=[bDT])
            P.op("scalar", lambda e: e.activation(out=nacum[0:16, 16:18, :].rearrange("p i h -> p (i h)"), in_=pc_[0:16, 256:288], func=AF.Copy, scale=-1.0), reads=[pcb], writes=[bDT])

            mG = AR.mark()
            for g in range(2):
                AR.reset(mG)
                xsT = AR.alloc(4 * T).rearrange("p (c t) -> p c t", c=4)
                zsT = AR.alloc(4 * T).rearrange("p (c t) -> p c t", c=4)
                BT, CT = AR.alloc(T), AR.alloc(T)
                xsB = [[Buf("xs%d_%d" % (c, ti)) for ti in range(5)] for c in range(4)]
                zB = [[Buf("zs%d_%d" % (c, ti)) for ti in range(5)] for c in range(4)]
                BB = [Buf("B%d" % ti) for ti in range(5)]
                CB = [Buf("C%d" % ti) for ti in range(5)]
                B_tm = AR.alloc(18 * 128).rearrange("p (i d) -> p i d", i=18)
                BtB = [Buf("Bt%d" % i) for i in range(18)]
                mPh = AR.mark()
                NR = 4
                Rp = [(AR.alloc(520, F32), Buf("R%d" % i)) for i in range(NR)]
                accp = Pool([(AR.alloc(512, F32), Buf("ca%d" % i)) for i in range(4)])
                P.tag = "ssm.z"
                wz, wzb = WA[1]
                wzv = wz.rearrange("p (k n) -> p k n", k=8)
                LOAD("gpsimd", wzv, srcw[:, :, SSM0 + 512 * g:SSM0 + 512 * g + 512], wzb)
                zunits = []
                for c in range(4):
                    for ti in range(5):
                        def zunit(c=c, ti=ti):
                            P.tag = "ssm.z"
                            c0, n = TILES[ti]
                            pp, ppb = PS.next()
                            for k in range(8):
                                MM(pp[:, 0:n], wzv[:, k, c * 128:(c + 1) * 128], hT[:, k, c0:c0 + n], k == 0, k == 7, [wzb, hB[k][ti]], [ppb], inc=(k == 7))
                            ACT(zsT[:, c, c0:c0 + n], pp[:, 0:n], AF.Silu, [ppb], [zB[c][ti]])
                            P.tag = "ssm.conv"
                        zunits.append(zunit)
                P.tag = "ssm.conv"
                wx, wxb = WA[0]
                wxv = wx.rearrange("p (k n) -> p k n", k=8)
                LOAD("gpsimd", wxv, srcw[:, :, SSM0 + 1024 + 512 * g:SSM0 + 1024 + 512 * g + 512], wxb)
                wbc, wbcb = WA[1]
                wbcv = wbc[:, 0:2048].rearrange("p (k n) -> p k n", k=8)
                ri = 0
                conv_pend = [None]
                for q in range(6):
                    if q == 4:
                        while zunits:
                            zunits.pop(0)()
                        LOAD("gpsimd", wbcv[:, :, 0:128], srcw[:, :, SSM0 + 2048 + 128 * g:SSM0 + 2048 + 128 * g + 128], wbcb)
                        LOAD("gpsimd", wbcv[:, :, 128:256], srcw[:, :, SSM0 + 2304 + 128 * g:SSM0 + 2304 + 128 * g + 128], wbcb)
                    if q < 4:
                        lw = lambda k, q=q: wxv[:, k, q * 128:(q + 1) * 128]
                        lwb = wxb
                        cc = 4 * g + q
                        dstf = lambda cs, q=q: xsT[:, q, cs]
                        dB = xsB[q]
                    else:
                        lw = lambda k, q=q: wbcv[:, k, (q - 4) * 128:(q - 3) * 128]
                        lwb = wbcb
                        cc = 8 + 2 * (q - 4) + g
                        dstf = (lambda cs: BT[:, cs]) if q == 4 else (lambda cs: CT[:, cs])
                        dB = BB if q == 4 else CB
                    segs = [(ti, TILES[ti][0], 512, 0) for ti in range(4)] + [(4, 2048, 16, 1), (4, 2064, 16, 2)]
                    for (ti, c0, n, sq_) in segs:
                        pp, ppb = PS.next()
                        for k in range(8):
                            MM(pp[:, 0:n], lw(k), hT[:, k, c0:c0 + n], k == 0, k == 7, [lwb, hB[k][ti]], [ppb], inc=(k == 7))
                        R, Rb = Rp[ri % NR]
                        Rprev, Rpb = Rp[(ri - 1) % NR]
                        ri += 1
                        if sq_ == 0 and ti == 0:
                            MEMSET("vector", R[:, 0:3], 0.0, [Rb])
                        elif sq_ == 0:
                            CP("vector", R[:, 0:3], Rprev[:, 512:515], [Rpb], [Rb])
                        else:
                            LOAD("sync", R[:, 0:3], c_conv[l, sq_ - 1].rearrange("r c -> c r")[cc * 128:(cc + 1) * 128, :], Rb, slow=True)
                        CP("scalar", R[:, 3:3 + n], pp[:, 0:n], [ppb], [Rb])
                        if (sq_ == 0 and ti == 3) or sq_ > 0:
                            CP("vector", convout[:, sq_, cc, :], R[:, n:n + 3], [Rb], [bCO])
                        a_, ab = accp.next()
                        ACT(a_[:, 0:n], pp[:, 0:n], AF.Copy, [ppb] + CONSTS, [ab], scale=pcol(l, "conv_w", cc * 4 + 3))
                        for w_ in range(0, 3):
                            STT(a_[:, 0:n], R[:, w_:w_ + n], pcol(l, "conv_w", cc * 4 + w_), a_[:, 0:n], ALU.mult, ALU.add, [Rb, ab] + CONSTS, [ab])
                        if conv_pend[0] is not None:
                            conv_pend[0]()
                        conv_pend[0] = (lambda d_=dstf(slice(c0, c0 + n)), a_=a_, n=n, ab=ab, db_=dB[ti], cc=cc:
                                        ACT(d_, a_[:, 0:n], AF.Silu, [ab] + CONSTS, [db_], bias=pcol(l, "conv_b", cc)))
                        if zunits:
                            zunits.pop(0)()
                conv_pend[0]()
                conv_pend[0] = None
                xacc_prefetch(l, 1024 + 512 * g)
                P.tag = "ssm.Btm"
                for ti in range(4):
                    pt, ptb = PS.next()
                    ptv = pt.bitcast(BF16)
                    for j in range(4):
                        i = ti * 4 + j
                        TR(ptv[:, j * 128:(j + 1) * 128], BT[:, i * 128:(i + 1) * 128], ident_b, [BB[ti]] + CONSTS, [ptb], inc=(j == 3))
                    CP("scalar", B_tm[:, ti * 4:ti * 4 + 4, :], ptv[:, 0:512].rearrange("p (a b) -> p a b", b=128), [ptb], BtB[ti * 4:ti * 4 + 4])
                pt, ptb = PS.next()
                ptv = pt.bitcast(BF16)
                for s_ in range(2):
                    TR(ptv[0:16, s_ * 128:(s_ + 1) * 128], BT[:, 2048 + 16 * s_:2064 + 16 * s_], ident_b, [BB[4]] + CONSTS, [ptb], inc=(s_ == 1))
                CP("scalar", B_tm[0:16, 16:18, :], ptv[0:16, 0:256].rearrange("p (a b) -> p a b", b=128), [ptb], BtB[16:18])

                if dbg == "ssm" and _DBG_ENV.get("SSM_DUMP"):
                    dump("xs%d" % g, xsT, [xsB[c][ti] for c in range(4) for ti in range(5)], BF16)
                    dump("zs%d" % g, zsT, [zB[c][ti] for c in range(4) for ti in range(5)], BF16)
                    dump("B%d" % g, BT, BB, BF16)
                    dump("C%d" % g, CT, CB, BF16)
                P.tag = "ssm.scan"
                P.barrier()
                AR.reset(mPh)
                xdtp = Pool([(AR.alloc(512).rearrange("p (h d) -> p h d", h=8), Buf("xdt%d" % i)) for i in range(2)])
                xwp = Pool([(AR.alloc(512).rearrange("p (h d) -> p h d", h=8), Buf("xw%d" % i)) for i in range(2)])
                UdA = AR.alloc(1024, F32)
                UdAB = Buf("UdA")
                Ea = AR.alloc(1024, F32)
                EaB = Buf("Ea")
                Da = AR.alloc(1024, F32)
                DaB = Buf("Da")
                scp = Pool([(AR.alloc(1024), Buf("sc%d" % i)) for i in range(2)])
                cdp = Pool([(AR.alloc(1024), Buf("cd%d" % i)) for i in range(2)])
                hs = AR.alloc(512, F32).rearrange("p (h d) -> p h d", h=8)
                hsB = Buf("hs")
                hsbf = [(AR.alloc(512), Buf("hsbf%d" % i)) for i in range(2)]
                ydp = Pool([(AR.alloc(512, F32), Buf("yd%d" % i)) for i in range(1)])
                hstg = Pool([(AR.alloc(128, F32), Buf("hst%d" % i)) for i in range(2)])
                PSB = Pool(psum[2:5])
                PSD = Pool(psum[5:8])

                def state_out(dst_ap):
                    for c in range(4):
                        pt, ptb = PSB.next()
                        TR(pt[:, 0:128], hs[:, 2 * c:2 * c + 2, :].rearrange("p h d -> p (h d)"), ident_f, [hsB] + CONSTS, [ptb])
                        sg_, sgb = hstg.next()
                        CP("scalar", sg_, pt[:, 0:128], [ptb], [sgb])
                        STORE(dst_ap[2 * c:2 * c + 2].rearrange("h p n -> (h p) n"), sg_, sgb)

                Eap = Pool([(Ea, EaB), (AR.alloc(1024, F32), Buf("Ea1"))])
                hstate = {"cur": 0}

                def phaseA(i):
                    if i < 16:
                        cl, cols, ti = 128, slice(i * 128, (i + 1) * 128), i // 4
                    else:
                        cl, cols, ti = 16, slice(2048 + 16 * (i - 16), 2064 + 16 * (i - 16)), 4
                    W = 8 * cl
                    gb, gbb = PSA.next()
                    MM(gb[0:cl, 0:cl], BT[:, cols], CT[:, cols], True, True, [BB[ti], CB[ti]], [gbb])
                    pt, ptb = PSB.next()
                    ptv = pt.bitcast(BF16)
                    for c in range(4):
                        TR(ptv[0:cl, c * 128:(c + 1) * 128], xsT[:, c, cols], ident_b, [xsB[c][ti]] + CONSTS, [ptb], inc=(c == 3))
                    xdt, xdtb = xdtp.next()
                    TT("vector", xdt[0:cl], ptv[0:cl, 0:512].rearrange("p (h d) -> p h d", h=8),
                       dtv[0:cl, i, 8 * g:8 * g + 8].unsqueeze(2).broadcast_to([cl, 8, 64]), ALU.mult, [ptb, bDT], [xdtb])
                    nb = 2 if cl == 128 else 1
                    bw = W // nb
                    bcs = [PSD.next() for _ in range(nb)]
                    for h in range(8):
                        bc, bcb = bcs[(h * cl) // bw]
                        o_ = (h * cl) % bw
                        MM(bc[:, o_:o_ + cl], dtA[0:cl, i, 8 * g + h:8 * g + h + 1].broadcast_to([cl, 128]), U[0:cl, 0:cl], o_ == 0, True, [bDT] + SCB, [bcb],
                           inc=(h % (8 // nb) == (8 // nb) - 1), sgc=True)
                    Ea_, EaB_ = Eap.next()
                    for hb_, (bc, bcb) in enumerate(bcs):
                        ACT(Ea_[:, hb_ * bw:(hb_ + 1) * bw], bc[:, 0:bw], AF.Exp, [bcb], [EaB_])
                    negt = NEG[0:cl, 0:512] if cl == 128 else NEG[0:cl, 512:640]
                    for hb_, (bc, bcb) in enumerate(bcs):
                        MM(bc[0:cl, 0:bw], ident_b[0:cl, 0:cl], negt, False, True, SCB + CONSTS + [EaB_], [bcb], sgc=True)
                    Da3 = Da[0:cl, 0:W].rearrange("p (h t) -> p h t", h=8)
                    Ea3 = Ea_[:, 0:W].rearrange("p (h t) -> p h t", h=8)
                    for h in range(8):
                        bc, bcb = bcs[(h * cl) // bw]
                        o_ = (h * cl) % bw
                        ACT(Da3[:, h, :], bc[0:cl, o_:o_ + cl], AF.Exp, [bcb, bDT], [DaB], bias=nacum[0:cl, i, 8 * g + h:8 * g + h + 1])
                    return dict(cl=cl, cols=cols, ti=ti, W=W, xdt=xdt, xdtb=xdtb, Ea3=Ea3, EaB=EaB_, Da3=Da3, gb=gb, gbb=gbb)

                def phaseA2(cx):
                    cl, cols, ti, W = cx["cl"], cx["cols"], cx["ti"], cx["W"]
                    sc_, scb_ = scp.next()
                    sc3 = sc_[0:cl, 0:W].rearrange("p (h t) -> p h t", h=8)
                    TT("vector", sc3, cx["gb"][0:cl, 0:cl].unsqueeze(1).broadcast_to([cl, 8, cl]), cx["Da3"], ALU.mult, [cx["gbb"], DaB], [scb_])
                    cd_, cdb_ = cdp.next()
                    cd3 = cd_[:, 0:W].rearrange("p (h t) -> p h t", h=8)
                    TT("vector", cd3, CT[:, cols].unsqueeze(1).broadcast_to([128, 8, cl]), cx["Ea3"], ALU.mult, [CB[ti], cx["EaB"]], [cdb_])
                    xw, xwb = xwp.next()
                    TT("vector", xw[0:cl], cx["xdt"][0:cl], cx["Da3"][:, :, cl - 1:cl].broadcast_to([cl, 8, 64]), ALU.mult, [cx["xdtb"], DaB], [xwb])
                    cx.update(sc3=sc3, scb=scb_, cd3=cd3, cdb=cdb_, xw=xw, xwb=xwb)

                def phaseB(i, cx):
                    cl, cols, ti = cx["cl"], cx["cols"], cx["ti"]
                    if i == 0:
                        MEMSET("vector", hs, 0.0, [hsB])
                        MEMSET("vector", hsbf[hstate["cur"]][0], 0.0, [hsbf[hstate["cur"]][1]])
                    if i >= 16:
                        for c in range(4):
                            sg_, sgb = hstg.next()
                            LOAD("sync", sg_, c_ssm[l, i - 16, 8 * g + 2 * c:8 * g + 2 * c + 2].rearrange("h p n -> (h p) n"), sgb)
                            pt, ptb = PSB.next()
                            TR(pt[:, 0:128], sg_, ident_f, [sgb] + CONSTS, [ptb])
                            CP("vector", hs[:, 2 * c:2 * c + 2, :].rearrange("p h d -> p (h d)"), pt[:, 0:128], [ptb], [hsB])
                        hstate["cur"] ^= 1
                        CP("scalar", hsbf[hstate["cur"]][0], hs.rearrange("p h d -> p (h d)"), [hsB], [hsbf[hstate["cur"]][1]])
                    hb_ap, hb_b = hsbf[hstate["cur"]]
                    yb, ybb = PSA.next()
                    for h in range(8):
                        r = slice(64 * (h % 2), 64 * (h % 2) + 64)
                        yc = slice((h // 2) * cl, (h // 2 + 1) * cl)
                        MM(yb[r, yc], cx["xdt"][0:cl, h, :], cx["sc3"][:, h, :], True, False, [cx["xdtb"], cx["scb"]], [ybb], inc=False)
                        MM(yb[r, yc], hb_ap[:, h * 64:(h + 1) * 64], cx["cd3"][:, h, :], False, True, [hb_b, cx["cdb"]], [ybb], inc=(h == 7))
                    kv, kvb = PSB.next()
                    MM(kv[:, :], B_tm[0:cl, i, :], cx["xw"][0:cl].rearrange("p h d -> p (h d)"), True, True, [BtB[i], cx["xwb"]], [kvb])
                    TT("vector", hs, hs, cx["Ea3"][:, :, cl - 1:cl].broadcast_to([128, 8, 64]), ALU.mult, [hsB, cx["EaB"]], [hsB])
                    TT("vector", hs, hs, kv[:, :].rearrange("p (h d) -> p h d", h=8), ALU.add, [hsB, kvb], [hsB])
                    if i < 15:
                        hstate["cur"] ^= 1
                        CP("scalar", hsbf[hstate["cur"]][0], hs.rearrange("p h d -> p (h d)"), [hsB], [hsbf[hstate["cur"]][1]])
                    if i == 15:
                        state_out(o_ssm_p[l, 8 * g:8 * g + 8])
                    if i >= 16:
                        state_out(o_ssm_s[l, i - 16, 8 * g:8 * g + 8])
                    yd, ydb = ydp.next()
                    for c in range(4):
                        STT(yd[:, c * cl:(c + 1) * cl], xsT[:, c, cols], pcol(l, "ssm_d", 4 * g + c), yb[:, c * cl:(c + 1) * cl], ALU.mult, ALU.add,
                            [xsB[c][ti], ybb] + CONSTS, [ydb])
                    TT("gpsimd", zsT[:, :, cols], zsT[:, :, cols], yd[:, 0:4 * cl].rearrange("p (c t) -> p c t", c=4), ALU.mult,
                       [ydb] + [zB[c][ti] for c in range(4)], [zB[c][ti] for c in range(4)])

                cxs = {0: phaseA(0)}
                phaseA2(cxs[0])
                for i in range(18):
                    if i + 1 < 18:
                        cxs[i + 1] = phaseA(i + 1)
                    phaseB(i, cxs.pop(i))
                    if i + 1 < 18:
                        phaseA2(cxs[i + 1])
                P.barrier()
                AR.reset(mPh)
                sqp = Pool([(AR.alloc(512), Buf("ssq%d" % i)) for i in range(2)])
                rsp = Pool([(AR.alloc(512, F32), Buf("srs%d" % i)) for i in range(2)])
                P.tag = "ssm.norm"
                dq = []
                for ti in range(5):
                    c0, n = TILES[ti]
                    rms_norm_fm([(zsT[:, c, c0:c0 + n], [zB[c][ti]]) for c in range(4)], 4, lambda c, g=g: pcol(l, "ssm_norm", 4 * g + c),
                                lambda c, ti=ti, c0=c0, n=n: (zsT[:, c, c0:c0 + n], [zB[c][ti]]), n, 1.0 / 512, (sqp, rsp), dq=dq)
                while dq:
                    dq.pop()()
                if dbg == "ssm":
                    dump("ssmmix%d" % g, zsT, [zB[c][ti] for c in range(4) for ti in range(5)], BF16)
                xacc(l, 1024 + 512 * g, 4, zsT, zB, "ssm", preloaded=True)
                P.barrier()
            for cc in range(12):
                STORE(o_conv_p[l][:, cc * 128:(cc + 1) * 128].rearrange("r p -> p r"), convout[:, 0, cc, :], bCO, slow=True)
                for s_ in range(2):
                    STORE(o_conv_s[l, s_][:, cc * 128:(cc + 1) * 128].rearrange("r p -> p r"), convout[:, 1 + s_, cc, :], bCO, slow=True)
            P.barrier()
            AR.reset(mR)

        NL = 2
        plan = dbg or "full"
        for l in range(NL):
            ffn(l, "ffn1")
            if plan == "ffn1":
                break
            mixer(l)
            if plan in ("ret", "mla", "ssm", "mix"):
                break
            ffn(l, "ffn2")

        m0 = AR.mark()
        ostg = [(AR.alloc(1024, F32), Buf("ostg%d" % i)) for i in range(2)]
        for i in range(17):
            s_ap, s_b = ostg[i % 2]
            if i < 16:
                rows, c0, ti = 128, i * 128, i // 4
            else:
                rows, c0, ti = 32, 2048, 4
            for hb in range(2):
                p_ap, p_b = PS.next()
                for cc in range(4):
                    c = hb * 4 + cc
                    P.op("tensor", lambda e, p_ap=p_ap, c=c, cc=cc, rows=rows, c0=c0: e.transpose(
                        out=p_ap[0:rows, cc * 128:(cc + 1) * 128], in_=xT[:, c, c0:c0 + rows], identity=ident_f),
                        reads=[xB[c][ti]] + CONSTS, writes=[p_b], inc=(cc == 3))
                if hb == 0:
                    P.op("scalar", lambda e, p_ap=p_ap, s_ap=s_ap, rows=rows: e.activation(out=s_ap[0:rows, 0:512], in_=p_ap[0:rows, :], func=AF.Copy),
                         reads=[p_b], writes=[s_b])
                else:
                    P.op("vector", lambda e, p_ap=p_ap, s_ap=s_ap, rows=rows: e.tensor_copy(out=s_ap[0:rows, 512:1024], in_=p_ap[0:rows, :]),
                         reads=[p_b], writes=[s_b])
            ob = Buf("out%d" % i)
            if i < 16:
                P.dma("sync", lambda e, s_ap=s_ap, i=i: e.dma_start(out=yp[i * 128:(i + 1) * 128, :], in_=s_ap), reads=[s_b], writes=[ob], sembuf=s_b)
            else:
                P.dma("sync", lambda e, s_ap=s_ap: e.dma_start(out=ys, in_=s_ap[0:32, :]), reads=[s_b], writes=[ob], sembuf=s_b)
            outbufs.append(ob)
        AR.reset(m0)

        P.wait_all("sync", outbufs)
        P.emit(nc, st)
    return nc


_NC_CACHE = {}


def make_in_maps(inp):
    ident, cb = const_tables()
    pcs = np.stack([pack_pcols(inp, l) for l in range(2)], axis=1)
    c_bf = np.stack([cb["ident_b"], cb["ones_b"]], axis=1)
    rtab, rmat, rc = ret_tables()
    mtab, mmat = mla_tables()
    U_np, NEG_np = ssm_tables()
    prow_np = np.ascontiguousarray(np.broadcast_to(
        np.concatenate([inp["ssm_dt_bias"], inp["ssm_a_log"], inp["mla_k_gain"]], axis=1)[None], (128, 2, 128))).astype(np.float32)
    maps = []
    for c in range(NCORES):
        m = {
            "xp": np.ascontiguousarray(inp["x_prompt"][c]),
            "xs": np.ascontiguousarray(inp["x_sample"][2 * c:2 * c + 2].reshape(32, D)),
            "pcols": pcs, "c_ident": ident, "c_bf": c_bf,
            "w_in": inp["w_in"], "w_out": inp["w_out"],
            "st_ret": np.ascontiguousarray(inp["state_ret"][:, 2 * c:2 * c + 2]),
            "c_rtab": rtab, "c_rmat": rmat, "c_rc": rc,
            "w_uq": inp["mla_w_uq"], "w_ukv": inp["mla_w_ukv"],
            "c_ckv": np.ascontiguousarray(inp["cache_mla_ckv"][:, 2 * c:2 * c + 2]),
            "c_kr": np.ascontiguousarray(inp["cache_mla_krope"][:, 2 * c:2 * c + 2]),
            "c_mtab": mtab, "c_mmat": mmat,
            "c_ssm": np.ascontiguousarray(inp["state_ssm"][:, 2 * c:2 * c + 2]),
            "c_conv": np.ascontiguousarray(inp["state_conv"][:, 2 * c:2 * c + 2]),
            "prow": prow_np, "c_U": U_np, "c_NEG": NEG_np,
            "ffn1_wgu": inp["ffn1_wgu"], "ffn1_wd": inp["ffn1_wd"],
            "ffn2_wgu": inp["ffn2_wgu"], "ffn2_wd": inp["ffn2_wd"],
        }
        maps.append(m)
    return maps


def kernel(**inputs):
    inp = {k: np.asarray(v) for k, v in inputs.items()}
    if "nc" not in _NC_CACHE:
        _NC_CACHE["nc"] = build_program()
    nc = _NC_CACHE["nc"]
    maps = make_in_maps(inp)
    res = run_bass_kernel_spmd(nc, maps, core_ids=list(range(NCORES)))
    r = res.results
    g = lambda c, k: np.asarray(r[c][k])
    y_p = np.stack([g(c, "yp") for c in range(NCORES)], axis=0)
    y_s = np.concatenate([g(c, "ys").reshape(2, 16, D) for c in range(NCORES)], axis=0)
    ckv_p = np.stack([g(c, "o_ckv_p") for c in range(NCORES)], axis=1)
    kr_p = np.stack([g(c, "o_kr_p") for c in range(NCORES)], axis=1)
    ret_p = np.stack([g(c, "o_ret_p") for c in range(NCORES)], axis=1)
    ssm_p = np.stack([g(c, "o_ssm_p") for c in range(NCORES)], axis=1)
    conv_p = np.stack([g(c, "o_conv_p") for c in range(NCORES)], axis=1)
    ckv_s = np.concatenate([g(c, "o_ckv_s").reshape(2, 2, 16, 256) for c in range(NCORES)], axis=1)
    kr_s = np.concatenate([g(c, "o_kr_s").reshape(2, 2, 16, 32) for c in range(NCORES)], axis=1)
    ret_s = np.concatenate([g(c, "o_ret_s") for c in range(NCORES)], axis=1)
    ssm_s = np.concatenate([g(c, "o_ssm_s") for c in range(NCORES)], axis=1)
    conv_s = np.concatenate([g(c, "o_conv_s") for c in range(NCORES)], axis=1)
    outs = (y_p, y_s, ckv_p, kr_p, ret_p, ssm_p, conv_p, ckv_s, kr_s, ret_s, ssm_s, conv_s)
    return tuple(np.ascontiguousarray(o, dtype=np.float32) for o in outs)
```

```python
import contextlib
import math
import os
import numpy as np
import ml_dtypes
import concourse.bass as bass
import concourse.mybir as mybir
from concourse.bass_utils import run_bass_kernel_spmd

F32 = mybir.dt.float32
BF16 = mybir.dt.bfloat16
AF = mybir.ActivationFunctionType
ALU = mybir.AluOpType

NCORES = 8
D = 1024
SEQ = 2048
T = 2080
DFF = 2816
NKF = DFF // 128
EPS = 1e-6
TILES = [(0, 512), (512, 512), (1024, 512), (1536, 512), (2048, 32)]
IN_COLS = 5296
RET0, MLA0, SSM0 = 0, 2048, 2720

_DBG_ENV = os.environ if os.environ.get("MK_DEBUG") else {}
ANNOTATE = bool(_DBG_ENV.get("MK_ANNOTATE"))
ENGS = ("sync", "tensor", "vector", "scalar", "gpsimd")
_SES = not _DBG_ENV.get("MK_NO_SES")
SAME_ENGINE_SYNC = {"vector": _SES, "scalar": _SES, "gpsimd": True, "tensor": False, "sync": False}


class Buf:
    __slots__ = ("name", "w", "r", "dsem", "dcnt", "excl")

    def __init__(self, name="", excl=False):
        self.name = name
        self.excl = excl
        self.w = None
        self.r = []
        self.dsem = None
        self.dcnt = 0


class Prog:
    def __init__(self):
        self.ops = {e: [] for e in ENGS}
        self.cnt = {e: 0 for e in ENGS}
        self.seen = {e: {} for e in ENGS}
        self.dma_sems = []
        self.dma_cnt = {}
        self.tag = ""

    def new_dma_sem(self):
        k = "d%d" % len(self.dma_sems)
        self.dma_sems.append(k)
        return k

    def _deps(self, eng, reads, writes, skip_waw=None):
        need = {}
        seen = self.seen[eng]

        def add(ev):
            if ev is None:
                return
            k, v = ev
            if k == eng and not SAME_ENGINE_SYNC[eng]:
                return
            if seen.get(k, 0) >= v:
                return
            if need.get(k, 0) < v:
                need[k] = v
        for b in reads:
            add(b.w)
            if b.excl:
                for ev in b.r:
                    if ev[0] != eng:
                        add(ev)
        for b in writes:
            if not (skip_waw is not None and b.w is not None and b.w[0] == skip_waw):
                add(b.w)
            for ev in b.r:
                add(ev)
        for k, v in need.items():
            seen[k] = v
        return list(need.items())

    def barrier(self, skip=()):
        for e in ENGS:
            if e in skip:
                continue
            waits = []
            for k in ENGS:
                v = self.cnt[k]
                if k != e and v > self.seen[e].get(k, 0):
                    waits.append((k, v))
                    self.seen[e][k] = v
            for k, v in self.dma_cnt.items():
                if v > self.seen[e].get(k, 0):
                    waits.append((k, v))
                    self.seen[e][k] = v
            self.ops[e].append((waits, None, None, ""))

    def op(self, eng, fn, reads=(), writes=(), inc=True):
        waits = self._deps(eng, reads, writes)
        self.ops[eng].append((waits, fn, ("E", eng) if inc else None, self.tag))
        if inc:
            self.cnt[eng] += 1
            ev = (eng, self.cnt[eng])
        else:
            ev = (eng, self.cnt[eng] + 1)
        for b in reads:
            b.r.append(ev)
            if len(b.r) > 48:
                b.r = b.r[-48:] if False else self._compact(b.r)
        for b in writes:
            b.w = ev
            b.r = []
        return ev

    @staticmethod
    def _compact(evs):
        best = {}
        for k, v in evs:
            if best.get(k, 0) < v:
                best[k] = v
        return list(best.items())

    def dma(self, eng, fn, reads=(), writes=(), sembuf=None):
        sb = sembuf if sembuf is not None else (writes[0] if writes else reads[0])
        if sb.dsem is None:
            sb.dsem = self.new_dma_sem()
        waits = self._deps(eng, reads, writes, skip_waw=sb.dsem)
        sb.dcnt += 16
        ev = (sb.dsem, sb.dcnt)
        self.dma_cnt[sb.dsem] = sb.dcnt
        self.ops[eng].append((waits, fn, ("D", sb.dsem), self.tag))
        for b in reads:
            b.r.append(ev)
        for b in writes:
            b.w = ev
            b.r = []
        return ev

    def wait_all(self, eng, bufs):
        waits = self._deps(eng, [], bufs)
        self.ops[eng].append((waits, None, None, ""))

    def check(self):
        sem = {}
        pc = {e: 0 for e in ENGS}
        total = sum(len(v) for v in self.ops.values())
        done = 0
        while done < total:
            prog = False
            for e in ENGS:
                ops = self.ops[e]
                while pc[e] < len(ops):
                    waits, fn, inc, _t = ops[pc[e]]
                    if any(sem.get(k, 0) < v for k, v in waits):
                        break
                    if inc is not None:
                        sem[inc[1]] = sem.get(inc[1], 0) + (1 if inc[0] == "E" else 16)
                    pc[e] += 1
                    done += 1
                    prog = True
            if not prog:
                msg = []
                for e in ENGS:
                    if pc[e] < len(self.ops[e]):
                        waits = self.ops[e][pc[e]][0]
                        msg.append((e, pc[e], [(k, v, sem.get(k, 0)) for k, v in waits if sem.get(k, 0) < v]))
                raise RuntimeError("DEADLOCK in recorded program: %s" % msg)
        return {e: len(v) for e, v in self.ops.items()}

    def emit(self, nc, st):
        print("[prog] ops per engine:", self.check(), "dma sems:", len(self.dma_sems))
        sems = {}
        for e in ENGS:
            sems[e] = st.enter_context(nc.semaphore("s_" + e))
        for k in self.dma_sems:
            sems[k] = st.enter_context(nc.semaphore("s_" + k))
        block = st.enter_context(nc.Block())
        ops = self.ops

        def body(ename):
            def run(eng):
                for waits, fn, inc, tag in ops[ename]:
                    for k, v in waits:
                        eng.wait_ge(sems[k], v)
                    if fn is None:
                        continue
                    ins = fn(eng)
                    if tag and ANNOTATE:
                        ins.annotate(tag)
                    if inc is not None:
                        ins.then_inc(sems[inc[1]], 1 if inc[0] == "E" else 16)
            return run

        block.sync(body("sync"))
        block.tensor(body("tensor"))
        block.vector(body("vector"))
        block.scalar(body("scalar"))
        block.gpsimd(body("gpsimd"))


class Arena:
    def __init__(self, ap, size):
        self.ap = ap
        self.size = size
        self.top = 0

    def alloc(self, ncols, dtype=BF16):
        n = ncols * (2 if dtype == F32 else 1)
        n = (n + 15) // 16 * 16
        a = self.top
        self.top += n
        assert self.top <= self.size, ("arena overflow", self.top, self.size)
        v = self.ap[:, a:a + ncols * (2 if dtype == F32 else 1)]
        if dtype == F32:
            v = v.bitcast(F32)
        return v

    def mark(self):
        return self.top

    def reset(self, m):
        self.top = m


class Pool:
    def __init__(self, slots):
        self.slots = slots
        self.i = 0

    def next(self):
        s = self.slots[self.i % len(self.slots)]
        self.i += 1
        return s


def pipeline(stages, depth):
    n = len(stages)
    for i in range(min(depth - 1, n)):
        stages[i][0](i % depth)
    for i in range(n):
        j = i + depth - 1
        if j < n:
            stages[j][0](j % depth)
        stages[i][1](i % depth)


PC = {}
_o = 0
for _n, _w in [("ffn1_norm", 8), ("mix_norm", 8), ("ffn2_norm", 8), ("ret_norm", 4), ("mla_q_norm", 3), ("mla_kv_norm", 2), ("mla_q_gain", 1), ("mla_k_gain", 1),
               ("conv_w", 48), ("conv_b", 12), ("ssm_d", 8), ("ssm_norm", 8)]:
    PC[_n] = (_o, _w)
    _o += _w
NPC = _o


def pack_pcols(inp, l):
    pc = np.zeros((128, NPC), np.float32)
    for n in ("ffn1_norm", "mix_norm", "ffn2_norm"):
        o, w = PC[n]
        pc[:, o:o + w] = inp[n][l].reshape(w, 128).T
    o, w = PC["ret_norm"]
    pc[:, o:o + w] = inp["ret_norm"][l].reshape(4, 128).T
    o, w = PC["mla_q_norm"]
    pc[:, o:o + w] = inp["mla_q_norm"][l].reshape(3, 128).T
    o, w = PC["mla_kv_norm"]
    pc[:, o:o + w] = inp["mla_kv_norm"][l].reshape(2, 128).T
    pc[0:96, PC["mla_q_gain"][0]] = inp["mla_q_gain"][l]
    pc[0:96, PC["mla_k_gain"][0]] = inp["mla_k_gain"][l]
    o, w = PC["conv_w"]
    pc[:, o:o + w] = inp["ssm_conv_w"][l].reshape(4, 12, 128).transpose(2, 1, 0).reshape(128, 48)
    o, w = PC["conv_b"]
    pc[:, o:o + w] = inp["ssm_conv_b"][l].reshape(12, 128).T
    o, w = PC["ssm_d"]
    pc[:, o:o + w] = np.repeat(inp["ssm_d"][l], 64).reshape(8, 128).T
    o, w = PC["ssm_norm"]
    pc[:, o:o + w] = inp["ssm_norm"][l].reshape(8, 128).T
    return pc


def ssm_tables():
    U = np.triu(np.ones((128, 128), np.float32))
    NEG = np.where(np.arange(128)[None, :] < np.arange(128)[:, None], -30000.0, 0.0).astype(np.float32)
    NEGt = np.zeros((128, 640), np.float32)
    NEGt[:, 0:512] = np.tile(NEG, (1, 4))
    NEGt[0:16, 512:640] = np.tile(NEG[0:16, 0:16], (1, 8))
    return U, NEGt.astype(ml_dtypes.bfloat16)


def mla_tables():
    pos = token_positions()
    theta = (1.0 / (10000.0 ** (np.arange(0, 32, 2, dtype=np.float32) / 32.0))).astype(np.float32)
    tab = np.zeros((128, 2, T), np.float64)
    for r in range(32):
        f = r % 16
        ang = (pos * theta[f]).astype(np.float32).astype(np.float64)
        tab[64 + r, 0] = np.cos(ang)
        tab[64 + r, 1] = np.sin(ang) * (-1.0 if r < 16 else 1.0)
    mats = np.zeros((128, 4, 128), np.float32)
    for m in range(32):
        mats[64 + (m + 16 if m < 16 else m - 16), 0, 64 + m] = 1.0
    mats[0:96, 1, 0:96] = 1.0
    mats[0, 2, 64:128] = 1.0
    mats[0, 3, 0:64] = -30000.0
    return tab.astype(ml_dtypes.bfloat16), mats.astype(ml_dtypes.bfloat16)


def token_positions():
    return np.concatenate([np.arange(2048), 1024 + np.arange(16), 1024 + np.arange(16)]).astype(np.float32)


RC = {"DmT": (0, 1024), "QD": (1024, 512), "KD": (1536, 512), "cdec": (2048, 4),
      "DmTs": (2052, 128), "QDs": (2180, 64), "KDs": (2244, 512), "cdecs": (2756, 4)}
NRC = 2760


def ret_tables():
    pos = token_positions()
    theta = (1.0 / (10000.0 ** np.linspace(0.0, 1.0, 32, dtype=np.float32))).astype(np.float32)
    p = np.arange(128)
    d = p % 64
    f = d % 32
    ang = (pos[None, :] * theta[f][:, None]).astype(np.float32).astype(np.float64)
    C = np.cos(ang)
    S = np.sin(ang) * np.where(d < 32, -1.0, 1.0)[:, None]
    tab = np.stack([C, S], axis=1).astype(ml_dtypes.bfloat16)
    Pm = np.zeros((128, 128), np.float32)
    for m in range(128):
        Pm[m + 32 if (m % 64) < 32 else m - 32, m] = 1.0
    BD = np.zeros((128, 128), np.float32)
    BD[0:64, 0:64] = 1.0
    BD[64:128, 64:128] = 1.0
    mats = np.stack([Pm, BD], axis=1).astype(ml_dtypes.bfloat16)
    g = 1.0 - 2.0 ** (-5.0 - np.arange(8, dtype=np.float64))
    rc = np.zeros((128, NRC), np.float64)

    def put(name, arr):
        o, w = RC[name]
        a = arr.reshape(arr.shape[0], -1)
        rc[:a.shape[0], o:o + w] = a
    for c, sfx in ((128, ""), (16, "s")):
        k = np.arange(c)[:, None, None]
        q = np.arange(c)[None, None, :]
        gh = g[None, :, None]
        put("DmT" + sfx, np.where(q >= k, gh ** np.maximum(q - k, 0), 0.0))
        gp = g[2 * np.arange(4)[None, :, None] + (p // 64)[:, None, None]]
        put("QD" + sfx, gp ** (np.arange(c)[None, None, :] + 1.0))
        gj = g[2 * np.arange(4)[None, :, None] + (p // 64)[None, None, :]]
        put("KD" + sfx, gj ** (c - 1.0 - np.arange(c)[:, None, None]))
        put("cdec" + sfx, (gp ** float(c))[:, :, 0])
    return tab, mats, rc.astype(np.float32)


def const_tables():
    ident = np.eye(128, dtype=np.float32)
    cb = {}
    cb["ident_b"] = ident.astype(ml_dtypes.bfloat16)
    cb["ones_b"] = np.ones((128, 128), ml_dtypes.bfloat16)
    return ident, cb


def build_program(dbg=None):
    nc = bass.Bass("TRN2", target_bir_lowering=False)
    dt = nc.dram_tensor
    xp = dt("xp", [SEQ, D], F32, kind="ExternalInput").ap()
    xs = dt("xs", [32, D], F32, kind="ExternalInput").ap()
    pcols = dt("pcols", [128, 2, NPC], F32, kind="ExternalInput").ap()
    c_ident = dt("c_ident", [128, 128], F32, kind="ExternalInput").ap()
    c_bf = dt("c_bf", [128, 2, 128], BF16, kind="ExternalInput").ap()
    wgu = [dt("ffn%d_wgu" % i, [2, D, 2 * DFF], F32, kind="ExternalInput").ap() for i in (1, 2)]
    wd = [dt("ffn%d_wd" % i, [2, DFF, D], F32, kind="ExternalInput").ap() for i in (1, 2)]
    w_in = dt("w_in", [2, D, IN_COLS], F32, kind="ExternalInput").ap()
    w_out = dt("w_out", [2, 2048, D], F32, kind="ExternalInput").ap()
    st_ret = dt("st_ret", [2, 2, 8, 64, 64], F32, kind="ExternalInput").ap()
    c_rtab = dt("c_rtab", [128, 2, T], BF16, kind="ExternalInput").ap()
    c_rmat = dt("c_rmat", [128, 2, 128], BF16, kind="ExternalInput").ap()
    c_rc = dt("c_rc", [128, NRC], F32, kind="ExternalInput").ap()
    o_ret_p = dt("o_ret_p", [2, 8, 64, 64], F32, kind="ExternalOutput").ap()
    o_ret_s = dt("o_ret_s", [2, 2, 8, 64, 64], F32, kind="ExternalOutput").ap()
    w_uq = dt("w_uq", [2, 384, 768], F32, kind="ExternalInput").ap()
    w_ukv = dt("w_ukv", [2, 256, 1024], F32, kind="ExternalInput").ap()
    c_ckv = dt("c_ckv", [2, 2, 1024, 256], F32, kind="ExternalInput").ap()
    c_kr = dt("c_kr", [2, 2, 1024, 32], F32, kind="ExternalInput").ap()
    c_mtab = dt("c_mtab", [128, 2, T], BF16, kind="ExternalInput").ap()
    c_mmat = dt("c_mmat", [128, 4, 128], BF16, kind="ExternalInput").ap()
    o_ckv_p = dt("o_ckv_p", [2, SEQ, 256], F32, kind="ExternalOutput").ap()
    o_kr_p = dt("o_kr_p", [2, SEQ, 32], F32, kind="ExternalOutput").ap()
    o_ckv_s = dt("o_ckv_s", [2, 32, 256], F32, kind="ExternalOutput").ap()
    o_kr_s = dt("o_kr_s", [2, 32, 32], F32, kind="ExternalOutput").ap()
    c_ssm = dt("c_ssm", [2, 2, 16, 64, 128], F32, kind="ExternalInput").ap()
    c_conv = dt("c_conv", [2, 2, 3, 1536], F32, kind="ExternalInput").ap()
    prow = dt("prow", [128, 2, 128], F32, kind="ExternalInput").ap()
    c_U = dt("c_U", [128, 128], F32, kind="ExternalInput").ap()
    c_NEG = dt("c_NEG", [128, 640], BF16, kind="ExternalInput").ap()
    o_ssm_p = dt("o_ssm_p", [2, 16, 64, 128], F32, kind="ExternalOutput").ap()
    o_ssm_s = dt("o_ssm_s", [2, 2, 16, 64, 128], F32, kind="ExternalOutput").ap()
    o_conv_p = dt("o_conv_p", [2, 3, 1536], F32, kind="ExternalOutput").ap()
    o_conv_s = dt("o_conv_s", [2, 2, 3, 1536], F32, kind="ExternalOutput").ap()
    dbg_out = {}
    yp = dt("yp", [SEQ, D], F32, kind="ExternalOutput").ap()
    ys = dt("ys", [32, D], F32, kind="ExternalOutput").ap()

    st = contextlib.ExitStack()
    with st:
        P = Prog()
        xT_t = st.enter_context(nc.sbuf_tensor("xT", [128, 8, T], F32))
        xT = xT_t[:]
        xB = [[Buf("x%d_%d" % (c, i)) for i in range(5)] for c in range(8)]
        WA_t = st.enter_context(nc.sbuf_tensor("WA", [128, 2, 4096], BF16))
        WA = [(WA_t[:, i, :], Buf("WA%d" % i)) for i in range(2)]
        cst_t = st.enter_context(nc.sbuf_tensor("cst", [128, 2 * NPC + 128], F32))
        pc_sb = cst_t[:, 0:2 * NPC].rearrange("p (l n) -> p l n", l=2)
        ident_f = cst_t[:, 2 * NPC:2 * NPC + 128]
        cbf_t = st.enter_context(nc.sbuf_tensor("cbf", [128, 2, 128], BF16))
        ident_b = cbf_t[:, 0, :]
        ones_b = cbf_t[:, 1, :]
        bC = Buf("consts")
        ARENA = 63800
        ar_t = st.enter_context(nc.sbuf_tensor("arena", [128, ARENA], BF16))
        AR = Arena(ar_t[:], ARENA)
        psum = []
        for i in range(8):
            pt = st.enter_context(nc.psum_tensor("ps%d" % i, [128, 512], F32))
            psum.append((pt[:], Buf("ps%d" % i, excl=True)))
        PS = Pool(psum[2:8])
        PSA = Pool(psum[0:2])
        outbufs = []

        def pcol(l, name, c):
            o, w = PC[name]
            return pc_sb[:, l, o + c:o + c + 1]

        P.dma("sync", lambda e: e.dma_start(out=pc_sb, in_=pcols), writes=[bC])
        bC2 = Buf("c2")
        P.dma("sync", lambda e: e.dma_start(out=ident_f, in_=c_ident), writes=[bC2])
        bC3 = Buf("c3")
        P.dma("sync", lambda e: e.dma_start(out=cbf_t[:], in_=c_bf), writes=[bC3])
        CONSTS = [bC, bC2, bC3]

        m0 = AR.mark()
        stg = [(AR.alloc(1024, F32), Buf("stg%d" % i)) for i in range(2)]
        for i in range(17):
            s_ap, s_b = stg[i % 2]
            if i < 16:
                rows = 128
                P.dma("sync", lambda e, s_ap=s_ap, i=i: e.dma_start(out=s_ap, in_=xp[i * 128:(i + 1) * 128, :]), writes=[s_b])
                c0 = i * 128
                ti = i // 4
            else:
                rows = 32
                P.dma("sync", lambda e, s_ap=s_ap: e.dma_start(out=s_ap[0:32, :], in_=xs), writes=[s_b])
                c0 = 2048
                ti = 4
            for hb in range(2):
                p_ap, p_b = PS.next()
                for cc in range(4):
                    c = hb * 4 + cc
                    P.op("tensor", lambda e, p_ap=p_ap, s_ap=s_ap, c=c, cc=cc, rows=rows: e.transpose(
                        out=p_ap[:, cc * rows:(cc + 1) * rows], in_=s_ap[0:rows, c * 128:(c + 1) * 128], identity=ident_f[0:rows, 0:rows]),
                        reads=[s_b] + CONSTS, writes=[p_b], inc=(cc == 3))
                eng = "scalar" if hb == 0 else "vector"
                src = p_ap[:, 0:4 * rows].rearrange("p (c t) -> p c t", c=4)
                dst = xT[:, hb * 4:hb * 4 + 4, c0:c0 + rows]
                if eng == "scalar":
                    P.op("scalar", lambda e, src=src, dst=dst: e.activation(out=dst, in_=src, func=AF.Copy),
                         reads=[p_b], writes=[xB[c][ti] for c in range(hb * 4, hb * 4 + 4)])
                else:
                    P.op("vector", lambda e, src=src, dst=dst: e.tensor_copy(out=dst, in_=src),
                         reads=[p_b], writes=[xB[c][ti] for c in range(hb * 4, hb * 4 + 4)])
        P.barrier(skip=("gpsimd",))
        AR.reset(m0)

        def rms_norm_fm(src_chunks, nch, gain_fn, dst_fn, n, inv_count, tmp, dq=None):
            sq, rs = tmp
            p_ap, p_b = PS.next()
            for c in range(nch):
                s_ap, s_bufs = src_chunks[c]
                q_ap, q_b = sq.next()
                P.op("scalar", lambda e, q_ap=q_ap, s_ap=s_ap: e.activation(out=q_ap[:, 0:n], in_=s_ap, func=AF.Square),
                     reads=s_bufs, writes=[q_b])
                P.op("tensor", lambda e, p_ap=p_ap, q_ap=q_ap, c=c: e.matmul(p_ap[:, 0:n], lhsT=ones_b, rhs=q_ap[:, 0:n], start=(c == 0), stop=(c == nch - 1)),
                     reads=[q_b] + CONSTS, writes=[p_b])
            def finish():
                r_ap, r_b = rs.next()
                P.op("scalar", lambda e: e.activation(out=r_ap[:, 0:n], in_=p_ap[:, 0:n], func=AF.Ln, bias=EPS, scale=inv_count),
                     reads=[p_b], writes=[r_b])
                P.op("scalar", lambda e: e.activation(out=r_ap[:, 0:n], in_=r_ap[:, 0:n], func=AF.Exp, scale=-0.5),
                     reads=[r_b], writes=[r_b])
                for c in range(nch):
                    s_ap, s_bufs = src_chunks[c]
                    d_ap, d_bufs = dst_fn(c)
                    g_ap = gain_fn(c)
                    P.op("vector", lambda e, d_ap=d_ap, s_ap=s_ap, g_ap=g_ap: e.scalar_tensor_tensor(
                        out=d_ap, in0=s_ap, scalar=g_ap, in1=r_ap[:, 0:n], op0=ALU.mult, op1=ALU.mult),
                        reads=s_bufs + [r_b] + CONSTS, writes=d_bufs)

            if dq is None:
                finish()
            else:
                while dq:
                    dq.pop()()
                dq.append(finish)

        def ffn(l, which):
            wgu_ap = wgu[0 if which == "ffn1" else 1]
            wd_ap = wd[0 if which == "ffn1" else 1]
            nname = which + "_norm"
            m = AR.mark()
            HB = 1056
            hT = AR.alloc(8 * HB).rearrange("p (c t) -> p c t", c=8)
            aT = AR.alloc(NKF * HB).rearrange("p (c t) -> p c t", c=NKF)
            WB = [(AR.alloc(NKF * 256).rearrange("p (k n) -> p k n", k=NKF), Buf("WB%d" % i)) for i in range(2)]
            sq = Pool([(AR.alloc(512), Buf("sq%d" % i)) for i in range(2)])
            rs = Pool([(AR.alloc(512, F32), Buf("rs%d" % i)) for i in range(2)])
            sg = Pool([(AR.alloc(512, F32), Buf("sg%d" % i)) for i in range(2)])
            hB = [[Buf("h%d_%d" % (c, ti)) for ti in range(2)] + [Buf("h%d_%d" % (c, ti)) for ti in range(3)] for c in range(8)]
            aB = [[Buf("a%d_%d" % (j, ti)) for ti in range(2)] + [Buf("a%d_%d" % (j, ti)) for ti in range(3)] for j in range(NKF)]
            for c in range(8):
                hB[c][2] = hB[c][0]; hB[c][3] = hB[c][1]
            for j in range(NKF):
                aB[j][2] = aB[j][0]; aB[j][3] = aB[j][1]
            def half_parts(half):
                tiles = [0, 1] if half == 0 else [2, 3, 4]
                hb0 = 0 if half == 0 else 1024

                def do_norm():
                    dq = []
                    for ti in tiles:
                        c0, n = TILES[ti]
                        rms_norm_fm([(xT[:, c, c0:c0 + n], [xB[c][ti]]) for c in range(8)], 8,
                                    lambda c: pcol(l, nname, c),
                                    lambda c, ti=ti, c0=c0, n=n: (hT[:, c, c0 - hb0:c0 - hb0 + n], [hB[c][ti]]),
                                    n, 1.0 / D, (sq, rs), dq=dq)
                    while dq:
                        dq.pop()()

                def do_gu():
                    P.tag = "ffn.gu"
                    stages = []
                    for g in range(11):
                        def load(slot, g=g):
                            w_ap, w_b = WA[slot]
                            wv = w_ap.rearrange("p (u k n) -> p u k n", u=2, k=8)
                            src = wgu_ap[l].rearrange("(k p) n -> p k n", p=128)
                            P.dma("gpsimd", lambda e: e.dma_start(out=wv[:, 0], in_=src[:, :, g * 256:(g + 1) * 256]), writes=[w_b])
                            P.dma("gpsimd", lambda e: e.dma_start(out=wv[:, 1], in_=src[:, :, DFF + g * 256:DFF + (g + 1) * 256]), writes=[w_b])

                        def comp(slot, g=g):
                            w_ap, w_b = WA[slot]
                            wv = w_ap.rearrange("p (u k n) -> p u k n", u=2, k=8)
                            for pr in range(2):
                                j = g * 2 + pr
                                for ti in tiles:
                                    c0, n = TILES[ti]
                                    cs = slice(c0 - hb0, c0 - hb0 + n)
                                    pg, pgb = PS.next()
                                    pu, pub = PS.next()
                                    for u, (pp, ppb) in enumerate(((pg, pgb), (pu, pub))):
                                        for k in range(8):
                                            P.op("tensor", lambda e, pp=pp, u=u, k=k, pr=pr, cs=cs, n=n: e.matmul(
                                                pp[:, 0:n], lhsT=wv[:, u, k, pr * 128:(pr + 1) * 128], rhs=hT[:, k, cs], start=(k == 0), stop=(k == 7)),
                                                reads=[w_b, hB[k][ti]], writes=[ppb], inc=(k == 7))
                                    s_ap, s_b = sg.next()
                                    P.op("scalar", lambda e, s_ap=s_ap, pg=pg, n=n: e.activation(out=s_ap[:, 0:n], in_=pg[:, 0:n], func=AF.Silu),
                                         reads=[pgb], writes=[s_b])
                                    P.op("vector", lambda e, s_ap=s_ap, pu=pu, j=j, cs=cs, n=n: e.tensor_tensor(
                                        out=aT[:, j, cs], in0=pu[:, 0:n], in1=s_ap[:, 0:n], op=ALU.mult),
                                        reads=[pub, s_b], writes=[aB[j][ti]])
                        stages.append((load, comp))
                    pipeline(stages, 2)

                def do_down():
                    P.tag = "ffn.down"
                    stages = []
                    for mb in range(4):
                        def load(slot, mb=mb):
                            w_ap, w_b = WB[slot]
                            src = wd_ap[l].rearrange("(k p) n -> p k n", p=128)
                            P.dma("gpsimd", lambda e: e.dma_start(out=w_ap, in_=src[:, :, mb * 256:(mb + 1) * 256]), writes=[w_b])

                        def comp(slot, mb=mb):
                            w_ap, w_b = WB[slot]
                            for mm in range(2):
                                mo = mb * 2 + mm
                                for ti in tiles:
                                    c0, n = TILES[ti]
                                    cs = slice(c0 - hb0, c0 - hb0 + n)
                                    pp, ppb = PS.next()
                                    for k in range(NKF):
                                        P.op("tensor", lambda e, pp=pp, k=k, mm=mm, cs=cs, n=n: e.matmul(
                                            pp[:, 0:n], lhsT=w_ap[:, k, mm * 128:(mm + 1) * 128], rhs=aT[:, k, cs], start=(k == 0), stop=(k == NKF - 1)),
                                            reads=[w_b, aB[k][ti]], writes=[ppb], inc=(k == NKF - 1))
                                    P.op("vector", lambda e, pp=pp, mo=mo, c0=c0, n=n: e.scalar_tensor_tensor(
                                        out=xT[:, mo, c0:c0 + n], in0=pp[:, 0:n], scalar=0.5, in1=xT[:, mo, c0:c0 + n], op0=ALU.mult, op1=ALU.add),
                                        reads=[ppb, xB[mo][ti]], writes=[xB[mo][ti]])
                        stages.append((load, comp))
                    pipeline(stages, 2)

                return do_norm, do_gu, do_down

            nA, gA, dA = half_parts(0)
            nB, gB, dB = half_parts(1)
            nA(); gA(); nB(); dA(); gB(); dB()
            P.barrier()
            AR.reset(m)

        def MM(out, lhsT, rhs, start, stop, reads, writes, inc=True, sgc=False):
            P.op("tensor", lambda e: e.matmul(out, lhsT=lhsT, rhs=rhs, start=start, stop=stop, skip_group_check=sgc), reads=reads, writes=writes, inc=inc)

        def TR(out, in_, ident, reads, writes, inc=True):
            P.op("tensor", lambda e: e.transpose(out=out, in_=in_, identity=ident), reads=reads, writes=writes, inc=inc)

        def ACT(out, in_, func, reads, writes, **kw):
            P.op("scalar", lambda e: e.activation(out=out, in_=in_, func=func, **kw), reads=reads, writes=writes)

        def TT(eng, out, in0, in1, op, reads, writes):
            if eng == "gpsimd":
                eng = _DBG_ENV.get("POOL_ENG", "gpsimd")
            P.op(eng, lambda e: e.tensor_tensor(out=out, in0=in0, in1=in1, op=op), reads=reads, writes=writes)

        def STT(out, in0, scalar, in1, op0, op1, reads, writes):
            P.op("vector", lambda e: e.scalar_tensor_tensor(out=out, in0=in0, scalar=scalar, in1=in1, op0=op0, op1=op1), reads=reads, writes=writes)

        def TS(eng, out, in0, s1, s2, op0, op1, reads, writes):
            P.op(eng, lambda e: e.tensor_scalar(out=out, in0=in0, scalar1=s1, scalar2=s2, op0=op0, op1=op1), reads=reads, writes=writes)

        def CP(eng, out, in_, reads, writes):
            if eng == "scalar":
                P.op("scalar", lambda e: e.activation(out=out, in_=in_, func=AF.Copy), reads=reads, writes=writes)
            else:
                P.op(eng, lambda e: e.tensor_copy(out=out, in_=in_), reads=reads, writes=writes)

        def MEMSET(eng, ap, val, writes):
            P.op(eng, lambda e: e.memset(ap, val), writes=writes)

        def LOAD(eng, out, in_, wbuf, reads=(), slow=False):
            P.dma(eng, lambda e: e.dma_start(out=out, in_=in_, allow_slow_non_contiguous=slow), reads=list(reads), writes=[wbuf])

        def STORE(out, in_, sbuf_buf, slow=False):
            ob = Buf("o")
            P.dma("sync", lambda e: e.dma_start(out=out, in_=in_, allow_slow_non_contiguous=slow), reads=[sbuf_buf], writes=[ob], sembuf=sbuf_buf)
            outbufs.append(ob)

        def dump(name, ap, bufs, dtype=F32):
            shp = list(ap.shape)
            t_ = dt("dbg_" + name, shp, dtype, kind="ExternalOutput").ap()
            for b in bufs[:1]:
                ob = Buf("o")
                P.dma("sync", lambda e: e.dma_start(out=t_, in_=ap), reads=list(bufs), writes=[ob], sembuf=Buf("dsem"))
                outbufs.append(ob)

        def xacc_prefetch(l, wrow0):
            w_ap, w_b = WA[0]
            wv = w_ap.rearrange("p (k n) -> p k n", k=4)
            LOAD("gpsimd", wv, w_out[l][wrow0:wrow0 + 512, :].rearrange("(k p) n -> p k n", p=128), w_b)

        def xacc(l, wrow0, nk, mixT, mixB, tag, preloaded=False):
            P.tag = "xacc." + tag
            for half in range(nk // 4):
                w_ap, w_b = WA[half % 2] if nk > 4 else WA[0]
                wv = w_ap.rearrange("p (k n) -> p k n", k=4)
                src = w_out[l][wrow0 + half * 512:wrow0 + (half + 1) * 512, :].rearrange("(k p) n -> p k n", p=128)
                if not preloaded:
                    LOAD("gpsimd", wv, src, w_b)
                for mo in range(8):
                    for ti in range(5):
                        c0, n = TILES[ti]
                        pp, ppb = PS.next()
                        for k in range(4):
                            MM(pp[:, 0:n], wv[:, k, mo * 128:(mo + 1) * 128], mixT[:, half * 4 + k, c0:c0 + n], k == 0, k == 3,
                               [w_b, mixB[half * 4 + k][ti]], [ppb], inc=(k == 3))
                        TT("vector", xT[:, mo, c0:c0 + n], pp[:, 0:n], xT[:, mo, c0:c0 + n], ALU.add, [ppb, xB[mo][ti]], [xB[mo][ti]])

        def mixer(l, stop_after=None):
            mM = AR.mark()
            hT_flat = AR.alloc(8 * T)
            hT = hT_flat.rearrange("p (c t) -> p c t", c=8)
            hB = [[Buf("mh%d_%d" % (c, ti)) for ti in range(5)] for c in range(8)]
            mt = AR.mark()
            sq = Pool([(AR.alloc(512), Buf("sq%d" % i)) for i in range(2)])
            rs = Pool([(AR.alloc(512, F32), Buf("rs%d" % i)) for i in range(2)])
            dq = []
            for ti in range(5):
                c0, n = TILES[ti]
                rms_norm_fm([(xT[:, c, c0:c0 + n], [xB[c][ti]]) for c in range(8)], 8,
                            lambda c: pcol(l, "mix_norm", c),
                            lambda c, ti=ti, c0=c0, n=n: (hT[:, c, c0:c0 + n], [hB[c][ti]]),
                            n, 1.0 / D, (sq, rs), dq=dq)
            while dq:
                dq.pop()()
            P.barrier()
            AR.reset(mt)
            if dbg not in ("mla", "ssm"):
                group_ret(l, hT, hB)
            if dbg == "ret":
                return
            if dbg != "mla":
                group_ssm(l, hT, hB)
            if dbg == "ssm":
                return
            group_mla(l, hT, hB, hT_flat)
            P.barrier()
            AR.reset(mM)

        def group_ret(l, hT, hB):
            mR = AR.mark()
            rtab = AR.alloc(2 * T).rearrange("p (a t) -> p a t", a=2)
            rmat = AR.alloc(256).rearrange("p (a t) -> p a t", a=2)
            rc = AR.alloc(NRC, F32)
            bRT, bRM, bRC = Buf("rtab"), Buf("rmat"), Buf("rc")
            LOAD("sync", rtab, c_rtab, bRT)
            LOAD("sync", rmat, c_rmat, bRM)
            LOAD("sync", rc, c_rc, bRC)
            Ctab, Stab = rtab[:, 0, :], rtab[:, 1, :]
            Pm, BD = rmat[:, 0, :], rmat[:, 1, :]

            def rcv(name, shape3=None):
                o, w = RC[name]
                v = rc[:, o:o + w]
                return v
            DmT = rcv("DmT").rearrange("p (h q) -> p h q", h=8)
            QD = rcv("QD").rearrange("p (h q) -> p h q", h=4)
            KD = rcv("KD").rearrange("p (h q) -> p h q", h=4)
            cdec = rcv("cdec")
            DmTs = rcv("DmTs").rearrange("p (h q) -> p h q", h=8)
            QDs = rcv("QDs").rearrange("p (h q) -> p h q", h=4)
            KDs = rcv("KDs").rearrange("p (h q) -> p h q", h=4)
            cdecs = rcv("cdecs")
            RCB = [bRT, bRM, bRC]

            mixT = AR.alloc(4 * T).rearrange("p (c t) -> p c t", c=4)
            mixB = [[Buf("rm%d_%d" % (c, ti)) for ti in range(5)] for c in range(4)]
            qT, kT, qdT, sgT = AR.alloc(T), AR.alloc(T), AR.alloc(T), AR.alloc(T)
            v_tm = AR.alloc(18 * 128).rearrange("p (i d) -> p i d", i=18)
            kd_tm = AR.alloc(18 * 128).rearrange("p (i d) -> p i d", i=18)
            S_bf = AR.alloc(18 * 64).rearrange("p (i d) -> p i d", i=18)
            S_f = [AR.alloc(64, F32) for _ in range(3)]
            rawp = Pool([(AR.alloc(512), Buf("raw%d" % i)) for i in range(2)])
            t1p = Pool([(AR.alloc(512, F32), Buf("t1_%d" % i)) for i in range(2)])
            t2p = Pool([(AR.alloc(512, F32), Buf("t2_%d" % i)) for i in range(2)])
            attp = Pool([(AR.alloc(128), Buf("att%d" % i)) for i in range(4)])
            sqp = Pool([(AR.alloc(512), Buf("rsq%d" % i)) for i in range(2)])
            rsp = Pool([(AR.alloc(512, F32), Buf("rrs%d" % i)) for i in range(2)])
            psb_view = lambda p_ap: p_ap.bitcast(BF16)

            qB = [Buf("q%d" % ti) for ti in range(5)]
            kB = [Buf("k%d" % ti) for ti in range(5)]
            qdB = [Buf("qd%d" % ti) for ti in range(5)]
            sgB = [Buf("sg%d" % ti) for ti in range(5)]
            vB = [Buf("v%d" % i) for i in range(18)]
            kdB = [Buf("kd%d" % i) for i in range(18)]
            SbB = [Buf("Sb%d" % i) for i in range(18)]
            SfB = [Buf("Sf%d" % i) for i in range(3)]
            for hp in range(4):
                if _DBG_ENV.get("RET_STOP") == "0":
                    break
                w_ap, w_b = WA[hp % 2]
                wv = w_ap.rearrange("p (u k n) -> p u k n", u=4, k=8)[:, :, :, 0:128] if False else w_ap[:, 0:4096].rearrange("p (u k n) -> p u k n", u=4, k=8)
                srcw = w_in[l].rearrange("(k p) n -> p k n", p=128)
                for u in range(4):
                    LOAD("gpsimd", wv[:, u], srcw[:, :, RET0 + 512 * u + 128 * hp:RET0 + 512 * u + 128 * hp + 128], w_b)
                if hp == 3:
                    xacc_prefetch(l, 0)
                for s_ in range(2):
                    LOAD("sync", S_f[1 + s_], st_ret[l, s_, 2 * hp:2 * hp + 2].rearrange("h k v -> (h k) v"), SfB[1 + s_])
                MEMSET("vector", S_f[0], 0.0, [SfB[0]])
                MEMSET("vector", S_bf[:, 0, :], 0.0, [SbB[0]])
                for s_ in range(2):
                    CP("scalar", S_bf[:, 16 + s_, :], S_f[1 + s_], [SfB[1 + s_]], [SbB[16 + s_]])
                if _DBG_ENV.get("RET_STOP") == "A1":
                    break
                P.tag = "ret.proj"
                pendB = None
                for ti in range(5):
                    c0, n = TILES[ti]
                    cs = slice(c0, c0 + n)
                    for u, (dst, dB, scale) in enumerate(((qT, qB, 1.0), (kT, kB, 0.125))):
                        pp, ppb = PS.next()
                        for k in range(8):
                            MM(pp[:, 0:n], wv[:, u, k, :], hT[:, k, cs], k == 0, k == 7, [w_b, hB[k][ti]], [ppb], inc=(k == 7))
                        r_ap, r_b = rawp.next()
                        CP("scalar", r_ap[:, 0:n], pp[:, 0:n], [ppb], [r_b])
                        if pendB is not None:
                            pendB()

                        def pendB(pp=pp, ppb=ppb, r_ap=r_ap, r_b=r_b, n=n, cs=cs, scale=scale, dst=dst, dB=dB, ti=ti, u=u):
                            p2, p2b = PS.next()
                            MM(p2[:, 0:n], Pm, r_ap[:, 0:n], True, True, [r_b] + RCB, [p2b])
                            a1, a1b = t1p.next()
                            a2, a2b = t2p.next()
                            STT(a1[:, 0:n], pp[:, 0:n], scale, Ctab[:, cs], ALU.mult, ALU.mult, [ppb] + RCB, [a1b])
                            STT(a2[:, 0:n], p2[:, 0:n], scale, Stab[:, cs], ALU.mult, ALU.mult, [p2b] + RCB, [a2b])
                            TT("vector", dst[:, cs], a1[:, 0:n], a2[:, 0:n], ALU.add, [a1b, a2b], [dB[ti]])
                            if u == 0:
                                if ti < 4:
                                    TT("vector", qdT[:, cs].rearrange("p (a b) -> p a b", b=128), qT[:, cs].rearrange("p (a b) -> p a b", b=128),
                                       QD[:, hp, :].unsqueeze(1).broadcast_to([128, 4, 128]), ALU.mult, [qB[ti]] + RCB, [qdB[ti]])
                                else:
                                    TT("vector", qdT[:, cs].rearrange("p (a b) -> p a b", b=16), qT[:, cs].rearrange("p (a b) -> p a b", b=16),
                                       QDs[:, hp, :].unsqueeze(1).broadcast_to([128, 2, 16]), ALU.mult, [qB[ti]] + RCB, [qdB[ti]])
                    pp, ppb = PS.next()
                    for k in range(8):
                        MM(pp[:, 0:n], wv[:, 3, k, :], hT[:, k, cs], k == 0, k == 7, [w_b, hB[k][ti]], [ppb], inc=(k == 7))
                    ACT(sgT[:, cs], pp[:, 0:n], AF.Silu, [ppb], [sgB[ti]])
                pendB()
                if _DBG_ENV.get("RET_STOP", "")[0:1] in ("A", "L"):
                    break
                P.tag = "ret.vk"
                for ti in range(4):
                    pp, ppb = PS.next()
                    for j in range(4):
                        i = ti * 4 + j
                        for k in range(8):
                            MM(pp[:, j * 128:(j + 1) * 128], hT[:, k, i * 128:(i + 1) * 128], wv[:, 2, k, :], k == 0, k == 7,
                               [w_b, hB[k][ti]], [ppb], inc=(k == 7 and j == 3))
                    CP("scalar", v_tm[:, ti * 4:ti * 4 + 4, :], pp[:, :].rearrange("p (a b) -> p a b", b=128), [ppb], vB[ti * 4:ti * 4 + 4])
                    pt, ptb = PS.next()
                    ptv = psb_view(pt)
                    for j in range(4):
                        i = ti * 4 + j
                        TR(ptv[:, j * 128:(j + 1) * 128], kT[:, i * 128:(i + 1) * 128], ident_b, [kB[ti]] + CONSTS, [ptb], inc=(j == 3))
                    TT("vector", kd_tm[:, ti * 4:ti * 4 + 4, :], ptv[:, 0:512].rearrange("p (a b) -> p a b", b=128),
                       KD[:, hp, :].unsqueeze(1).broadcast_to([128, 4, 128]), ALU.mult, [ptb] + RCB, kdB[ti * 4:ti * 4 + 4])
                pp, ppb = PS.next()
                pt, ptb = PS.next()
                ptv = psb_view(pt)
                for s_ in range(2):
                    sc = slice(2048 + 16 * s_, 2064 + 16 * s_)
                    for k in range(8):
                        MM(pp[0:16, s_ * 128:(s_ + 1) * 128], hT[:, k, sc], wv[:, 2, k, :], k == 0, k == 7, [w_b, hB[k][4]], [ppb], inc=(k == 7 and s_ == 1))
                    TR(ptv[0:16, s_ * 128:(s_ + 1) * 128], kT[:, sc], ident_b, [kB[4]] + CONSTS, [ptb], inc=(s_ == 1))
                CP("scalar", v_tm[0:16, 16:18, :], pp[0:16, 0:256].rearrange("p (a b) -> p a b", b=128), [ppb], vB[16:18])
                TT("vector", kd_tm[0:16, 16:18, :], ptv[0:16, 0:256].rearrange("p (a b) -> p a b", b=128),
                   KDs[0:16, hp, :].unsqueeze(1).broadcast_to([16, 2, 128]), ALU.mult, [ptb] + RCB, kdB[16:18])
                if _DBG_ENV.get("RET_STOP") == "B":
                    break
                P.tag = "ret.kv"
                kvps = []
                for ti in range(4):
                    pp, ppb = PS.next()
                    for j in range(4):
                        i = ti * 4 + j
                        MM(pp[:, j * 128:(j + 1) * 128], kd_tm[:, i, :], v_tm[:, i, :], True, True, [kdB[i], vB[i]], [ppb], inc=(j == 3))
                    kvps.append((pp, ppb))
                    for j in range(4):
                        i = ti * 4 + j
                        for h2 in range(2):
                            r = slice(64 * h2, 64 * h2 + 64)
                            STT(S_f[0][r, :], S_f[0][r, :], cdec[r, hp:hp + 1], pp[r, j * 128 + 64 * h2:j * 128 + 64 * h2 + 64], ALU.mult, ALU.add,
                                [ppb, SfB[0]] + RCB, [SfB[0]])
                        if i < 15:
                            CP("vector", S_bf[:, i + 1, :], S_f[0], [SfB[0]], [SbB[i + 1]])
                STORE(o_ret_p[l, 2 * hp:2 * hp + 2].rearrange("h k v -> (h k) v"), S_f[0], SfB[0])
                pp, ppb = PS.next()
                for s_ in range(2):
                    MM(pp[:, s_ * 128:(s_ + 1) * 128], kd_tm[0:16, 16 + s_, :], v_tm[0:16, 16 + s_, :], True, True, [kdB[16 + s_], vB[16 + s_]], [ppb], inc=(s_ == 1))
                for s_ in range(2):
                    for h2 in range(2):
                        r = slice(64 * h2, 64 * h2 + 64)
                        STT(S_f[1 + s_][r, :], S_f[1 + s_][r, :], cdecs[r, hp:hp + 1], pp[r, s_ * 128 + 64 * h2:s_ * 128 + 64 * h2 + 64], ALU.mult, ALU.add,
                            [ppb, SfB[1 + s_], SbB[16 + s_]] + RCB, [SfB[1 + s_]])
                    STORE(o_ret_s[l, s_, 2 * hp:2 * hp + 2].rearrange("h k v -> (h k) v"), S_f[1 + s_], SfB[1 + s_])
                if _DBG_ENV.get("RET_STOP") == "C":
                    break
                P.tag = "ret.attn"
                post_state = [None]
                for ti in range(5):
                    c0, n = TILES[ti]
                    po, pob = PSA.next()
                    rpend = []
                    nunit = 0
                    nchunk = 4 if ti < 4 else 2
                    cl = 128 if ti < 4 else 16
                    for j in range(nchunk):
                        i = ti * 4 + j if ti < 4 else 16 + j
                        cc = slice(c0 + j * cl, c0 + (j + 1) * cl)
                        for h2 in range(2):
                            r = slice(64 * h2, 64 * h2 + 64)
                            hd = 2 * hp + h2
                            ps_, psb_ = PS.next()
                            MM(ps_[0:cl, 0:cl], kT[r, cc], qT[r, cc], True, True, [kB[ti], qB[ti]], [psb_])
                            at, atb = attp.next()
                            dm = DmT[0:cl, hd, 0:cl] if ti < 4 else DmTs[0:cl, hd, 0:cl]
                            TT("vector", at[0:cl, 0:cl], ps_[0:cl, 0:cl], dm, ALU.mult, [psb_] + RCB, [atb])
                            nunit += 1
                            if nunit == 3 and post_state[0] is not None:
                                post_state[0]()
                                post_state[0] = None

                            def pend(po=po, pob=pob, r=r, j=j, cl=cl, i=i, at=at, atb=atb, cc=cc, ti=ti):
                                MM(po[r, j * cl:(j + 1) * cl], v_tm[0:cl, i, r], at[0:cl, 0:cl], True, False, [vB[i], atb], [pob], inc=False)
                                MM(po[r, j * cl:(j + 1) * cl], S_bf[r, i, :], qdT[r, cc], False, True, [SbB[i], qdB[ti]], [pob])
                            rpend.append(pend)
                            if len(rpend) > 2:
                                rpend.pop(0)()
                    while rpend:
                        rpend.pop(0)()
                    o_sb, o_sbb = t1p.next()
                    CP("scalar", o_sb[:, 0:n], po[:, 0:n], [pob], [o_sbb])
                    q_ap, q_b = sqp.next()
                    ACT(q_ap[:, 0:n], po[:, 0:n], AF.Square, [pob], [q_b])
                    p2, p2b = PS.next()
                    MM(p2[:, 0:n], BD, q_ap[:, 0:n], True, True, [q_b] + RCB, [p2b])

                    def post2(p2=p2, p2b=p2b, o_sb=o_sb, o_sbb=o_sbb, n=n, c0=c0, ti=ti, hp=hp):
                        r_ap, r_b = rsp.next()
                        ACT(r_ap[:, 0:n], p2[:, 0:n], AF.Ln, [p2b], [r_b], bias=EPS, scale=1.0 / 64)
                        ACT(r_ap[:, 0:n], r_ap[:, 0:n], AF.Exp, [r_b], [r_b], scale=-0.5)
                        a2, a2b = t2p.next()
                        STT(a2[:, 0:n], o_sb[:, 0:n], pcol(l, "ret_norm", hp), r_ap[:, 0:n], ALU.mult, ALU.mult, [o_sbb, r_b] + CONSTS, [a2b])
                        TT("vector", mixT[:, hp, c0:c0 + n], a2[:, 0:n], sgT[:, c0:c0 + n], ALU.mult, [a2b, sgB[ti]], [mixB[hp][ti]])
                    post_state[0] = post2
                if post_state[0] is not None:
                    post_state[0]()
                    post_state[0] = None
            if dbg == "ret":
                dump("retmix", mixT, [mixB[c][ti] for c in range(4) for ti in range(5)], BF16)
            xacc(l, 0, 4, mixT, mixB, "ret", preloaded=True)
            P.barrier()
            AR.reset(mR)

        def group_mla(l, hT, hB, hT_region):
            mR = AR.mark()
            mtab = AR.alloc(2 * T).rearrange("p (a t) -> p a t", a=2)
            mmat = AR.alloc(4 * 128).rearrange("p (a t) -> p a t", a=4)
            bMT, bMM = Buf("mtab"), Buf("mmat")
            LOAD("sync", mtab, c_mtab, bMT)
            LOAD("sync", mmat, c_mmat, bMM)
            MCB = [bMT, bMM]
            Cm, Sm = mtab[:, 0, :], mtab[:, 1, :]
            Pm96, ones96, mrow = mmat[:, 0, :], mmat[:, 1, :], mmat[:, 2:4, :]
            mixT = AR.alloc(4 * T).rearrange("p (c t) -> p c t", c=4)
            mixB = [[Buf("mm%d_%d" % (c, ti)) for ti in range(5)] for c in range(4)]
            cqnT = AR.alloc(3 * T).rearrange("p (c t) -> p c t", c=3)
            cqB = [Buf("cq%d" % ti) for ti in range(5)]
            ckvT = AR.alloc(2 * 2048).rearrange("p (c t) -> p c t", c=2)
            ckvB = [Buf("ckv%d" % i) for i in range(4)]
            ckvN = AR.alloc(2 * 32).rearrange("p (c t) -> p c t", c=2)
            ckvNB = Buf("ckvN")
            krT = AR.alloc(2048)
            krB = [Buf("kr%d" % i) for i in range(4)]
            krN = AR.alloc(32)
            krNB = Buf("krN")
            qT = AR.alloc(2048)
            qB = [Buf("mq%d" % i) for i in range(4)]
            kT = AR.alloc(2048)
            kB = [Buf("mk%d" % i) for i in range(4)]
            vA = AR.alloc(16 * 128).rearrange("p (i d) -> p i d", i=16)
            vB = [Buf("mv%d" % i) for i in range(16)]
            sqp = Pool([(AR.alloc(512), Buf("msq%d" % i)) for i in range(2)])
            rsp = Pool([(AR.alloc(512, F32), Buf("mrs%d" % i)) for i in range(1)])
            xnp = Pool([(AR.alloc(512, F32), Buf("mxn%d" % i)) for i in range(2)])
            kf_t1 = AR.alloc(512, F32)
            t1p = Pool([(kf_t1, Buf("mt1%d" % i)) for i in range(1)])
            t2p = Pool([(AR.alloc(512, F32), Buf("mt2%d" % i)) for i in range(1)])
            xbp = Pool([(AR.alloc(512), Buf("mxb%d" % i)) for i in range(2)])
            pbp = Pool([(AR.alloc(512), Buf("mpb%d" % i)) for i in range(2)])
            recp = Pool([(AR.alloc(512, F32), Buf("mrc%d" % i)) for i in range(1)])
            cfp = Pool([(AR.alloc(2 * 512, F32).rearrange("p (c t) -> p c t", c=2), Buf("mcf%d" % i)) for i in range(1)])
            kfp = Pool([(kf_t1, Buf("mkf%d" % i)) for i in range(1)])
            cstg = Pool([(AR.alloc(288, F32), Buf("mcs%d" % i)) for i in range(2)])
            ostg = Pool([(AR.alloc(288, F32), Buf("mos%d" % i)) for i in range(2)])
            MEMSET("vector", vA[:, :, 64:128], 1.0, vB)
            growK = AR.alloc(96, F32)
            bGR = Buf("growK")
            LOAD("sync", growK, prow[:, l, 32:128], bGR)
            krg = AR.alloc(64, F32)
            krgB = Buf("krg")
            ssrP = AR.alloc(16, F32)
            ssrPB = Buf("ssrP")
            ssrS = AR.alloc(2 * 9, F32).rearrange("p (s i) -> p s i", s=2)
            ssrSB = Buf("ssrS")
            ssq = AR.alloc(16 * 8, F32).rearrange("p (i h) -> p i h", i=16)
            ksc = AR.alloc(16 * 8, F32).rearrange("p (i h) -> p i h", i=16)
            ssqB, kscB = Buf("ssq"), Buf("ksc")
            krRB = [Buf("krR0"), Buf("krR1")]

            P.tag = "mla.M1"
            srcw = w_in[l].rearrange("(k p) n -> p k n", p=128)
            wa, wab = WA[0]
            wb, wbb = WA[1]
            wav = wa[:, 0:3072].rearrange("p (k n) -> p k n", k=8)
            wbv = wb[:, 0:2304].rearrange("p (k n) -> p k n", k=8)
            LOAD("gpsimd", wav, srcw[:, :, MLA0:MLA0 + 384], wab)
            LOAD("gpsimd", wbv, srcw[:, :, MLA0 + 384:MLA0 + 672], wbb)
            for ti in range(5):
                c0, n = TILES[ti]
                cs = slice(c0, c0 + n)
                pcs = []
                for c in range(3):
                    pp, ppb = PS.next()
                    for k in range(8):
                        MM(pp[:, 0:n], wav[:, k, c * 128:(c + 1) * 128], hT[:, k, cs], k == 0, k == 7, [wab, hB[k][ti]], [ppb], inc=(k == 7))
                    pcs.append((pp, ppb))
                rms_norm_fm([(pp[:, 0:n], [ppb]) for pp, ppb in pcs], 3, lambda c: pcol(l, "mla_q_norm", c),
                            lambda c, ti=ti, cs=cs: (cqnT[:, c, cs], [cqB[ti]]), n, 1.0 / 384, (sqp, rsp))
                pcs = []
                for c in range(2):
                    pp, ppb = PS.next()
                    for k in range(8):
                        MM(pp[:, 0:n], wbv[:, k, c * 128:(c + 1) * 128], hT[:, k, cs], k == 0, k == 7, [wbb, hB[k][ti]], [ppb], inc=(k == 7))
                    pcs.append((pp, ppb))
                cf, cfb = cfp.next()
                rms_norm_fm([(pp[:, 0:n], [ppb]) for pp, ppb in pcs], 2, lambda c: pcol(l, "mla_kv_norm", c),
                            lambda c, cf=cf, cfb=cfb, n=n: (cf[:, c, 0:n], [cfb]), n, 1.0 / 256, (sqp, rsp))
                if ti < 4:
                    CP("scalar", ckvT[:, :, cs], cf[:, :, 0:n], [cfb], [ckvB[ti]])
                else:
                    CP("scalar", ckvN[:, :, 0:n], cf[:, :, 0:n], [cfb], [ckvNB])
                pp, ppb = PS.next()
                for k in range(8):
                    MM(pp[0:32, 0:n], wbv[:, k, 256:288], hT[:, k, cs], k == 0, k == 7, [wbb, hB[k][ti]], [ppb], inc=(k == 7))
                kf, kfb = kfp.next()
                CP("vector", kf[0:32, 0:n], pp[0:32, 0:n], [ppb], [kfb])
                if ti < 4:
                    CP("scalar", krT[64:96, cs], pp[0:32, 0:n], [ppb], [krB[ti]])
                else:
                    CP("scalar", krN[64:96, 0:n], pp[0:32, 0:n], [ppb], [krNB])
                nb = 4 if ti < 4 else 1
                bw = 128 if ti < 4 else 32
                for j in range(nb):
                    pt, ptb = PS.next()
                    for c in range(2):
                        TR(pt[0:bw, c * 128:(c + 1) * 128], cf[:, c, j * bw:(j + 1) * bw], ident_f, [cfb] + CONSTS, [ptb], inc=False)
                    TR(pt[0:bw, 256:288], kf[0:32, j * bw:(j + 1) * bw], ident_f[0:32, 0:32], [kfb] + CONSTS, [ptb])
                    og, ogb = ostg.next()
                    CP("vector", og[0:bw, :], pt[0:bw, 0:288], [ptb], [ogb])
                    if ti < 4:
                        jt = ti * 4 + j
                        TT("vector", krg[:, 0:32], og[:, 256:288], growK[:, 64:96], ALU.mult, [ogb, bGR], [krgB])
                        P.op("scalar", lambda e, jt=jt: e.activation(out=krg[:, 32:64], in_=krg[:, 0:32], func=AF.Square, accum_out=ssrP[:, jt:jt + 1]),
                             reads=[krgB], writes=[krgB, ssrPB])
                    else:
                        for s2 in range(2):
                            pt2, pt2b = PS.next()
                            TR(pt2[0:16, 0:32], kf[0:32, 16 * s2:16 * s2 + 16], ident_f[0:32, 0:32], [kfb] + CONSTS, [pt2b])
                            TT("vector", krg[0:16, 0:32], pt2[0:16, 0:32], growK[0:16, 64:96], ALU.mult, [pt2b, bGR], [krgB])
                            P.op("scalar", lambda e, s2=s2: e.activation(out=krg[0:16, 32:64], in_=krg[0:16, 0:32], func=AF.Square, accum_out=ssrS[0:16, s2, 8:9]),
                                 reads=[krgB], writes=[krgB, ssrSB])
                    if ti < 4:
                        r0 = c0 + j * 128
                        STORE(o_ckv_p[l, r0:r0 + 128, :], og[:, 0:256], ogb)
                        STORE(o_kr_p[l, r0:r0 + 128, :], og[:, 256:288], ogb)
                    else:
                        STORE(o_ckv_s[l], og[0:32, 0:256], ogb)
                        STORE(o_kr_s[l], og[0:32, 256:288], ogb)

            P.barrier()
            AR2 = Arena(hT_region, 8 * T)
            qT2 = [qT, AR2.alloc(2048)]
            kT2 = [kT, AR2.alloc(2048)]
            vA2 = [vA, AR2.alloc(16 * 128).rearrange("p (i d) -> p i d", i=16)]
            qB2 = [qB, [Buf("mq1_%d" % i) for i in range(4)]]
            kB2 = [kB, [Buf("mk1_%d" % i) for i in range(4)]]
            vB2 = [vB, [Buf("mv1_%d" % i) for i in range(16)]]
            MEMSET("vector", vA2[1][:, :, 64:128], 1.0, vB2[1])
            sqp = Pool(sqp.slots + [(AR2.alloc(512), Buf("msq2%d" % i)) for i in range(1)])
            rsp = Pool(rsp.slots + [(AR2.alloc(512, F32), Buf("mrs2%d" % i)) for i in range(1)])
            xnp = Pool(xnp.slots + [(AR2.alloc(512, F32), Buf("mxn2%d" % i)) for i in range(1)])
            t1p = Pool([(AR2.alloc(512, F32), Buf("mt12%d" % i)) for i in range(2)])
            t2p = Pool(t2p.slots + [(AR2.alloc(512, F32), Buf("mt22%d" % i)) for i in range(1)])
            xbp = Pool(xbp.slots + [(AR2.alloc(512), Buf("mxb2%d" % i)) for i in range(1)])
            pbp = Pool(pbp.slots + [(AR2.alloc(512), Buf("mpb2%d" % i)) for i in range(2)])
            recp = Pool(recp.slots + [(AR2.alloc(512, F32), Buf("mrc2%d" % i)) for i in range(1)])
            P.tag = "mla.w"
            wqv = wa[:, 0:2304].rearrange("p (k n) -> p k n", k=3)
            wkv = wb[:, 0:2048].rearrange("p (k n) -> p k n", k=2)
            LOAD("gpsimd", wqv, w_uq[l].rearrange("(k p) n -> p k n", p=128), wab)
            LOAD("gpsimd", wkv, w_ukv[l].rearrange("(k p) n -> p k n", p=128), wbb)
            wkv4 = wkv.rearrange("p k (h t d) -> p k h t d", h=8, t=2)
            for c in range(2):
                TT("vector", wkv4[:, c, :, 0, :], wkv4[:, c, :, 0, :], growK[:, 0:64].unsqueeze(1).broadcast_to([128, 8, 64]), ALU.mult, [wbb, bGR], [wbb])

            def normrope(ps, psb, n, gname, pos0, dst, dstb, psp):
                q_ap, q_b = sqp.next()
                ACT(q_ap[0:96, 0:n], ps[0:96, 0:n], AF.Square, [psb], [q_b])
                yield
                p2, p2b = psp.next()
                MM(p2[0:96, 0:n], ones96[0:96, 0:96], q_ap[0:96, 0:n], True, True, [q_b] + MCB, [p2b])
                yield
                r_ap, r_b = rsp.next()
                ACT(r_ap[0:96, 0:n], p2[0:96, 0:n], AF.Ln, [p2b], [r_b], bias=EPS, scale=1.0 / 96)
                ACT(r_ap[0:96, 0:n], r_ap[0:96, 0:n], AF.Exp, [r_b], [r_b], scale=-0.5)
                yield
                xn, xnb = xnp.next()
                STT(xn[0:96, 0:n], ps[0:96, 0:n], pcol(l, gname, 0)[0:96], r_ap[0:96, 0:n], ALU.mult, ALU.mult, [psb, r_b] + CONSTS, [xnb])
                yield
                CP("vector", dst[0:64, 0:n], xn[0:64, 0:n], [xnb], dstb)
                xb, xbb = xbp.next()
                CP("vector", xb[64:96, 0:n], xn[64:96, 0:n], [xnb], [xbb])
                yield
                p3, p3b = psp.next()
                MM(p3[64:96, 0:n], Pm96[64:96, 64:96], xb[64:96, 0:n], True, True, [xbb] + MCB, [p3b])
                yield
                a1, a1b = t1p.next()
                a2, a2b = t2p.next()
                TT("vector", a1[64:96, 0:n], xn[64:96, 0:n], Cm[64:96, pos0:pos0 + n], ALU.mult, [xnb] + MCB, [a1b])
                TT("vector", a2[64:96, 0:n], p3[64:96, 0:n], Sm[64:96, pos0:pos0 + n], ALU.mult, [p3b] + MCB, [a2b])
                yield
                TT("vector", dst[64:96, 0:n], a1[64:96, 0:n], a2[64:96, 0:n], ALU.add, [a1b, a2b], dstb)

            SC = 96.0 ** -0.5
            for st_ in range(3):
                if st_ == 0:
                    ktiles = [(i * 512, 512, i * 512, i) for i in range(4)]
                    nk = 2048
                    qcol0, nq = 0, 2048
                else:
                    s_ = st_ - 1
                    P.tag = "mla.past"
                    for i in range(8):
                        cg, cgb = cstg.next()
                        LOAD("sync", cg[:, 0:256], c_ckv[l, s_, i * 128:(i + 1) * 128, :], cgb)
                        LOAD("sync", cg[:, 256:288], c_kr[l, s_, i * 128:(i + 1) * 128, :], cgb)
                        pt, ptb = PS.next()
                        for c in range(2):
                            TR(pt[:, c * 128:(c + 1) * 128], cg[:, c * 128:(c + 1) * 128], ident_f, [cgb] + CONSTS, [ptb], inc=False)
                        TR(pt[0:32, 256:384], cg[:, 256:288], ident_f, [cgb] + CONSTS, [ptb])
                        CP("scalar", ckvT[:, :, i * 128:(i + 1) * 128], pt[:, 0:256].rearrange("p (c t) -> p c t", c=2), [ptb], [ckvB[i // 4]])
                        CP("vector", krT[64:96, i * 128:(i + 1) * 128], pt[0:32, 256:384], [ptb], [krB[i // 4]])
                        TT("vector", krg[:, 0:32], cg[:, 256:288], growK[:, 64:96], ALU.mult, [cgb, bGR], [krgB])
                        P.op("scalar", lambda e, i=i, s_=s_: e.activation(out=krg[:, 32:64], in_=krg[:, 0:32], func=AF.Square, accum_out=ssrS[:, s_, i:i + 1]),
                             reads=[krgB], writes=[krgB, ssrSB])
                    CP("scalar", ckvT[:, :, 1024:1040], ckvN[:, :, 16 * s_:16 * s_ + 16], [ckvNB], [ckvB[2]])
                    CP("scalar", krT[64:96, 1024:1040], krN[64:96, 16 * s_:16 * s_ + 16], [krNB], [krB[2]])
                    ktiles = [(0, 512, 0, 0), (512, 512, 512, 1), (1024, 16, 2048, 2)]
                    nk = 1040
                    qcol0, nq = 2048 + 16 * s_, 16
                P.tag = "mla.kset"
                ntile = 16 if st_ == 0 else 9
                for jt in range(ntile):
                    kn = 128 if (st_ == 0 or jt < 8) else 16
                    pk, pkb = PS.next()
                    for c in range(2):
                        MM(pk[0:kn, :], ckvT[:, c, jt * 128:jt * 128 + kn], wkv4[:, c, :, 0, :], c == 0, c == 1, [wbb, ckvB[jt // 4]], [pkb], inc=(c == 1))
                    q_ap, q_b = xnp.next()
                    ACT(q_ap[0:kn, :], pk[0:kn, :], AF.Square, [pkb], [q_b])
                    P.op("vector", lambda e, q_ap=q_ap, kn=kn, jt=jt: e.tensor_reduce(out=ssq[0:kn, jt, :], in_=q_ap[0:kn, :].rearrange("p (h d) -> p h d", h=8),
                                                                                   axis=mybir.AxisListType.X, op=ALU.add), reads=[q_b], writes=[ssqB])
                parts = [(128, 0, 16)] if st_ == 0 else [(128, 0, 8), (16, 8, 9)]
                for (pr, t0_, t1_) in parts:
                    ssr_v = ssrP[0:pr, t0_:t1_] if st_ == 0 else ssrS[0:pr, st_ - 1, t0_:t1_]
                    kv_ = ksc[0:pr, t0_:t1_, :]
                    TT("vector", kv_, ssq[0:pr, t0_:t1_, :], ssr_v.unsqueeze(2).broadcast_to([pr, t1_ - t0_, 8]), ALU.add, [ssqB, ssrPB, ssrSB], [kscB])
                    ACT(kv_, kv_, AF.Ln, [kscB], [kscB], bias=EPS, scale=1.0 / 96)
                    ACT(kv_, kv_, AF.Exp, [kscB], [kscB], scale=-0.5, bias=math.log(SC))
                for (k0, n, pos0, bi) in ktiles:
                    xn, xnb = xnp.next()
                    TS("vector", xn[64:96, 0:n], krT[64:96, k0:k0 + n], pcol(l, "mla_k_gain", 0)[64:96], None, ALU.mult, ALU.bypass, [krB[bi]] + CONSTS, [xnb])
                    xb, xbb = xbp.next()
                    CP("scalar", xb[64:96, 0:n], xn[64:96, 0:n], [xnb], [xbb])
                    p3, p3b = PS.next()
                    MM(p3[64:96, 0:n], Pm96[64:96, 64:96], xb[64:96, 0:n], True, True, [xbb] + MCB, [p3b])
                    a1, a1b = t1p.next()
                    a2, a2b = t2p.next()
                    TT("vector", a1[64:96, 0:n], xn[64:96, 0:n], Cm[64:96, pos0:pos0 + n], ALU.mult, [xnb] + MCB, [a1b])
                    TT("vector", a2[64:96, 0:n], p3[64:96, 0:n], Sm[64:96, pos0:pos0 + n], ALU.mult, [p3b] + MCB, [a2b])
                    for b in range(2):
                        TT("vector", kT2[b][64:96, k0:k0 + n], a1[64:96, 0:n], a2[64:96, 0:n], ALU.add, [a1b, a2b], [krRB[b]])
                NFILL = int(_DBG_ENV.get("MK_FILL", "0"))
                PSP = Pool(psum[5:8] if NFILL == 0 else psum[5:7])
                fillB = Buf("fill")

                def filler():
                    for _ in range(NFILL):
                        MM(psum[7][0][:, 0:512], ident_b, cqnT[:, 0, 0:512], True, True, [], [fillB], inc=False)
                PSC = Pool(psum[2:5])
                def prep_items(h, st_=st_, ktiles=ktiles, qcol0=qcol0):
                    b = h % 2
                    qT_, kT_, vA_ = qT2[b], kT2[b], vA2[b]
                    items = []

                    def q_item(ti):
                        P.tag = "mla.q"
                        if st_ == 0:
                            c0 = ti * 512
                            pp, ppb = PSP.next()
                            for k in range(3):
                                MM(pp[0:96, 0:512], wqv[:, k, h * 96:(h + 1) * 96], cqnT[:, k, c0:c0 + 512], k == 0, k == 2, [wab, cqB[ti]], [ppb], inc=(k == 2))
                            yield
                            for _ in normrope(pp, ppb, 512, "mla_q_gain", c0, qT_[:, c0:c0 + 512], [qB2[b][ti]], PSP):
                                yield
                        else:
                            pp, ppb = PSP.next()
                            for k in range(3):
                                MM(pp[0:96, 0:16], wqv[:, k, h * 96:(h + 1) * 96], cqnT[:, k, qcol0:qcol0 + 16], k == 0, k == 2, [wab, cqB[4]], [ppb], inc=(k == 2))
                            yield
                            for _ in normrope(pp, ppb, 16, "mla_q_gain", 2048, qT_[:, 0:16], [qB2[b][0]], PSP):
                                yield

                    def k_item(kt):
                        P.tag = "mla.kv"
                        (k0, n, pos0, bi) = kt
                        pp, ppb = PSP.next()
                        for c in range(2):
                            MM(pp[0:64, 0:n], wkv4[:, c, h, 0, :], ckvT[:, c, k0:k0 + n], c == 0, c == 1, [wbb, ckvB[bi]], [ppb], inc=(c == 1))
                        yield
                        P.tag = "mla.kv"
                        CP("vector", kT_[0:64, k0:k0 + n], pp[0:64, 0:n], [ppb], [kB2[b][bi]])

                    def v_item(kt):
                        P.tag = "mla.kv"
                        (k0, n, pos0, bi) = kt
                        pv, pvb = PSP.next()
                        nt = (n + 127) // 128
                        for j in range(nt):
                            kn = min(128, n - j * 128)
                            for c in range(2):
                                MM(pv[0:kn, j * 64:(j + 1) * 64], ckvT[:, c, k0 + j * 128:k0 + j * 128 + kn], wkv4[:, c, h, 1, :], c == 0, c == 1,
                                   [wbb, ckvB[bi]], [pvb], inc=(c == 1 and j == nt - 1))
                        yield
                        P.tag = "mla.kv"
                        t0 = k0 // 128
                        kn0 = min(128, n)
                        CP("vector", vA_[0:kn0, t0:t0 + nt, 0:64], pv[0:kn0, 0:nt * 64].rearrange("p (a b) -> p a b", b=64), [pvb], vB2[b][t0:t0 + nt])
                    for ti in range(4 if st_ == 0 else 1):
                        items.append(q_item(ti))
                    for kt in ktiles:
                        items.append(k_item(kt))
                        items.append(v_item(kt))
                    return items

                def step(nxt):
                    while nxt:
                        try:
                            next(nxt[0])
                            return
                        except StopIteration:
                            nxt.pop(0)

                fin_state = {"f": None}

                def attention(h, nxt, st_=st_, qcol0=qcol0):
                    b = h % 2
                    qT_, kT_, vA_ = qT2[b], kT2[b], vA2[b]
                    it = 0
                    r = slice(64 * (h % 2), 64 * (h % 2) + 64)
                    if st_ == 0:
                        for g in range(4):
                            acc, accb = PSA.next()
                            last = 4 * g + 3
                            pendq = []
                            PV_DEPTH = int(_DBG_ENV.get("MK_PVD", "2"))
                            for j in range(last + 1):
                                P.tag = "mla.attn"
                                d_ = j - 4 * g
                                lo = 128 * d_ if d_ > 0 else 0
                                n = 512 - lo
                                sc, scb = PSC.next()
                                MM(sc[:, 0:n], kT_[0:96, j * 128:(j + 1) * 128], qT_[0:96, 512 * g + lo:512 * g + 512], True, d_ < 0,
                                   [kB2[b][j // 4], krRB[b], qB2[b][g]], [scb], inc=(d_ < 0))
                                if d_ >= 0:
                                    MM(sc[:, 0:128], mrow[0:1, 0, :], mrow[0:1, 1, :], False, True, MCB, [scb])
                                pb, pbb = pbp.next()
                                ACT(pb[:, 0:n], sc[:, 0:n], AF.Exp, [scb, kscB], [pbb], scale=ksc[:, j, h:h + 1])
                                pendq.append(lambda acc=acc, accb=accb, lo=lo, n=n, j=j, pb=pb, pbb=pbb, last=last:
                                             MM(acc[:, lo:512], vA_[:, j, :], pb[:, 0:n], j == 0, j == last, [vB2[b][j], pbb], [accb], inc=(j == last)))
                                if len(pendq) > PV_DEPTH:
                                    pendq.pop(0)()
                                it += 1
                                step(nxt)
                                if it % 2 == 0:
                                    step(nxt)
                            while pendq:
                                pendq.pop(0)()

                            def fin(acc=acc, accb=accb, r=r, h=h, g=g):
                                P.tag = "mla.attn"
                                rc_, rcb = recp.next()
                                ACT(rc_[64:128, :], acc[64:128, :], AF.Ln, [accb], [rcb])
                                ACT(rc_[64:128, :], rc_[64:128, :], AF.Exp, [rcb], [rcb], scale=-1.0)
                                TT("vector", mixT[r, h // 2, 512 * g:512 * g + 512], acc[0:64, :], rc_[64:128, :], ALU.mult, [accb, rcb], [mixB[h // 2][g]])
                            fin()
                    else:
                        acc, accb = PSA.next()
                        pend = None
                        for j in range(9):
                            P.tag = "mla.attn"
                            kn = 128 if j < 8 else 16
                            sc, scb = PSC.next()
                            MM(sc[0:kn, 0:16], kT_[0:96, j * 128:j * 128 + kn], qT_[0:96, 0:16], True, True, [kB2[b][j // 4], krRB[b], qB2[b][0]], [scb])
                            pb, pbb = pbp.next()
                            ACT(pb[0:kn, 0:16], sc[0:kn, 0:16], AF.Exp, [scb, kscB], [pbb], scale=ksc[0:kn, j, h:h + 1])
                            if pend is not None:
                                pend()
                            pend = (lambda acc=acc, accb=accb, kn=kn, j=j, pb=pb, pbb=pbb:
                                    MM(acc[:, 0:16], vA_[0:kn, j, :], pb[0:kn, 0:16], j == 0, j == 8, [vB2[b][j], pbb], [accb], inc=(j == 8)))
                            step(nxt)
                            step(nxt)
                        pend()
                        P.tag = "mla.attn"
                        rc_, rcb = recp.next()
                        ACT(rc_[64:128, 0:16], acc[64:128, 0:16], AF.Ln, [accb], [rcb])
                        ACT(rc_[64:128, 0:16], rc_[64:128, 0:16], AF.Exp, [rcb], [rcb], scale=-1.0)
                        TT("vector", mixT[r, h // 2, qcol0:qcol0 + 16], acc[0:64, 0:16], rc_[64:128, 0:16], ALU.mult, [accb, rcb], [mixB[h // 2][4]])
                    while nxt:
                        step(nxt)

                for g_ in prep_items(0):
                    for _ in g_:
                        pass
                for h in range(8):
                    attention(h, prep_items(h + 1) if h < 7 else [])
                if fin_state["f"] is not None:
                    fin_state["f"]()
                    fin_state["f"] = None
            if dbg == "mla":
                dump("mlamix", mixT, [mixB[c][ti] for c in range(4) for ti in range(5)], BF16)
            xacc(l, 512, 4, mixT, mixB, "mla")
            P.barrier()
            AR.reset(mR)

        def group_ssm(l, hT, hB):
            mR = AR.mark()
            U = AR.alloc(128, F32)
            NEG = AR.alloc(640)
            ones_f = AR.alloc(128, F32)
            MEMSET("vector", ones_f, 1.0, [Buf("onesf")])
            prow_sb = AR.alloc(128, F32)
            bSC = Buf("ssmc")
            LOAD("sync", U, c_U, bSC)
            LOAD("sync", NEG, c_NEG, bSC)
            LOAD("sync", prow_sb, prow[:, l, :], bSC)
            SCB = [bSC]
            dtv = AR.alloc(18 * 16, F32).rearrange("p (i h) -> p i h", i=18)
            dtA = AR.alloc(18 * 16, F32).rearrange("p (i h) -> p i h", i=18)
            nacum = AR.alloc(18 * 16, F32).rearrange("p (i h) -> p i h", i=18)
            ea = AR.alloc(16, F32)
            bDT = Buf("dt")
            convout = AR.alloc(3 * 12 * 3, F32).rearrange("p (q c r) -> p q c r", q=3, c=12)
            bCO = Buf("convout")
            srcw = w_in[l].rearrange("(k p) n -> p k n", p=128)

            P.tag = "ssm.dt"
            wa, wab = WA[0]
            wdt = wa[:, 0:128].rearrange("p (k n) -> p k n", k=8)
            LOAD("gpsimd", wdt, srcw[:, :, SSM0 + 2560:SSM0 + 2576], wab)
            pd, pdb = PSA.next()
            for i in range(18):
                if i < 16:
                    cols, rows, ti = slice(i * 128, (i + 1) * 128), 128, i // 4
                else:
                    cols, rows, ti = slice(2048 + 16 * (i - 16), 2064 + 16 * (i - 16)), 16, 4
                for k in range(8):
                    MM(pd[0:rows, i * 16:(i + 1) * 16], hT[:, k, cols], wdt[:, k, :], k == 0, k == 7, [wab, hB[k][ti]], [pdb], inc=(k == 7 and i == 17))
            pdv = pd[:, 0:288].rearrange("p (i h) -> p i h", i=18)
            ACT(ea, prow_sb[:, 16:32], AF.Exp, SCB, [bDT])
            for (pr, sl_, ns) in ((128, slice(0, 16), 16), (16, slice(16, 18), 2)):
                TT("vector", dtv[0:pr, sl_, :], pdv[0:pr, sl_, :], prow_sb[0:pr, 0:16].unsqueeze(1).broadcast_to([pr, ns, 16]), ALU.add, [pdb] + SCB, [bDT])
                ACT(dtv[0:pr, sl_, :], dtv[0:pr, sl_, :], AF.Exp, [bDT], [bDT])
                ACT(dtv[0:pr, sl_, :], dtv[0:pr, sl_, :], AF.Ln, [bDT], [bDT], bias=1.0)
                STT(dtA[0:pr, sl_, :], dtv[0:pr, sl_, :], -1.0, ea[0:pr].unsqueeze(1).broadcast_to([pr, ns, 16]), ALU.mult, ALU.mult, [bDT], [bDT])
            pc_, pcb = PS.next()
            MM(pc_[:, 0:256], U, dtA[:, 0:16, :].rearrange("p i h -> p (i h)"), True, True, [bDT] + SCB, [pcb])
            MM(pc_[0:16, 256:288], U[0:16, 0:16], dtA[0:16, 16:18, :].rearrange("p i h -> p (i h)"), True, True, [bDT] + SCB, [pcb])
            P.op("scalar", lambda e: e.activation(out=nacum[:, 0:16, :].rearrange("p i h -> p (i h)"), in_=pc_[:, 0:256], func=AF.Copy, scale=-1.0), reads=[pcb], writes=[bDT])
            P.op("scalar", lambda e: e.activation(out=nacum[0:16, 16:18, :].rearrange("p i h -> p (i h)"), in_=pc_[0:16, 256:288], func=AF.Copy, scale=-1.0), reads=[pcb], writes=[bDT])

            mG = AR.mark()
            for g in range(2):
                AR.reset(mG)
                xsT = AR.alloc(4 * T).rearrange("p (c t) -> p c t", c=4)
                zsT = AR.alloc(4 * T).rearrange("p (c t) -> p c t", c=4)
                BT, CT = AR.alloc(T), AR.alloc(T)
                xsB = [[Buf("xs%d_%d" % (c, ti)) for ti in range(5)] for c in range(4)]
                zB = [[Buf("zs%d_%d" % (c, ti)) for ti in range(5)] for c in range(4)]
                BB = [Buf("B%d" % ti) for ti in range(5)]
                CB = [Buf("C%d" % ti) for ti in range(5)]
                B_tm = AR.alloc(18 * 128).rearrange("p (i d) -> p i d", i=18)
                BtB = [Buf("Bt%d" % i) for i in range(18)]
                mPh = AR.mark()
                NR = 4
                Rp = [(AR.alloc(520, F32), Buf("R%d" % i)) for i in range(NR)]
                accp = Pool([(AR.alloc(512, F32), Buf("ca%d" % i)) for i in range(4)])
                P.tag = "ssm.z"
                wz, wzb = WA[1]
                wzv = wz.rearrange("p (k n) -> p k n", k=8)
                LOAD("gpsimd", wzv, srcw[:, :, SSM0 + 512 * g:SSM0 + 512 * g + 512], wzb)
                zunits = []
                for c in range(4):
                    for ti in range(5):
                        def zunit(c=c, ti=ti):
                            P.tag = "ssm.z"
                            c0, n = TILES[ti]
                            pp, ppb = PS.next()
                            for k in range(8):
                                MM(pp[:, 0:n], wzv[:, k, c * 128:(c + 1) * 128], hT[:, k, c0:c0 + n], k == 0, k == 7, [wzb, hB[k][ti]], [ppb], inc=(k == 7))
                            ACT(zsT[:, c, c0:c0 + n], pp[:, 0:n], AF.Silu, [ppb], [zB[c][ti]])
                            P.tag = "ssm.conv"
                        zunits.append(zunit)
                P.tag = "ssm.conv"
                wx, wxb = WA[0]
                wxv = wx.rearrange("p (k n) -> p k n", k=8)
                LOAD("gpsimd", wxv, srcw[:, :, SSM0 + 1024 + 512 * g:SSM0 + 1024 + 512 * g + 512], wxb)
                wbc, wbcb = WA[1]
                wbcv = wbc[:, 0:2048].rearrange("p (k n) -> p k n", k=8)
                ri = 0
                conv_pend = [None]
                for q in range(6):
                    if q == 4:
                        while zunits:
                            zunits.pop(0)()
                        LOAD("gpsimd", wbcv[:, :, 0:128], srcw[:, :, SSM0 + 2048 + 128 * g:SSM0 + 2048 + 128 * g + 128], wbcb)
                        LOAD("gpsimd", wbcv[:, :, 128:256], srcw[:, :, SSM0 + 2304 + 128 * g:SSM0 + 2304 + 128 * g + 128], wbcb)
                    if q < 4:
                        lw = lambda k, q=q: wxv[:, k, q * 128:(q + 1) * 128]
                        lwb = wxb
                        cc = 4 * g + q
                        dstf = lambda cs, q=q: xsT[:, q, cs]
                        dB = xsB[q]
                    else:
                        lw = lambda k, q=q: wbcv[:, k, (q - 4) * 128:(q - 3) * 128]
                        lwb = wbcb
                        cc = 8 + 2 * (q - 4) + g
                        dstf = (lambda cs: BT[:, cs]) if q == 4 else (lambda cs: CT[:, cs])
                        dB = BB if q == 4 else CB
                    segs = [(ti, TILES[ti][0], 512, 0) for ti in range(4)] + [(4, 2048, 16, 1), (4, 2064, 16, 2)]
                    for (ti, c0, n, sq_) in segs:
                        pp, ppb = PS.next()
                        for k in range(8):
                            MM(pp[:, 0:n], lw(k), hT[:, k, c0:c0 + n], k == 0, k == 7, [lwb, hB[k][ti]], [ppb], inc=(k == 7))
                        R, Rb = Rp[ri % NR]
                        Rprev, Rpb = Rp[(ri - 1) % NR]
                        ri += 1
                        if sq_ == 0 and ti == 0:
                            MEMSET("vector", R[:, 0:3], 0.0, [Rb])
                        elif sq_ == 0:
                            CP("vector", R[:, 0:3], Rprev[:, 512:515], [Rpb], [Rb])
                        else:
                            LOAD("sync", R[:, 0:3], c_conv[l, sq_ - 1].rearrange("r c -> c r")[cc * 128:(cc + 1) * 128, :], Rb, slow=True)
                        CP("scalar", R[:, 3:3 + n], pp[:, 0:n], [ppb], [Rb])
                        if (sq_ == 0 and ti == 3) or sq_ > 0:
                            CP("vector", convout[:, sq_, cc, :], R[:, n:n + 3], [Rb], [bCO])
                        a_, ab = accp.next()
                        ACT(a_[:, 0:n], pp[:, 0:n], AF.Copy, [ppb] + CONSTS, [ab], scale=pcol(l, "conv_w", cc * 4 + 3))
                        for w_ in range(0, 3):
                            STT(a_[:, 0:n], R[:, w_:w_ + n], pcol(l, "conv_w", cc * 4 + w_), a_[:, 0:n], ALU.mult, ALU.add, [Rb, ab] + CONSTS, [ab])
                        if conv_pend[0] is not None:
                            conv_pend[0]()
                        conv_pend[0] = (lambda d_=dstf(slice(c0, c0 + n)), a_=a_, n=n, ab=ab, db_=dB[ti], cc=cc:
                                        ACT(d_, a_[:, 0:n], AF.Silu, [ab] + CONSTS, [db_], bias=pcol(l, "conv_b", cc)))
                        if zunits:
                            zunits.pop(0)()
                conv_pend[0]()
                conv_pend[0] = None
                xacc_prefetch(l, 1024 + 512 * g)
                P.tag = "ssm.Btm"
                for ti in range(4):
                    pt, ptb = PS.next()
                    ptv = pt.bitcast(BF16)
                    for j in range(4):
                        i = ti * 4 + j
                        TR(ptv[:, j * 128:(j + 1) * 128], BT[:, i * 128:(i + 1) * 128], ident_b, [BB[ti]] + CONSTS, [ptb], inc=(j == 3))
                    CP("scalar", B_tm[:, ti * 4:ti * 4 + 4, :], ptv[:, 0:512].rearrange("p (a b) -> p a b", b=128), [ptb], BtB[ti * 4:ti * 4 + 4])
                pt, ptb = PS.next()
                ptv = pt.bitcast(BF16)
                for s_ in range(2):
                    TR(ptv[0:16, s_ * 128:(s_ + 1) * 128], BT[:, 2048 + 16 * s_:2064 + 16 * s_], ident_b, [BB[4]] + CONSTS, [ptb], inc=(s_ == 1))
                CP("scalar", B_tm[0:16, 16:18, :], ptv[0:16, 0:256].rearrange("p (a b) -> p a b", b=128), [ptb], BtB[16:18])

                if dbg == "ssm" and _DBG_ENV.get("SSM_DUMP"):
                    dump("xs%d" % g, xsT, [xsB[c][ti] for c in range(4) for ti in range(5)], BF16)
                    dump("zs%d" % g, zsT, [zB[c][ti] for c in range(4) for ti in range(5)], BF16)
                    dump("B%d" % g, BT, BB, BF16)
                    dump("C%d" % g, CT, CB, BF16)
                P.tag = "ssm.scan"
                P.barrier()
                AR.reset(mPh)
                xdtp = Pool([(AR.alloc(512).rearrange("p (h d) -> p h d", h=8), Buf("xdt%d" % i)) for i in range(2)])
                xwp = Pool([(AR.alloc(512).rearrange("p (h d) -> p h d", h=8), Buf("xw%d" % i)) for i in range(2)])
                UdA = AR.alloc(1024, F32)
                UdAB = Buf("UdA")
                Ea = AR.alloc(1024, F32)
                EaB = Buf("Ea")
                Da = AR.alloc(1024, F32)
                DaB = Buf("Da")
                scp = Pool([(AR.alloc(1024), Buf("sc%d" % i)) for i in range(2)])
                cdp = Pool([(AR.alloc(1024), Buf("cd%d" % i)) for i in range(2)])
                hs = AR.alloc(512, F32).rearrange("p (h d) -> p h d", h=8)
                hsB = Buf("hs")
                hsbf = [(AR.alloc(512), Buf("hsbf%d" % i)) for i in range(2)]
                ydp = Pool([(AR.alloc(512, F32), Buf("yd%d" % i)) for i in range(1)])
                hstg = Pool([(AR.alloc(128, F32), Buf("hst%d" % i)) for i in range(2)])
                PSB = Pool(psum[2:5])
                PSD = Pool(psum[5:8])

                def state_out(dst_ap):
                    for c in range(4):
                        pt, ptb = PSB.next()
                        TR(pt[:, 0:128], hs[:, 2 * c:2 * c + 2, :].rearrange("p h d -> p (h d)"), ident_f, [hsB] + CONSTS, [ptb])
                        sg_, sgb = hstg.next()
                        CP("scalar", sg_, pt[:, 0:128], [ptb], [sgb])
                        STORE(dst_ap[2 * c:2 * c + 2].rearrange("h p n -> (h p) n"), sg_, sgb)

                Eap = Pool([(Ea, EaB), (AR.alloc(1024, F32), Buf("Ea1"))])
                hstate = {"cur": 0}

                def phaseA(i):
                    if i < 16:
                        cl, cols, ti = 128, slice(i * 128, (i + 1) * 128), i // 4
                    else:
                        cl, cols, ti = 16, slice(2048 + 16 * (i - 16), 2064 + 16 * (i - 16)), 4
                    W = 8 * cl
                    gb, gbb = PSA.next()
                    MM(gb[0:cl, 0:cl], BT[:, cols], CT[:, cols], True, True, [BB[ti], CB[ti]], [gbb])
                    pt, ptb = PSB.next()
                    ptv = pt.bitcast(BF16)
                    for c in range(4):
                        TR(ptv[0:cl, c * 128:(c + 1) * 128], xsT[:, c, cols], ident_b, [xsB[c][ti]] + CONSTS, [ptb], inc=(c == 3))
                    xdt, xdtb = xdtp.next()
                    TT("vector", xdt[0:cl], ptv[0:cl, 0:512].rearrange("p (h d) -> p h d", h=8),
                       dtv[0:cl, i, 8 * g:8 * g + 8].unsqueeze(2).broadcast_to([cl, 8, 64]), ALU.mult, [ptb, bDT], [xdtb])
                    nb = 2 if cl == 128 else 1
                    bw = W // nb
                    bcs = [PSD.next() for _ in range(nb)]
                    for h in range(8):
                        bc, bcb = bcs[(h * cl) // bw]
                        o_ = (h * cl) % bw
                        MM(bc[:, o_:o_ + cl], dtA[0:cl, i, 8 * g + h:8 * g + h + 1].broadcast_to([cl, 128]), U[0:cl, 0:cl], o_ == 0, True, [bDT] + SCB, [bcb],
                           inc=(h % (8 // nb) == (8 // nb) - 1), sgc=True)
                    Ea_, EaB_ = Eap.next()
                    for hb_, (bc, bcb) in enumerate(bcs):
                        ACT(Ea_[:, hb_ * bw:(hb_ + 1) * bw], bc[:, 0:bw], AF.Exp, [bcb], [EaB_])
                    negt = NEG[0:cl, 0:512] if cl == 128 else NEG[0:cl, 512:640]
                    for hb_, (bc, bcb) in enumerate(bcs):
                        MM(bc[0:cl, 0:bw], ident_b[0:cl, 0:cl], negt, False, True, SCB + CONSTS + [EaB_], [bcb], sgc=True)
                    Da3 = Da[0:cl, 0:W].rearrange("p (h t) -> p h t", h=8)
                    Ea3 = Ea_[:, 0:W].rearrange("p (h t) -> p h t", h=8)
                    for h in range(8):
                        bc, bcb = bcs[(h * cl) // bw]
                        o_ = (h * cl) % bw
                        ACT(Da3[:, h, :], bc[0:cl, o_:o_ + cl], AF.Exp, [bcb, bDT], [DaB], bias=nacum[0:cl, i, 8 * g + h:8 * g + h + 1])
                    return dict(cl=cl, cols=cols, ti=ti, W=W, xdt=xdt, xdtb=xdtb, Ea3=Ea3, EaB=EaB_, Da3=Da3, gb=gb, gbb=gbb)

                def phaseA2(cx):
                    cl, cols, ti, W = cx["cl"], cx["cols"], cx["ti"], cx["W"]
                    sc_, scb_ = scp.next()
                    sc3 = sc_[0:cl, 0:W].rearrange("p (h t) -> p h t", h=8)
                    TT("vector", sc3, cx["gb"][0:cl, 0:cl].unsqueeze(1).broadcast_to([cl, 8, cl]), cx["Da3"], ALU.mult, [cx["gbb"], DaB], [scb_])
                    cd_, cdb_ = cdp.next()
                    cd3 = cd_[:, 0:W].rearrange("p (h t) -> p h t", h=8)
                    TT("vector", cd3, CT[:, cols].unsqueeze(1).broadcast_to([128, 8, cl]), cx["Ea3"], ALU.mult, [CB[ti], cx["EaB"]], [cdb_])
                    xw, xwb = xwp.next()
                    TT("vector", xw[0:cl], cx["xdt"][0:cl], cx["Da3"][:, :, cl - 1:cl].broadcast_to([cl, 8, 64]), ALU.mult, [cx["xdtb"], DaB], [xwb])
                    cx.update(sc3=sc3, scb=scb_, cd3=cd3, cdb=cdb_, xw=xw, xwb=xwb)

                def phaseB(i, cx):
                    cl, cols, ti = cx["cl"], cx["cols"], cx["ti"]
                    if i == 0:
                        MEMSET("vector", hs, 0.0, [hsB])
                        MEMSET("vector", hsbf[hstate["cur"]][0], 0.0, [hsbf[hstate["cur"]][1]])
                    if i >= 16:
                        for c in range(4):
                            sg_, sgb = hstg.next()
                            LOAD("sync", sg_, c_ssm[l, i - 16, 8 * g + 2 * c:8 * g + 2 * c + 2].rearrange("h p n -> (h p) n"), sgb)
                            pt, ptb = PSB.next()
                            TR(pt[:, 0:128], sg_, ident_f, [sgb] + CONSTS, [ptb])
                            CP("vector", hs[:, 2 * c:2 * c + 2, :].rearrange("p h d -> p (h d)"), pt[:, 0:128], [ptb], [hsB])
                        hstate["cur"] ^= 1
                        CP("scalar", hsbf[hstate["cur"]][0], hs.rearrange("p h d -> p (h d)"), [hsB], [hsbf[hstate["cur"]][1]])
                    hb_ap, hb_b = hsbf[hstate["cur"]]
                    yb, ybb = PSA.next()
                    for h in range(8):
                        r = slice(64 * (h % 2), 64 * (h % 2) + 64)
                        yc = slice((h // 2) * cl, (h // 2 + 1) * cl)
                        MM(yb[r, yc], cx["xdt"][0:cl, h, :], cx["sc3"][:, h, :], True, False, [cx["xdtb"], cx["scb"]], [ybb], inc=False)
                        MM(yb[r, yc], hb_ap[:, h * 64:(h + 1) * 64], cx["cd3"][:, h, :], False, True, [hb_b, cx["cdb"]], [ybb], inc=(h == 7))
                    kv, kvb = PSB.next()
                    MM(kv[:, :], B_tm[0:cl, i, :], cx["xw"][0:cl].rearrange("p h d -> p (h d)"), True, True, [BtB[i], cx["xwb"]], [kvb])
                    TT("vector", hs, hs, cx["Ea3"][:, :, cl - 1:cl].broadcast_to([128, 8, 64]), ALU.mult, [hsB, cx["EaB"]], [hsB])
                    TT("vector", hs, hs, kv[:, :].rearrange("p (h d) -> p h d", h=8), ALU.add, [hsB, kvb], [hsB])
                    if i < 15:
                        hstate["cur"] ^= 1
                        CP("scalar", hsbf[hstate["cur"]][0], hs.rearrange("p h d -> p (h d)"), [hsB], [hsbf[hstate["cur"]][1]])
                    if i == 15:
                        state_out(o_ssm_p[l, 8 * g:8 * g + 8])
                    if i >= 16:
                        state_out(o_ssm_s[l, i - 16, 8 * g:8 * g + 8])
                    yd, ydb = ydp.next()
                    for c in range(4):
                        STT(yd[:, c * cl:(c + 1) * cl], xsT[:, c, cols], pcol(l, "ssm_d", 4 * g + c), yb[:, c * cl:(c + 1) * cl], ALU.mult, ALU.add,
                            [xsB[c][ti], ybb] + CONSTS, [ydb])
                    TT("gpsimd", zsT[:, :, cols], zsT[:, :, cols], yd[:, 0:4 * cl].rearrange("p (c t) -> p c t", c=4), ALU.mult,
                       [ydb] + [zB[c][ti] for c in range(4)], [zB[c][ti] for c in range(4)])

                cxs = {0: phaseA(0)}
                phaseA2(cxs[0])
                for i in range(18):
                    if i + 1 < 18:
                        cxs[i + 1] = phaseA(i + 1)
                    phaseB(i, cxs.pop(i))
                    if i + 1 < 18:
                        phaseA2(cxs[i + 1])
                P.barrier()
                AR.reset(mPh)
                sqp = Pool([(AR.alloc(512), Buf("ssq%d" % i)) for i in range(2)])
                rsp = Pool([(AR.alloc(512, F32), Buf("srs%d" % i)) for i in range(2)])
                P.tag = "ssm.norm"
                dq = []
                for ti in range(5):
                    c0, n = TILES[ti]
                    rms_norm_fm([(zsT[:, c, c0:c0 + n], [zB[c][ti]]) for c in range(4)], 4, lambda c, g=g: pcol(l, "ssm_norm", 4 * g + c),
                                lambda c, ti=ti, c0=c0, n=n: (zsT[:, c, c0:c0 + n], [zB[c][ti]]), n, 1.0 / 512, (sqp, rsp), dq=dq)
                while dq:
                    dq.pop()()
                if dbg == "ssm":
                    dump("ssmmix%d" % g, zsT, [zB[c][ti] for c in range(4) for ti in range(5)], BF16)
                xacc(l, 1024 + 512 * g, 4, zsT, zB, "ssm", preloaded=True)
                P.barrier()
            for cc in range(12):
                STORE(o_conv_p[l][:, cc * 128:(cc + 1) * 128].rearrange("r p -> p r"), convout[:, 0, cc, :], bCO, slow=True)
                for s_ in range(2):
                    STORE(o_conv_s[l, s_][:, cc * 128:(cc + 1) * 128].rearrange("r p -> p r"), convout[:, 1 + s_, cc, :], bCO, slow=True)
            P.barrier()
            AR.reset(mR)

        NL = 2
        plan = dbg or "full"
        for l in range(NL):
            ffn(l, "ffn1")
            if plan == "ffn1":
                break
            mixer(l)
            if plan in ("ret", "mla", "ssm", "mix"):
                break
            ffn(l, "ffn2")

        m0 = AR.mark()
        ostg = [(AR.alloc(1024, F32), Buf("ostg%d" % i)) for i in range(2)]
        for i in range(17):
            s_ap, s_b = ostg[i % 2]
            if i < 16:
                rows, c0, ti = 128, i * 128, i // 4
            else:
                rows, c0, ti = 32, 2048, 4
            for hb in range(2):
                p_ap, p_b = PS.next()
                for cc in range(4):
                    c = hb * 4 + cc
                    P.op("tensor", lambda e, p_ap=p_ap, c=c, cc=cc, rows=rows, c0=c0: e.transpose(
                        out=p_ap[0:rows, cc * 128:(cc + 1) * 128], in_=xT[:, c, c0:c0 + rows], identity=ident_f),
                        reads=[xB[c][ti]] + CONSTS, writes=[p_b], inc=(cc == 3))
                if hb == 0:
                    P.op("scalar", lambda e, p_ap=p_ap, s_ap=s_ap, rows=rows: e.activation(out=s_ap[0:rows, 0:512], in_=p_ap[0:rows, :], func=AF.Copy),
                         reads=[p_b], writes=[s_b])
                else:
                    P.op("vector", lambda e, p_ap=p_ap, s_ap=s_ap, rows=rows: e.tensor_copy(out=s_ap[0:rows, 512:1024], in_=p_ap[0:rows, :]),
                         reads=[p_b], writes=[s_b])
            ob = Buf("out%d" % i)
            if i < 16:
                P.dma("sync", lambda e, s_ap=s_ap, i=i: e.dma_start(out=yp[i * 128:(i + 1) * 128, :], in_=s_ap), reads=[s_b], writes=[ob], sembuf=s_b)
            else:
                P.dma("sync", lambda e, s_ap=s_ap: e.dma_start(out=ys, in_=s_ap[0:32, :]), reads=[s_b], writes=[ob], sembuf=s_b)
            outbufs.append(ob)
        AR.reset(m0)

        P.wait_all("sync", outbufs)
        P.emit(nc, st)
    return nc


_NC_CACHE = {}


def make_in_maps(inp):
    ident, cb = const_tables()
    pcs = np.stack([pack_pcols(inp, l) for l in range(2)], axis=1)
    c_bf = np.stack([cb["ident_b"], cb["ones_b"]], axis=1)
    rtab, rmat, rc = ret_tables()
    mtab, mmat = mla_tables()
    U_np, NEG_np = ssm_tables()
    prow_np = np.ascontiguousarray(np.broadcast_to(
        np.concatenate([inp["ssm_dt_bias"], inp["ssm_a_log"], inp["mla_k_gain"]], axis=1)[None], (128, 2, 128))).astype(np.float32)
    maps = []
    for c in range(NCORES):
        m = {
            "xp": np.ascontiguousarray(inp["x_prompt"][c]),
            "xs": np.ascontiguousarray(inp["x_sample"][2 * c:2 * c + 2].reshape(32, D)),
            "pcols": pcs, "c_ident": ident, "c_bf": c_bf,
            "w_in": inp["w_in"], "w_out": inp["w_out"],
            "st_ret": np.ascontiguousarray(inp["state_ret"][:, 2 * c:2 * c + 2]),
            "c_rtab": rtab, "c_rmat": rmat, "c_rc": rc,
            "w_uq": inp["mla_w_uq"], "w_ukv": inp["mla_w_ukv"],
            "c_ckv": np.ascontiguousarray(inp["cache_mla_ckv"][:, 2 * c:2 * c + 2]),
            "c_kr": np.ascontiguousarray(inp["cache_mla_krope"][:, 2 * c:2 * c + 2]),
            "c_mtab": mtab, "c_mmat": mmat,
            "c_ssm": np.ascontiguousarray(inp["state_ssm"][:, 2 * c:2 * c + 2]),
            "c_conv": np.ascontiguousarray(inp["state_conv"][:, 2 * c:2 * c + 2]),
            "prow": prow_np, "c_U": U_np, "c_NEG": NEG_np,
            "ffn1_wgu": inp["ffn1_wgu"], "ffn1_wd": inp["ffn1_wd"],
            "ffn2_wgu": inp["ffn2_wgu"], "ffn2_wd": inp["ffn2_wd"],
        }
        maps.append(m)
    return maps


def kernel(**inputs):
    inp = {k: np.asarray(v) for k, v in inputs.items()}
    if "nc" not in _NC_CACHE:
        _NC_CACHE["nc"] = build_program()
    nc = _NC_CACHE["nc"]
    maps = make_in_maps(inp)
    res = run_bass_kernel_spmd(nc, maps, core_ids=list(range(NCORES)))
    r = res.results
    g = lambda c, k: np.asarray(r[c][k])
    y_p = np.stack([g(c, "yp") for c in range(NCORES)], axis=0)
    y_s = np.concatenate([g(c, "ys").reshape(2, 16, D) for c in range(NCORES)], axis=0)
    ckv_p = np.stack([g(c, "o_ckv_p") for c in range(NCORES)], axis=1)
    kr_p = np.stack([g(c, "o_kr_p") for c in range(NCORES)], axis=1)
    ret_p = np.stack([g(c, "o_ret_p") for c in range(NCORES)], axis=1)
    ssm_p = np.stack([g(c, "o_ssm_p") for c in range(NCORES)], axis=1)
    conv_p = np.stack([g(c, "o_conv_p") for c in range(NCORES)], axis=1)
    ckv_s = np.concatenate([g(c, "o_ckv_s").reshape(2, 2, 16, 256) for c in range(NCORES)], axis=1)
    kr_s = np.concatenate([g(c, "o_kr_s").reshape(2, 2, 16, 32) for c in range(NCORES)], axis=1)
    ret_s = np.concatenate([g(c, "o_ret_s") for c in range(NCORES)], axis=1)
    ssm_s = np.concatenate([g(c, "o_ssm_s") for c in range(NCORES)], axis=1)
    conv_s = np.concatenate([g(c, "o_conv_s") for c in range(NCORES)], axis=1)
    outs = (y_p, y_s, ckv_p, kr_p, ret_p, ssm_p, conv_p, ckv_s, kr_s, ret_s, ssm_s, conv_s)
    return tuple(np.ascontiguousarray(o, dtype=np.float32) for o in outs)
```
